# Optimizing a Trainium2 kernel written in Bass

```python
import jax, jax.numpy as jnp
from jax import lax
import numpy as np

D_MODEL = 1024
BATCH = 8
SEQ = 8192
DEPTH = 2

EXPAND = 2
MIX_WIDTH = EXPAND * D_MODEL
GMLP_WIDTH = MIX_WIDTH // 2
GMLP_GROUPS = 4
GMLP_GROUP_DIM = GMLP_WIDTH // GMLP_GROUPS
GMLP_CHUNK = 128
ATTN_WIDTH = MIX_WIDTH // 2
HEAD_DIM = 128
N_Q_HEADS = ATTN_WIDTH // HEAD_DIM
N_KV_HEADS = 2
GQA_GROUP = N_Q_HEADS // N_KV_HEADS
KV_WIDTH = N_KV_HEADS * HEAD_DIM
WINDOW = 128
ATTN_BLOCK = 128
HGRN_WIDTH = MIX_WIDTH
HGRN_HEAD_DIM = 128
HGRN_HEADS = HGRN_WIDTH // HGRN_HEAD_DIM
HGRN_CHUNK = 64

EPS = 1e-6
N_EVEN = (DEPTH + 1) // 2
N_ODD = DEPTH // 2
EVEN_SPLITS = [GMLP_WIDTH, GMLP_WIDTH, GMLP_WIDTH, ATTN_WIDTH, KV_WIDTH, KV_WIDTH, ATTN_WIDTH]
ODD_SPLITS = [HGRN_WIDTH] * 5
IN_EVEN = sum(EVEN_SPLITS)
IN_ODD = sum(ODD_SPLITS)

kernel_name = "hybrid_gmlp_swa_hgrn2_encoder"

F32 = jnp.float32


def rms_norm(x, g):
    xf = x.astype(F32)
    y = xf * lax.rsqrt(jnp.mean(xf * xf, axis=-1, keepdims=True) + EPS)
    return (y * g.astype(F32)).astype(x.dtype)


def split_cols(t, sizes):
    idx = [int(i) for i in np.cumsum(sizes)[:-1]]
    return jnp.split(t, idx, axis=-1)


def alibi_slopes(n):
    return jnp.exp2(-8.0 * jnp.arange(1, n + 1, dtype=F32) / n)


def chunked_sgu(u, v, ln_g, ln_b, w_s, b_s):
    B, S, _ = v.shape
    vf = v.astype(F32)
    mu = jnp.mean(vf, axis=-1, keepdims=True)
    var = jnp.mean(jnp.square(vf - mu), axis=-1, keepdims=True)
    vn = ((vf - mu) * lax.rsqrt(var + EPS) * ln_g.astype(F32) + ln_b.astype(F32)).astype(v.dtype)
    vc = vn.reshape(B, S // GMLP_CHUNK, GMLP_CHUNK, GMLP_GROUPS, GMLP_GROUP_DIM)
    mixed = jnp.einsum('gts,bnsgc->bntgc', w_s, vc) + b_s.T[None, None, :, :, None]
    return u * mixed.reshape(B, S, GMLP_WIDTH)


def window_attention(q, k, v, sink):
    B, S = q.shape[0], q.shape[1]
    nb = S // ATTN_BLOCK
    qb = q.reshape(B, nb, ATTN_BLOCK, N_KV_HEADS, GQA_GROUP, HEAD_DIM)

    def band(t):
        tp = jnp.pad(t, ((0, 0), (ATTN_BLOCK, ATTN_BLOCK), (0, 0), (0, 0)))
        tb = tp.reshape(B, nb + 2, ATTN_BLOCK, N_KV_HEADS, HEAD_DIM)
        return jnp.concatenate([tb[:, :-2], tb[:, 1:-1], tb[:, 2:]], axis=2)

    kb, vb = band(k), band(v)
    scores = jnp.einsum('bnqhgd,bnshd->bnhgqs', qb, kb).astype(F32) * (HEAD_DIM ** -0.5)
    qi = jnp.arange(ATTN_BLOCK)
    kj = jnp.arange(3 * ATTN_BLOCK) - ATTN_BLOCK
    dist = jnp.abs(kj[None, :] - qi[:, None])
    kpos = (jnp.arange(nb) * ATTN_BLOCK)[:, None] + kj[None, :]
    valid = (dist[None] <= WINDOW) & (kpos[:, None, :] >= 0) & (kpos[:, None, :] < S)
    slopes = alibi_slopes(N_Q_HEADS).reshape(N_KV_HEADS, GQA_GROUP)
    scores = scores - slopes[:, :, None, None] * dist.astype(F32)
    scores = jnp.where(valid[None, :, None, None], scores, -jnp.inf)
    sink_l = sink.astype(F32).reshape(N_KV_HEADS, GQA_GROUP)[None, None, :, :, None, None]
    m = jnp.maximum(jnp.max(scores, axis=-1, keepdims=True), sink_l)
    p = jnp.exp(scores - m)
    p = p / (jnp.sum(p, axis=-1, keepdims=True) + jnp.exp(sink_l - m))
    out = jnp.einsum('bnhgqs,bnshd->bnqhgd', p.astype(v.dtype), vb)
    return out.reshape(B, S, N_Q_HEADS * HEAD_DIM)


def hgrn2_direction(q, f_logit, i, lb):
    B, S, H, D = q.shape
    nc = S // HGRN_CHUNK
    C = HGRN_CHUNK
    f = lb + (1.0 - lb) * jax.nn.sigmoid(f_logit.astype(F32))
    k = 1.0 - f
    g = jnp.log(f)

    def chunk(t):
        return t.reshape(B, nc, C, H, D)

    qc, kc, vc, gc = chunk(q.astype(F32)), chunk(k), chunk(i.astype(F32)), chunk(g)
    bcum = jnp.cumsum(gc, axis=2)
    blast = bcum[:, :, -1:]
    q_t = qc * jnp.exp(bcum)
    k_t = kc * jnp.exp(-bcum)
    k_end = kc * jnp.exp(blast - bcum)
    a = jnp.einsum('bnthd,bnshd->bnhts', q_t, k_t)
    a = jnp.where(jnp.tril(jnp.ones((C, C), dtype=bool)), a, 0.0)
    o_intra = jnp.einsum('bnhts,bnshv->bnthv', a, vc)

    def step(state, xs):
        qn, kn, vn, dn = xs
        o = jnp.einsum('bthd,bhdv->bthv', qn, state)
        state = state * dn[..., None] + jnp.einsum('bshd,bshv->bhdv', kn, vn)
        return state, o

    xs = (jnp.moveaxis(q_t, 1, 0), jnp.moveaxis(k_end, 1, 0), jnp.moveaxis(vc, 1, 0),
          jnp.moveaxis(jnp.exp(blast[:, :, 0]), 1, 0))
    s0 = jnp.zeros((B, H, D, D), dtype=F32)
    _, o_inter = lax.scan(step, s0, xs)
    o = o_intra + jnp.moveaxis(o_inter, 0, 1)
    return o.reshape(B, S, H, D)


def hgrn_lower_bound(gamma, layer):
    c = jnp.cumsum(jax.nn.softmax(gamma.astype(F32), axis=0), axis=0)
    return (c[layer] - c[0]).reshape(HGRN_HEADS, HGRN_HEAD_DIM)


def even_layer(x, norm_g, w_in, ln_g, ln_b, w_s, b_s, sink, w_out):
    B, S, _ = x.shape
    h = rms_norm(x, norm_g)
    proj = h @ w_in
    u_a, v_a, z_a, q_b, k_b, v_b, z_b = split_cols(proj, EVEN_SPLITS)
    a_out = chunked_sgu(u_a, v_a, ln_g, ln_b, w_s, b_s) * jax.nn.silu(z_a)
    b_out = window_attention(q_b.reshape(B, S, N_Q_HEADS, HEAD_DIM),
                             k_b.reshape(B, S, N_KV_HEADS, HEAD_DIM),
                             v_b.reshape(B, S, N_KV_HEADS, HEAD_DIM), sink) * jax.nn.silu(z_b)
    return jnp.concatenate([a_out, b_out], axis=-1) @ w_out


def odd_layer(x, norm_g, w_in, gamma_f, gamma_b, head_norm_g, w_out, layer):
    B, S, _ = x.shape
    h = rms_norm(x, norm_g)
    proj = h @ w_in
    q, f_f, f_b, i, z = split_cols(proj, ODD_SPLITS)
    shp = (B, S, HGRN_HEADS, HGRN_HEAD_DIM)
    q = jax.nn.silu(q).reshape(shp)
    i = i.reshape(shp)
    o_f = hgrn2_direction(q, f_f.reshape(shp), i, hgrn_lower_bound(gamma_f, layer))
    o_b = jnp.flip(hgrn2_direction(jnp.flip(q, 1), jnp.flip(f_b.reshape(shp), 1), jnp.flip(i, 1),
                                   hgrn_lower_bound(gamma_b, layer)), 1)
    o = o_f + o_b
    o = o * lax.rsqrt(jnp.mean(o * o, axis=-1, keepdims=True) + EPS)
    o = (o.reshape(B, S, HGRN_WIDTH) * head_norm_g.astype(F32)).astype(x.dtype)
    return (o * jax.nn.silu(z)) @ w_out


def setup_inputs(seed: int = 0) -> dict:
    key = jax.random.key(seed)
    ks = jax.random.split(key, 16)
    nrm = jax.random.normal
    return {
        "x": nrm(ks[0], (BATCH, SEQ, D_MODEL), F32),
        "norm_g_even": 1.0 + 0.02 * nrm(ks[1], (N_EVEN, D_MODEL), F32),
        "w_in_even": nrm(ks[2], (N_EVEN, D_MODEL, IN_EVEN), F32) * D_MODEL ** -0.5,
        "gmlp_ln_g": 1.0 + 0.02 * nrm(ks[3], (N_EVEN, GMLP_WIDTH), F32),
        "gmlp_ln_b": 0.02 * nrm(ks[4], (N_EVEN, GMLP_WIDTH), F32),
        "gmlp_w_s": nrm(ks[5], (N_EVEN, GMLP_GROUPS, GMLP_CHUNK, GMLP_CHUNK), F32) * GMLP_CHUNK ** -0.5,
        "gmlp_b_s": 1.0 + 0.02 * nrm(ks[6], (N_EVEN, GMLP_GROUPS, GMLP_CHUNK), F32),
        "attn_sink": 0.5 * nrm(ks[7], (N_EVEN, N_Q_HEADS), F32),
        "w_out_even": nrm(ks[8], (N_EVEN, GMLP_WIDTH + ATTN_WIDTH, D_MODEL), F32) * (GMLP_WIDTH + ATTN_WIDTH) ** -0.5,
        "norm_g_odd": 1.0 + 0.02 * nrm(ks[9], (N_ODD, D_MODEL), F32),
        "w_in_odd": nrm(ks[10], (N_ODD, D_MODEL, IN_ODD), F32) * D_MODEL ** -0.5,
        "hgrn_gamma_fwd": 1.0 + 0.1 * nrm(ks[11], (DEPTH, HGRN_WIDTH), F32),
        "hgrn_gamma_bwd": 1.0 + 0.1 * nrm(ks[12], (DEPTH, HGRN_WIDTH), F32),
        "hgrn_head_norm_g": 1.0 + 0.02 * nrm(ks[13], (N_ODD, HGRN_WIDTH), F32),
        "w_out_odd": nrm(ks[14], (N_ODD, HGRN_WIDTH, D_MODEL), F32) * HGRN_WIDTH ** -0.5,
        "final_norm_g": 1.0 + 0.02 * nrm(ks[15], (D_MODEL,), F32),
    }


def reference(x, norm_g_even, w_in_even, gmlp_ln_g, gmlp_ln_b, gmlp_w_s, gmlp_b_s, attn_sink,
              w_out_even, norm_g_odd, w_in_odd, hgrn_gamma_fwd, hgrn_gamma_bwd, hgrn_head_norm_g,
              w_out_odd, final_norm_g):
    for layer in range(DEPTH):
        j = layer // 2
        if layer % 2 == 0:
            x = x + even_layer(x, norm_g_even[j], w_in_even[j], gmlp_ln_g[j], gmlp_ln_b[j],
                               gmlp_w_s[j], gmlp_b_s[j], attn_sink[j], w_out_even[j])
        else:
            x = x + odd_layer(x, norm_g_odd[j], w_in_odd[j], hgrn_gamma_fwd, hgrn_gamma_bwd,
                              hgrn_head_norm_g[j], w_out_odd[j], layer)
    return rms_norm(x, final_norm_g)
```

```python
import os
import numpy as np
import concourse.bass as bass
import concourse.mybir as mybir
from concourse.bass_utils import run_bass_kernel_spmd

F32 = mybir.dt.float32
BF16 = mybir.dt.bfloat16
AF = mybir.ActivationFunctionType
ALU = mybir.AluOpType
AX = mybir.AxisListType

KSTOP = int(os.environ.get('KSTOP', '99'))
D = 1024
EPS = 1e-6
NCORES = 8
SEQ = 8192
ARENA_WORDS = 53000


class _Op:
    __slots__ = ("eng", "fn", "reads", "writes", "dma", "deps", "alld", "sig", "need", "dur", "seg")

    def __init__(self, eng, fn, reads, writes, dma, dur, seg):
        self.eng, self.fn, self.reads, self.writes, self.dma = eng, fn, tuple(reads), tuple(writes), dma
        self.deps = ()
        self.alld = ()
        self.sig = None
        self.need = False
        self.dur = dur
        self.seg = seg


class Sched:
    ENGS = ("pe", "act", "dve", "pool", "sp")
    XLAT = 350.0

    def __init__(self):
        self.ops = []
        self.last_w = {}
        self.readers = {}
        self.seg = 0

    def add(self, eng, fn, r=(), w=(), dma=None, dur=300.0):
        ops = self.ops
        i = len(ops)
        w = tuple(w) + tuple(k for k in r if isinstance(k, tuple) and k[0] == "ps" and k not in w)
        op = _Op(eng, fn, r, w, dma, dur, self.seg)
        deps = set()
        for k in op.reads:
            j = self.last_w.get(k)
            if j is not None:
                deps.add(j)
        for k in op.writes:
            j = self.last_w.get(k)
            if j is not None:
                deps.add(j)
            deps.update(self.readers.get(k, ()))
        deps = set(j for j in deps if ops[j].seg == self.seg)
        op.alld = deps
        pr = set()
        for j in deps:
            oj = ops[j]
            if oj.dma is None and dma is None and oj.eng == eng:
                if eng == "pe":
                    continue
                if not any(k in oj.writes for k in op.reads):
                    continue
            pr.add(j)
        op.deps = pr
        for k in op.reads:
            self.readers.setdefault(k, []).append(i)
        for k in op.writes:
            self.last_w[k] = i
            self.readers[k] = []
        ops.append(op)
        return i

    def barrier(self):
        self.seg += 1

    def _sched_segment(self, idxs):
        import heapq
        ops = self.ops
        nd = {}
        succ = {}
        for i in idxs:
            d = ops[i].alld
            nd[i] = len(d)
            for j in d:
                succ.setdefault(j, []).append(i)
        free = {e: 0.0 for e in self.ENGS}
        fut = {e: [] for e in self.ENGS}
        av = {e: [] for e in self.ENGS}
        rt = {}
        fin = {}
        for i in idxs:
            if nd[i] == 0:
                heapq.heappush(fut[ops[i].eng], (0.0, i))
        order = []
        left = len(idxs)
        while left:
            best = None
            for e in self.ENGS:
                f, a = fut[e], av[e]
                while f and f[0][0] <= free[e]:
                    heapq.heappush(a, heapq.heappop(f)[1])
                if a:
                    cand = (free[e], a[0], e, True)
                elif f:
                    cand = (f[0][0], f[0][1], e, False)
                else:
                    continue
                if best is None or cand[:2] < best[:2]:
                    best = cand
            start, i, e, fa = best
            if fa:
                heapq.heappop(av[e])
            else:
                heapq.heappop(fut[e])
            op = ops[i]
            if op.dma is not None:
                free[e] = start + 120.0
            else:
                free[e] = start + op.dur
            fin[i] = start + op.dur
            order.append(i)
            left -= 1
            for sidx in succ.get(i, ()):
                so = ops[sidx]
                if so.eng == e and op.dma is None and so.dma is None:
                    lat = 60.0 if sidx in so.deps or i in so.deps else 0.0
                else:
                    lat = self.XLAT
                t = fin[i] + lat
                if rt.get(sidx, 0.0) < t:
                    rt[sidx] = t
                nd[sidx] -= 1
                if nd[sidx] == 0:
                    heapq.heappush(fut[so.eng], (rt[sidx], sidx))
        self.makespan.append(max(fin.values()) if fin else 0.0)
        return order

    def schedule(self):
        ops = self.ops
        nseg = self.seg + 1
        segs = [[] for _ in range(nseg)]
        for i, op in enumerate(ops):
            segs[op.seg].append(i)
        self.makespan = []
        order = []
        for sg in segs:
            o = self._sched_segment(sg)
            order.extend(o)
            last = {}
            for i in o:
                op = ops[i]
                if op.dma is not None:
                    last[("d", op.dma)] = i
                elif op.fn is not None:
                    last[("e", op.eng)] = i
            deps = set(last.values())
            for e in self.ENGS:
                order.append(("bar", e, deps))
        self.order = order

    def finalize(self, nc, stack):
        ops = self.ops
        self.schedule()
        for op in ops:
            for j in op.deps:
                ops[j].need = True
        for it in self.order:
            if isinstance(it, tuple):
                for j in it[2]:
                    ops[j].need = True
        sems = {}
        cnt = {}

        def sem_for(key):
            if key not in sems:
                sems[key] = stack.enter_context(nc.semaphore("s%d" % len(sems)))
                cnt[key] = 0
            return sems[key]

        for e in self.ENGS:
            sem_for(("e", e))
        for it in self.order:
            if isinstance(it, tuple):
                continue
            op = ops[it]
            if op.dma is not None:
                k = ("d", op.dma)
                sem_for(k)
                cnt[k] += 16
                op.sig = (k, cnt[k])
            elif op.need and op.fn is not None:
                k = ("e", op.eng)
                cnt[k] += 1
                op.sig = (k, cnt[k])
        self.sems = sems
        return sems

    def emit(self, eng_name, eng):
        ops = self.ops
        sems = self.sems
        known = {}
        for it in self.order:
            if isinstance(it, tuple):
                if it[1] != eng_name:
                    continue
                deps, op = it[2], None
            else:
                op = ops[it]
                if op.eng != eng_name:
                    continue
                deps = op.deps
            need = {}
            for j in deps:
                sg = ops[j].sig
                if sg is None:
                    continue
                if need.get(sg[0], 0) < sg[1]:
                    need[sg[0]] = sg[1]
            for k, v in need.items():
                if known.get(k, 0) < v:
                    eng.wait_ge(sems[k], v)
                    known[k] = v
            if op is None or op.fn is None:
                continue
            ins = op.fn(eng)
            if op.sig is not None:
                ins.then_inc(sems[op.sig[0]], 16 if op.dma is not None else 1)


class Arena:
    def __init__(self, t, words):
        self.t = t
        self.words = words
        self.off = 0

    def alloc(self, shape, dtype):
        n = 1
        for s in shape:
            n *= s
        words = n if dtype == F32 else (n + 1) // 2
        words = (words + 7) // 8 * 8
        assert self.off + words <= self.words, ("arena overflow", self.off, words)
        ap = self.t[:, self.off:self.off + words]
        self.off += words
        if dtype != F32:
            ap = ap.bitcast(dtype)
        ap = ap[:, 0:n]
        if len(shape) == 2:
            ap = ap.rearrange("p (a b) -> p a b", a=shape[0], b=shape[1])
        elif len(shape) == 3:
            ap = ap.rearrange("p (a b c) -> p a b c", a=shape[0], b=shape[1], c=shape[2])
        return ap


class Prog:
    def __init__(self, NT, upto=3):
        self.NT = NT
        self.S = NT * 128
        self.upto = upto
        self.sc = Sched()
        self.nc = bass.Bass("TRN2", target_bir_lowering=False)

    @staticmethod
    def _n(ap):
        n = 1
        for v in ap.shape[1:]:
            n *= v
        return n

    def mm(self, out, lhsT, rhs, start, stop, r, w):
        d = max(self._n(rhs) / 2.4, 104.0)
        self.sc.add("pe", lambda e: e.matmul(out, lhsT, rhs, start=start, stop=stop), r, w, dur=d)

    def tr(self, out, in_, r, w):
        ident = self.ident
        self.sc.add("pe", lambda e: e.transpose(out, in_, ident), tuple(r) + ("const",), w, dur=110.0)

    def act(self, out, in_, func, r, w, bias=None, scale=None, accum=None):
        kw = {}
        if bias is not None:
            kw["bias"] = bias
        if scale is not None:
            kw["scale"] = scale
        if accum is not None:
            kw["accum_out"] = accum
        d = 210.0 + 0.75 * self._n(in_) + (100.0 if accum is not None else 0.0)
        self.sc.add("act", lambda e: e.activation(out, in_, func, **kw), r, w, dur=d)

    def _vd(self, eng, n, k=1.35):
        return (110.0 + 2.25 * n) if eng == "pool" else (70.0 + k * n)

    def ts(self, eng, out, in0, s1, op0, r, w, s2=None, op1=None):
        d = self._vd(eng, self._n(in0))
        if op1 is None:
            self.sc.add(eng, lambda e: e.tensor_scalar(out, in0, s1, None, op0), r, w, dur=d)
        else:
            self.sc.add(eng, lambda e: e.tensor_scalar(out, in0, s1, s2, op0, op1), r, w, dur=d)

    def tt(self, eng, out, in0, in1, op, r, w):
        self.sc.add(eng, lambda e: e.tensor_tensor(out, in0, in1, op), r, w, dur=self._vd(eng, self._n(in0)))

    def stt(self, out, in0, scalar, in1, op0, op1, r, w):
        self.sc.add("dve", lambda e: e.scalar_tensor_tensor(out, in0, scalar, in1, op0, op1), r, w,
                    dur=90.0 + 2.0 * self._n(in0))

    def cp(self, eng, out, in_, r, w):
        if eng == "act":
            self.sc.add("act", lambda e: e.copy(out, in_), r, w, dur=210.0 + 0.75 * self._n(in_))
        else:
            self.sc.add(eng, lambda e: e.tensor_copy(out, in_), r, w, dur=self._vd(eng, self._n(in_), 1.0))

    def dma(self, eng, out, in_, key, r, w, cast=False):
        nbytes = 128 * self._n(out) * 4
        d = 2500.0 + nbytes / 150.0
        if cast:
            self.sc.add(eng, lambda e: e.dma_start(out=out, in_=in_, max_dma_last_dim=4096), r, w, dma=key, dur=d)
        else:
            self.sc.add(eng, lambda e: e.dma_start(out=out, in_=in_), r, w, dma=key, dur=d)

    def silu(self, out, src, tmp, r, w, tk):
        self.act(tmp, src, AF.Exp, r, [tk], scale=-1.0)
        self.act(tmp, tmp, AF.Ln, [tk], [tk], bias=1.0)
        self.act(tmp, tmp, AF.Exp, [tk], [tk], scale=-1.0)
        self.tt("dve", out, src, tmp, ALU.mult, tuple(r) + (tk,), w)

    def rms(self, x_ap, xk, n=1024):
        self.act(self.hb, x_ap, AF.Square, [xk], ["hb", "ss"], accum=self.ss)
        self.act(self.lnv, self.ss, AF.Ln, ["ss"], ["lnv"], scale=1.0 / n, bias=EPS)
        self.act(self.rstd, self.lnv, AF.Exp, ["lnv"], ["rstd"], scale=-0.5)

    def make_hT(self, x_ap, xk, gbc, gk, dst=None, dk="hT"):
        self.rms(x_ap, xk)
        self.stt(self.hb, x_ap, self.rstd, gbc, ALU.mult, ALU.mult, [xk, "rstd", gk], ["hb"])
        Tb = self.psb[0]
        for kc in range(8):
            self.tr(Tb[:, kc * 128:(kc + 1) * 128], self.hb[:, kc * 128:(kc + 1) * 128], ["hb"], [("ps", 0)])
        if dst is None:
            self.cp("act", self.hT.rearrange("p a b -> p (a b)"), Tb[:, 0:1024], [("ps", 0)], ["hT"])
        else:
            self.cp("act", dst, Tb[:, 0:1024].rearrange("p (a b) -> p a b", b=128), [("ps", 0)], [dk])

    def proj_tok(self, bank, ncols, W, col0):
        pk = ("ps", bank)
        out = self.ps[bank][:, 0:ncols]
        for kc in range(8):
            self.mm(out, self.hT[:, kc, :], W[:, kc, col0:col0 + ncols], kc == 0, kc == 7, ["hT", ("W", kc)], [pk])

    def proj_feat(self, bank, hh, W, col0):
        pk = ("ps", bank)
        out = self.ps[bank][:, hh * 128:(hh + 1) * 128]
        for kc in range(8):
            self.mm(out, W[:, kc, col0:col0 + 128], self.hT[:, kc, :], kc == 0, kc == 7, ["hT", ("W", kc)], [pk])

    def build(self):
        nc = self.nc
        NT, S = self.NT, self.S
        dt = nc.dram_tensor
        self.x_d = dt("x", [S, D], F32, kind="ExternalInput")
        self.wie_d = dt("w_in_even", [D, 5632], F32, kind="ExternalInput")
        self.woe_d = dt("w_out_even", [2048, D], F32, kind="ExternalInput")
        self.wio_d = dt("w_in_odd", [D, 10240], F32, kind="ExternalInput")
        self.woo_d = dt("w_out_odd", [2048, D], F32, kind="ExternalInput")
        self.vecs_d = dt("vecs", [5, D], F32, kind="ExternalInput")
        self.wsT_d = dt("wsT", [128, 512], F32, kind="ExternalInput")
        self.cols_d = dt("cols", [128, 128], F32, kind="ExternalInput")
        self.sink_d = dt("sink", [1, 8], F32, kind="ExternalInput")
        self.cst_d = dt("cst", [128, 3200], F32, kind="ExternalInput")
        self.out_d = dt("out", [S, D], F32, kind="ExternalOutput")
        self.x1_d = dt("x1s", [S, D], F32, kind="Internal")
        self.ob_d = dt("obs", [NT // 2, 128, 4096], F32, kind="Internal")
        self.on_d = dt("ons", [NT // 2, 128, 4096], F32, kind="Internal")

        from contextlib import ExitStack
        with ExitStack() as st:
            arena_t = st.enter_context(nc.sbuf_tensor("arena", [128, ARENA_WORDS], F32))
            self.ps = [st.enter_context(nc.psum_tensor("ps%d" % b, [128, 512], F32)) for b in range(8)]
            self.psb = [p.bitcast(BF16) for p in self.ps]
            self.A = Arena(arena_t, ARENA_WORDS)
            self.consts()
            base = self.A.off
            self.phase0()
            if self.upto >= 1 and self.upto != 12:
                self.sc.barrier()
                self.A.off = base
                self.phase_scan(fwd=False)
            if self.upto >= 2:
                self.sc.barrier()
                self.A.off = base
                self.phase_scan(fwd=True)
            if self.upto >= 3 and self.upto != 12:
                self.sc.barrier()
                self.A.off = base
                self.phase3()
            self.sc.finalize(nc, st)
            with nc.Block() as block:
                sc = self.sc

                @block.sync
                def _(e):
                    sc.emit("sp", e)

                @block.scalar
                def _(e):
                    sc.emit("act", e)

                @block.vector
                def _(e):
                    sc.emit("dve", e)

                @block.gpsimd
                def _(e):
                    sc.emit("pool", e)

                @block.tensor
                def _(e):
                    sc.emit("pe", e)
        return nc

    def consts(self):
        A = self.A
        a = A.alloc
        self.ident = a((128,), BF16)
        self.ones = a((128,), BF16)
        self.maskf = a((512,), BF16)
        self.maskb = a((512,), BF16)
        self.resf = a((512,), F32)
        self.resb = a((512,), F32)
        self.distp = a((384,), F32)
        self.cols = a((128,), F32)
        self.sinkbc = a((8,), F32)
        self.nsink = a((8,), F32)
        self.lbf = a((16,), F32)
        self.omlf = a((16,), F32)
        self.lbb = a((16,), F32)
        self.omlb = a((16,), F32)
        self.lnomlf = a((16,), F32)
        self.lnomlb = a((16,), F32)
        self.ss = a((1,), F32)
        self.lnv = a((1,), F32)
        self.rstd = a((1,), F32)
        self.hb = a((1024,), BF16)
        self.hT = a((8, 128), BF16)
        c = self.cst_d
        p = "pool"
        k = "const"
        self.dma(p, self.ident, c[:, 0:128], "c0", [], [k], cast=True)
        self.dma(p, self.ones, c[:, 128:256], "c1", [], [k], cast=True)
        self.dma(p, self.maskf, c[:, 256:768], "c2", [], [k], cast=True)
        self.dma(p, self.maskb, c[:, 768:1280], "c3", [], [k], cast=True)
        self.dma("sp", self.resf, c[:, 1280:1792], "c4", [], [k])
        self.dma("sp", self.resb, c[:, 1792:2304], "c5", [], [k])
        self.dma("sp", self.distp, c[:, 2304:2688], "c6", [], [k])
        self.dma("sp", self.cols, self.cols_d[:, :], "c7", [], [k])
        self.dma("sp", self.sinkbc, self.sink_d[0:1, :].partition_broadcast(128), "c8", [], [k])
        self.ts("dve", self.nsink, self.sinkbc, -1.0, ALU.mult, [k], ["const2"])
        for (lb, oml, lno, o) in ((self.lbf, self.omlf, self.lnomlf, 4), (self.lbb, self.omlb, self.lnomlb, 36)):
            self.tt("dve", lb, self.cols[:, o:o + 16], self.cols[:, o + 16:o + 32], ALU.subtract, [k], ["const2"])
            self.act(lb, lb, AF.Exp, ["const2"], ["const2"])
            self.ts("dve", lb, lb, 1.0, ALU.add, ["const2"], ["const2"])
            self.sc.add("dve", (lambda lb: (lambda e: e.reciprocal(lb, lb)))(lb), ["const2"], ["const2"])
            self.ts("dve", oml, lb, -1.0, ALU.mult, ["const2"], ["const2"], 1.0, ALU.add)
            self.act(lno, oml, AF.Ln, ["const2"], ["const2"])
        self.bs = self.cols[:, 0:4]
        self.hg = self.cols[:, 68:84]
        self.cmask = self.cols[:, 84:86]

    def load_vec(self, i, key):
        t = self.A.alloc((1024,), F32)
        self.dma("sp", t, self.vecs_d[i:i + 1, :].partition_broadcast(128), key, [], [key])
        return t

    def phase0(self):
        A, NT = self.A, self.NT
        a = A.alloc
        ps, psb = self.ps, self.psb
        W = a((8, 5632), BF16)
        WO = a((16, 1024), BF16)
        gE = self.load_vec(0, "gE")
        lng = self.load_vec(1, "lng")
        lnb = self.load_vec(2, "lnb")
        wsT = a((4, 128), BF16)
        self.dma("pool", wsT.rearrange("p a b -> p (a b)"), self.wsT_d[:, :], "wsT", [], ["wsT"], cast=True)
        wv = self.wie_d.ap().rearrange("(kc p) n -> p kc n", p=128)
        for kc in range(8):
            self.dma("pool", W[:, kc, :], wv[:, kc, :], ("W", kc), [], [("W", kc)], cast=True)
        self.dma("pool", WO, self.woe_d.ap().rearrange("(kc p) n -> p kc n", p=128), "WO", [], ["WO"], cast=True)
        xs = [a((1024,), F32) for _ in range(3)]
        qT = [a((8, 128), BF16) for _ in range(2)]
        kT = [a((2, 128), BF16) for _ in range(3)]
        vb = [a((256,), BF16) for _ in range(3)]
        szb = [a((1024,), BF16) for _ in range(2)]
        cat = [a((2048,), BF16) for _ in range(2)]
        catT = a((16, 128), BF16)
        vh = a((1024,), F32)
        vn = a((1024,), BF16)
        mp = vh
        sza = a((1024,), F32)
        tmpE = [a((512,), F32) for _ in range(2)]
        stats = a((2, 6), F32)
        mv = a((2,), F32)
        lnv2 = a((1,), F32)
        rstd2 = a((1,), F32)
        ssb = [a((384,), F32) for _ in range(2)]
        pp = [a((384,), BF16) for _ in range(2)]
        pT = [a((384,), BF16) for _ in range(2)]
        nrm = a((8,), F32)
        negm = a((8,), F32)
        rsum = a((8,), F32)
        t4 = a((4,), F32)
        rinv = a((4,), F32)
        dst = self.x1_d if self.upto >= 1 else self.out_d

        def load_x(n):
            sl = n % 3
            self.dma("sp", xs[sl], self.x_d[n * 128:(n + 1) * 128, :], ("xs", sl), [], [("xs", sl)])

        load_x(0)
        for n in range(NT + 1):
            if n + 1 < NT:
                load_x(n + 1)
            if n < NT:
                sl, par, s3 = n % 3, n % 2, n % 3
                xk = ("xs", sl)
                self.make_hT(xs[sl], xk, gE, "gE")
                self.proj_tok(1, 512, W, 1024)
                self.proj_tok(2, 512, W, 1536)
                for h in range(8):
                    self.proj_feat(3 + h // 4, h % 4, W, 3072 + h * 128)
                for b in range(2):
                    self.sc.add("dve", (lambda o, i: (lambda e: e.bn_stats(o, i)))(stats[:, b, :], ps[1 + b][:, :]),
                                [("ps", 1 + b)], ["stats"], dur=700.0)
                self.sc.add("dve", lambda e: e.bn_aggr(mv, stats.rearrange("p a b -> p (a b)")), ["stats"], ["mv"])
                self.act(lnv2, mv[:, 1:2], AF.Ln, ["mv"], ["lnv2"], bias=EPS)
                self.act(rstd2, lnv2, AF.Exp, ["lnv2"], ["rstd2"], scale=-0.5)
                for b in range(2):
                    self.ts("dve", vh[:, b * 512:(b + 1) * 512], ps[1 + b][:, :], mv[:, 0:1], ALU.subtract,
                            [("ps", 1 + b), "mv", "rstd2"], [("vm", b)], rstd2, ALU.mult)
                for b in range(2):
                    sl_ = slice(b * 512, (b + 1) * 512)
                    self.tt("pool", vh[:, sl_], vh[:, sl_], lng[:, sl_], ALU.mult, [("vm", b), "lng"], [("vm", b)])
                    self.tt("pool", vn[:, sl_], vh[:, sl_], lnb[:, sl_], ALU.add, [("vm", b), "lnb"], [("vn", b)])
                for g in range(2):
                    self.act(qT[par][:, 4 * g:4 * g + 4, :].rearrange("p a b -> p (a b)"), ps[3 + g][:, :], AF.Identity,
                             [("ps", 3 + g)], [("qT", par)], scale=128.0 ** -0.5)
                for hk in range(2):
                    self.proj_feat(1, hk, W, 4096 + hk * 128)
                self.proj_tok(2, 256, W, 4352)
                self.cp("dve", kT[s3].rearrange("p a b -> p (a b)"), ps[1][:, 0:256], [("ps", 1)], [("kT", s3)])
                self.cp("dve", vb[s3], ps[2][:, 0:256], [("ps", 2)], [("vb", s3)])
                self.proj_tok(3, 512, W, 2048)
                self.proj_tok(4, 512, W, 2560)
                for b in range(2):
                    self.silu(sza[:, b * 512:(b + 1) * 512], ps[3 + b][:, :], tmpE[b], [("ps", 3 + b)], [("sza", b)],
                              ("tmpE", b))
                for g in range(4):
                    self.mm(ps[1 + g // 2][:, (g % 2) * 256:(g % 2) * 256 + 256], wsT[:, g, :],
                            vn[:, g * 256:(g + 1) * 256], True, True, ["wsT", ("vn", g // 2)], [("ps", 1 + g // 2)])
                for g in range(4):
                    self.act(mp[:, g * 256:(g + 1) * 256], ps[1 + g // 2][:, (g % 2) * 256:(g % 2) * 256 + 256],
                             AF.Identity, [("ps", 1 + g // 2), "const"], [("vm", g // 2)], bias=self.bs[:, g:g + 1])
                self.proj_tok(3, 512, W, 0)
                self.proj_tok(4, 512, W, 512)
                for b in range(2):
                    sl_ = slice(b * 512, (b + 1) * 512)
                    self.tt("dve", mp[:, sl_], ps[3 + b][:, :], mp[:, sl_], ALU.mult, [("ps", 3 + b), ("vm", b)],
                            [("vm", b)])
                    self.tt("pool", cat[par][:, sl_], mp[:, sl_], sza[:, sl_], ALU.mult, [("vm", b), ("sza", b)],
                            [("cat", par, b)])
                self.proj_tok(1, 512, W, 4608)
                self.proj_tok(2, 512, W, 5120)
                for b in range(2):
                    self.silu(szb[par][:, b * 512:(b + 1) * 512], ps[1 + b][:, :], tmpE[b], [("ps", 1 + b)],
                              [("szb", par, b)], ("tmpE", b))
            if n >= 1:
                m = n - 1
                pm = m % 2
                js = [j for j in (-1, 0, 1) if 0 <= m + j < NT]
                c0, c1 = (js[0] + 1) * 128, (js[-1] + 2) * 128
                Sb, PTb, Ob = ps[5], psb[6], ps[7]
                for h in range(8):
                    kvh, hh, hs = h // 4, h % 4, h % 2
                    for j in js:
                        self.mm(Sb[:, (j + 1) * 128:(j + 2) * 128], qT[pm][:, h, :], kT[(m + j) % 3][:, kvh, :], True, True,
                                [("qT", pm), ("kT", (m + j) % 3)], [("ps", 5)])
                    self.stt(ssb[hs][:, c0:c1], self.distp[:, c0:c1], -(2.0 ** -(h + 1)), Sb[:, c0:c1], ALU.mult, ALU.add,
                             ["const", ("ps", 5)], [("ssb", hs)])
                    self.sc.add("dve", (lambda o, i: (lambda e: e.tensor_reduce(o, i, AX.X, ALU.max, negate=True)))(
                        nrm[:, h:h + 1], ssb[hs][:, c0:c1]), [("ssb", hs)], [("nrm", h)], dur=550.0)
                    self.tt("dve", negm[:, h:h + 1], nrm[:, h:h + 1], self.nsink[:, h:h + 1], ALU.min,
                            [("nrm", h), "const2"], [("negm", h)])
                    self.act(pp[hs][:, c0:c1], ssb[hs][:, c0:c1], AF.Exp, [("ssb", hs), ("negm", h)],
                             [("pp", hs), ("rsum", h)], bias=negm[:, h:h + 1], accum=rsum[:, h:h + 1])
                    for j in js:
                        cs = slice((j + 1) * 128, (j + 2) * 128)
                        self.tr(PTb[:, cs], pp[hs][:, cs], [("pp", hs)], [("ps", 6)])
                    self.cp("act", pT[hs][:, c0:c1], PTb[:, c0:c1], [("ps", 6)], [("pT", hs)])
                    for j in js:
                        cs = slice((j + 1) * 128, (j + 2) * 128)
                        self.mm(Ob[:, hh * 128:(hh + 1) * 128], pT[hs][:, cs], vb[(m + j) % 3][:, kvh * 128:(kvh + 1) * 128],
                                j == js[0], j == js[-1], [("pT", hs), ("vb", (m + j) % 3)], [("ps", 7)])
                    if hh == 3:
                        g4 = slice(h - 3, h + 1)
                        self.tt("dve", t4, self.sinkbc[:, g4], negm[:, g4], ALU.add,
                                ["const"] + [("negm", q) for q in range(h - 3, h + 1)], ["t4"])
                        self.act(t4, t4, AF.Exp, ["t4"], ["t4"])
                        self.tt("dve", t4, t4, rsum[:, g4], ALU.add, ["t4"] + [("rsum", q) for q in range(h - 3, h + 1)],
                                ["t4"])
                        self.sc.add("dve", lambda e: e.reciprocal(rinv, t4), ["t4"], ["rinv"])
                        for q in range(4):
                            hq = h - 3 + q
                            self.stt(cat[pm][:, 1024 + hq * 128:1024 + (hq + 1) * 128], Ob[:, q * 128:(q + 1) * 128],
                                     rinv[:, q:q + 1], szb[pm][:, hq * 128:(hq + 1) * 128], ALU.mult, ALU.mult,
                                     [("ps", 7), "rinv", ("szb", pm, hq // 4)], [("cat", pm, 2 + hq // 4)])
                for half in range(2):
                    for kc in range(8):
                        c = half * 8 + kc
                        self.tr(psb[0][:, kc * 128:(kc + 1) * 128], cat[pm][:, c * 128:(c + 1) * 128],
                                [("cat", pm, c // 4)], [("ps", 0)])
                    self.cp("act", catT[:, half * 8:half * 8 + 8, :].rearrange("p a b -> p (a b)"), psb[0][:, 0:1024],
                            [("ps", 0)], [("catT", half)])
                for half in range(2):
                    for kc in range(16):
                        self.mm(ps[3 + half][:, :], catT[:, kc, :], WO[:, kc, half * 512:(half + 1) * 512], kc == 0, kc == 15,
                                [("catT", kc // 8), "WO"], [("ps", 3 + half)])
                xo = xs[m % 3]
                for half in range(2):
                    sl_ = slice(half * 512, (half + 1) * 512)
                    self.tt("dve", xo[:, sl_], ps[3 + half][:, :], xs[m % 3][:, sl_], ALU.add,
                            [("ps", 3 + half), ("xs", m % 3)], [("xs", m % 3)])
                self.dma("sp", dst[m * 128:(m + 1) * 128, :], xo, ("st", m % 3), [("xs", m % 3)], [("x1d", m)])

    def phase_scan(self, fwd):
        A, NT = self.A, self.NT
        NM = NT // 2
        a = A.alloc
        ps, psb = self.ps, self.psb
        W = a((8, 6144), BF16)
        wv = self.wio_d.ap().rearrange("(kc p) n -> p kc n", p=128)
        fcol = 2048 if fwd else 4096
        for bi, c0 in enumerate((6144, 0, fcol)):
            bo = (2, 0, 1)[bi]
            for kc in range(8):
                self.dma("pool", W[:, kc, bo * 2048:(bo + 1) * 2048], wv[:, kc, c0:c0 + 2048], ("W", kc), [],
                         [("W", kc)], cast=True)
        gO = self.load_vec(3, "gO")
        xs = [a((1024,), F32) for _ in range(4)]
        hT2 = a((8, 256), BF16)
        itok = [[a((2048,), BF16) for _ in range(2)] for _ in range(2)]
        st32 = a((16, 128), F32)
        stbf = a((16, 128), BF16)
        E = a((512,), F32)
        L1 = a((512,), F32)
        L2 = a((512,), F32)
        bc = a((512,), F32)
        wk = a((512,), F32)
        Eq = a((512,), F32)
        w2 = a((512,), F32)
        dn = [a((8,), F32) for _ in range(2)]
        qt = [a((512,), BF16) for _ in range(2)]
        kt = [a((512,), BF16) for _ in range(2)]
        kend = a((512,), BF16)
        kttok = [a((512,), BF16) for _ in range(2)]
        aT = a((512,), BF16)
        osb = [a((512,), F32) for _ in range(2)]
        obl = [a((512,), F32) for _ in range(2)]
        osq = a((512,), BF16)
        lnr = a((512,), F32)
        lb, lnoml = (self.lbf, self.lnomlf) if fwd else (self.lbb, self.lnomlb)
        mask = self.maskf if fwd else self.maskb
        res = self.resf if fwd else self.resb
        self.sc.add("pool", lambda e: e.memset(st32.rearrange("p a b -> p (a b)"), 0.0), [],
                    [("st32", h) for h in range(16)], dur=1200.0)
        self.sc.add("pool", lambda e: e.memset(stbf.rearrange("p a b -> p (a b)"), 0.0), [],
                    [("stbf", h) for h in range(16)], dur=700.0)
        morder = list(range(NM)) if fwd else list(range(NM - 1, -1, -1))
        c4order = (0, 1, 2, 3) if fwd else (3, 2, 1, 0)

        def load_x(idx):
            M = morder[idx]
            for j in range(2):
                sl = (idx % 2) * 2 + j
                n = 2 * M + j
                self.dma("sp", xs[sl], self.x1_d[n * 128:(n + 1) * 128, :], ("xs", sl), [("x1d", n)], [("xs", sl)])

        def rev(ap):
            return ap if fwd else ap[:, ::-1]

        def prologue(idx):
            if idx + 1 < NM:
                load_x(idx + 1)
            mp = idx % 2
            for j in range(2):
                sl = mp * 2 + j
                self.make_hT(xs[sl], ("xs", sl), gO, "gO", dst=hT2[:, :, j * 128:(j + 1) * 128], dk=("hT2", j))
            for j in range(2):
                for blk in range(4):
                    bank = (3, 5)[blk % 2]
                    pk = ("ps", bank)
                    for kc in range(8):
                        self.mm(ps[bank][:, :], hT2[:, kc, j * 128:(j + 1) * 128],
                                W[:, kc, 4096 + blk * 512:4096 + (blk + 1) * 512], kc == 0, kc == 7,
                                [("hT2", j), ("W", kc)], [pk])
                    cs = slice(blk * 512, (blk + 1) * 512)
                    self.cp("act", itok[mp][j][:, cs], ps[bank][:, :], [pk], [("itok", mp, j, blk)])

        def projf(bank, hh, col0):
            pk = ("ps", bank)
            out = ps[bank][:, hh * 256:(hh + 1) * 256]
            for kc in range(8):
                self.mm(out, W[:, kc, col0:col0 + 128], hT2[:, kc, :], kc == 0, kc == 7,
                        [("hT2", 0), ("hT2", 1), ("W", kc)], [pk])

        def stageA(idx, G):
            p = (idx * 8 + G) % 2
            qb, fb = (1, 2) if p == 0 else (6, 7)
            for hh in range(2):
                projf(qb, hh, (2 * G + hh) * 128)
            for hh in range(2):
                projf(fb, hh, 2048 + (2 * G + hh) * 128)
            Q, Fp = ps[qb], ps[fb]
            self.act(E, Fp[:, :], AF.Exp, [("ps", fb)], ["E"], scale=-1.0)
            self.act(L1, E, AF.Ln, ["E"], ["L1"], bias=1.0)
            for hh in range(2):
                h = 2 * G + hh
                cs = slice(hh * 256, (hh + 1) * 256)
                self.act(L2[:, cs], E[:, cs], AF.Ln, ["E", "const2"], ["L2"], scale=lb[:, h:h + 1], bias=1.0)
            self.tt("pool", L2, L2, L1, ALU.subtract, ["L2", "L1"], ["L2"])
            self.sc.add("dve", (lambda: (lambda e: e.tensor_tensor_scan(rev(bc), rev(res), rev(L2), 0.0, ALU.mult,
                                                                         ALU.add)))(), ["L2", "const"], ["bc"], dur=1250.0)
            bcv = bc.rearrange("p (a b) -> p a b", b=64)
            self.act(dn[p], bcv[:, :, 63] if fwd else bcv[:, :, 0], AF.Exp, ["bc"], [("dn", p)])
            self.act(Eq, Q[:, :], AF.Exp, [("ps", qb)], ["Eq"], scale=-1.0)
            self.act(Eq, Eq, AF.Ln, ["Eq"], ["Eq"], bias=1.0)
            self.tt("dve", wk, Fp[:, :], L1, ALU.add, [("ps", fb), "L1"], ["wk"])
            self.tt("pool", wk, wk, bc, ALU.add, ["wk", "bc"], ["wk"])
            for hh in range(2):
                h = 2 * G + hh
                cs = slice(hh * 256, (hh + 1) * 256)
                self.act(kt[p][:, cs], wk[:, cs], AF.Exp, ["wk", "const2"], [("kt", p)], scale=-1.0,
                         bias=lnoml[:, h:h + 1])
            self.tt("pool", w2, bc, Eq, ALU.subtract, ["bc", "Eq"], ["w2"])
            self.act(w2, w2, AF.Exp, ["w2"], ["w2"])
            self.tt("dve", qt[p], Q[:, :], w2, ALU.mult, [("ps", qb), "w2"], [("qt", p)])
            self.tt("pool", kend.rearrange("p (a b) -> p a b", b=64), kt[p].rearrange("p (a b) -> p a b", b=64),
                    dn[p].unsqueeze(2).to_broadcast([128, 8, 64]), ALU.mult, [("kt", p), ("dn", p)], ["kend"])
            for q4 in range(4):
                cs = slice(q4 * 128, (q4 + 1) * 128)
                self.tr(psb[0][:, cs], kend[:, cs], ["kend"], [("ps", 0)])
            self.cp("act", kttok[p], psb[0][:, 0:512], [("ps", 0)], [("kttok", p)])

        def stageB(idx, G):
            M = morder[idx]
            p = (idx * 8 + G) % 2
            mp = idx % 2
            for q4 in range(4):
                cs = slice(q4 * 128, (q4 + 1) * 128)
                self.mm(ps[3][:, cs], kt[p][:, cs], qt[p][:, cs], True, True, [("kt", p), ("qt", p)], [("ps", 3)])
            self.tt("dve", aT, ps[3][:, :], mask, ALU.mult, [("ps", 3), "const"], ["aT"])
            O = ps[4]
            for q4 in range(4):
                hh, j = q4 // 2, q4 % 2
                h = 2 * G + hh
                cs = slice(q4 * 128, (q4 + 1) * 128)
                self.mm(O[:, cs], itok[mp][j][:, h * 128:(h + 1) * 128], aT[:, cs], q4 == 0, False,
                        [("itok", mp, j, h // 4), "aT"], [("ps", 4)])
            for ci, c4 in enumerate(c4order):
                j, c = c4 // 2, c4 % 2
                for hh in range(2):
                    h = 2 * G + hh
                    cs = slice(hh * 256 + c4 * 64, hh * 256 + c4 * 64 + 64)
                    self.mm(O[:, cs], stbf[:, h, :], qt[p][:, cs], False, ci == 3 and hh == 1,
                            [("stbf", h), ("qt", p)], [("ps", 4)])
                for hh in range(2):
                    h = 2 * G + hh
                    q4 = hh * 2 + j
                    self.mm(ps[5][:, hh * 128:(hh + 1) * 128], kttok[p][c * 64:(c + 1) * 64, q4 * 128:(q4 + 1) * 128],
                            itok[mp][j][c * 64:(c + 1) * 64, h * 128:(h + 1) * 128], True, True,
                            [("kttok", p), ("itok", mp, j, h // 4)], [("ps", 5)])
                for hh in range(2):
                    h = 2 * G + hh
                    dcol = dn[p][:, hh * 4 + c4:hh * 4 + c4 + 1]
                    self.stt(st32[:, h, :], st32[:, h, :], dcol, ps[5][:, hh * 128:(hh + 1) * 128], ALU.mult, ALU.add,
                             [("st32", h), ("dn", p), ("ps", 5)], [("st32", h)])
                    self.cp("pool", stbf[:, h, :], st32[:, h, :], [("st32", h)], [("stbf", h)])
            if not fwd:
                ob_ = osb[G % 2]
                self.cp("act", ob_, O[:, :], [("ps", 4)], [("osb", G % 2)])
                self.dma("sp", self.ob_d[M, :, G * 512:(G + 1) * 512], ob_, ("st", G % 2), [("osb", G % 2)],
                         [("obd", M, G)])
            else:
                l_ = obl[G % 2]
                o_ = osb[G % 2]
                self.dma("sp", l_, self.ob_d[M, :, G * 512:(G + 1) * 512], ("ld", G % 2), [("obd", M, G)],
                         [("obl", G % 2)])
                self.tt("dve", l_, O[:, :], l_, ALU.add, [("ps", 4), ("obl", G % 2)], [("obl", G % 2)])
                self.act(osq, l_, AF.Square, [("obl", G % 2)], ["osq"])
                self.mm(ps[3][:, :], self.ones, osq, True, True, ["osq", "const"], [("ps", 3)])
                self.act(lnr, ps[3][:, :], AF.Ln, [("ps", 3)], ["lnr"], scale=1.0 / 128, bias=EPS)
                self.act(lnr, lnr, AF.Exp, ["lnr"], ["lnr"], scale=-0.5)
                for hh in range(2):
                    h = 2 * G + hh
                    cs = slice(hh * 256, (hh + 1) * 256)
                    self.stt(o_[:, cs], l_[:, cs], self.hg[:, h:h + 1], lnr[:, cs], ALU.mult, ALU.mult,
                             [("obl", G % 2), "const", "lnr"], [("osb", G % 2)])
                self.dma("sp", self.on_d[M, :, G * 512:(G + 1) * 512], o_, ("st", G % 2), [("osb", G % 2)],
                         [("ond", M, G)])

        load_x(0)
        items = [(idx, G) for idx in range(NM) for G in range(8)]
        for k, (idx, G) in enumerate(items):
            if G == 0:
                prologue(idx)
            stageA(idx, G)
            if k >= 1:
                stageB(*items[k - 1])
        stageB(*items[-1])

    def phase3(self):
        A, NT = self.A, self.NT
        NM = NT // 2
        a = A.alloc
        ps, psb = self.ps, self.psb
        W = a((8, 2048), BF16)
        WO = a((16, 1024), BF16)
        wv = self.wio_d.ap().rearrange("(kc p) n -> p kc n", p=128)
        for kc in range(8):
            self.dma("pool", W[:, kc, :], wv[:, kc, 8192:10240], ("W", kc), [], [("W", kc)], cast=True)
        self.dma("pool", WO, self.woo_d.ap().rearrange("(kc p) n -> p kc n", p=128), "WO", [], ["WO"], cast=True)
        gO = self.load_vec(3, "gO")
        gF = self.load_vec(4, "gF")
        xs = [a((1024,), F32) for _ in range(4)]
        hT2 = a((8, 256), BF16)
        onl = [a((4096,), F32) for _ in range(2)]
        tmpE = [a((512,), F32) for _ in range(2)]
        szt = [a((512,), F32) for _ in range(2)]
        yT = a((16, 256), BF16)
        x2 = [a((1024,), F32) for _ in range(2)]

        def load(M):
            mp = M % 2
            for j in range(2):
                n = 2 * M + j
                sl = mp * 2 + j
                self.dma("sp", xs[sl], self.x1_d[n * 128:(n + 1) * 128, :], ("xs", sl), [("x1d", n)], [("xs", sl)])
            for hf in range(2):
                self.dma("sp", onl[mp][:, hf * 2048:(hf + 1) * 2048], self.on_d[M, :, hf * 2048:(hf + 1) * 2048],
                         ("ld", mp, hf), [("ond", M, g) for g in range(4 * hf, 4 * hf + 4)], [("onl", mp, hf)])

        def stA(M, G):
            mp = M % 2
            if G == 0:
                for j in range(2):
                    sl = mp * 2 + j
                    self.make_hT(xs[sl], ("xs", sl), gO, "gO", dst=hT2[:, :, j * 128:(j + 1) * 128], dk=("hT2", j))
            bank = 1 + G % 2
            pk = ("ps", bank)
            for hh in range(2):
                col0 = (2 * G + hh) * 128
                for kc in range(8):
                    self.mm(ps[bank][:, hh * 256:(hh + 1) * 256], W[:, kc, col0:col0 + 128], hT2[:, kc, :], kc == 0,
                            kc == 7, [("hT2", 0), ("hT2", 1), ("W", kc)], [pk])

        def stB(M, G):
            mp = M % 2
            bank = 1 + G % 2
            self.silu(szt[G % 2], ps[bank][:, :], tmpE[G % 2], [("ps", bank)], [("szt", G % 2)], ("tmpE", G % 2))
            self.tt("pool", yT[:, 2 * G:2 * G + 2, :].rearrange("p a b -> p (a b)"), szt[G % 2],
                    onl[mp][:, G * 512:(G + 1) * 512], ALU.mult, [("szt", G % 2), ("onl", mp, G // 4)], [("yT", G)])
            for hh in range(2):
                h = 2 * G + hh
                for j in range(2):
                    for half in range(2):
                        bk = 3 + 2 * j + half
                        self.mm(ps[bk][:, :], yT[:, h, j * 128:(j + 1) * 128], WO[:, h, half * 512:(half + 1) * 512],
                                h == 0, h == 15, [("yT", G), "WO"], [("ps", bk)])
            if G == 7:
                for j in range(2):
                    n = 2 * M + j
                    sl = mp * 2 + j
                    xo = x2[j]
                    for half in range(2):
                        sl_ = slice(half * 512, (half + 1) * 512)
                        bk = 3 + 2 * j + half
                        self.tt("dve", xo[:, sl_], ps[bk][:, :], xs[sl][:, sl_], ALU.add,
                                [("ps", bk), ("xs", sl)], [("x2", j)])
                    self.rms(xo, ("x2", j))
                    self.stt(xo, xo, self.rstd, gF, ALU.mult, ALU.mult, [("x2", j), "rstd", "gF"], [("x2", j)])
                    self.dma("sp", self.out_d[n * 128:(n + 1) * 128, :], xo, ("st", j), [("x2", j)], [("outd", n)])
                if M + 2 < NM:
                    load(M + 2)

        load(0)
        if NM > 1:
            load(1)
        items = [(M, G) for M in range(NM) for G in range(8)]
        for k, it in enumerate(items):
            stA(*it)
            if k >= 1:
                stB(*items[k - 1])
        stB(*items[-1])


def _host_consts():
    c = np.zeros((128, 3200), np.float32)
    c[:, 0:128] = np.eye(128, dtype=np.float32)
    c[:, 128:256] = 1.0
    s = np.arange(128)[:, None]
    t = np.arange(128)[None, :]
    same = (s // 64) == (t // 64)
    mf = (same & (s <= t)).astype(np.float32)
    mb = (same & (s >= t)).astype(np.float32)
    c[:, 256:768] = np.tile(mf, (1, 4))
    c[:, 768:1280] = np.tile(mb, (1, 4))
    col = np.arange(512)
    c[:, 1280:1792] = (col % 64 != 0).astype(np.float32)[None, :]
    c[:, 1792:2304] = (col % 64 != 63).astype(np.float32)[None, :]
    qi = np.arange(128)[:, None]
    kj = np.arange(384)[None, :] - 128
    dist = np.abs(kj - qi).astype(np.float32)
    c[:, 2304:2688] = np.where(dist <= 128, dist, 1.0e9)
    return c


_PROG_CACHE = {}


def _get_prog(NT, upto=3):
    key = (NT, upto)
    if key not in _PROG_CACHE:
        p = Prog(NT, upto)
        p.build()
        _PROG_CACHE[key] = p
    return _PROG_CACHE[key]


def make_in_maps(inputs, ncores, S):
    f = lambda v: np.ascontiguousarray(np.asarray(v, dtype=np.float32))
    x = f(inputs["x"])
    vecs = np.stack([f(inputs["norm_g_even"])[0], f(inputs["gmlp_ln_g"])[0], f(inputs["gmlp_ln_b"])[0],
                     f(inputs["norm_g_odd"])[0], f(inputs["final_norm_g"])], axis=0)
    ws = f(inputs["gmlp_w_s"])[0]
    wsT = np.ascontiguousarray(ws.transpose(2, 0, 1).reshape(128, 512))
    cols = np.zeros((128, 128), np.float32)
    cols[:, 0:4] = f(inputs["gmlp_b_s"])[0].T
    gf = f(inputs["hgrn_gamma_fwd"]).reshape(2, 16, 128)
    gb = f(inputs["hgrn_gamma_bwd"]).reshape(2, 16, 128)
    cols[:, 4:20] = gf[0].T
    cols[:, 20:36] = gf[1].T
    cols[:, 36:52] = gb[0].T
    cols[:, 52:68] = gb[1].T
    cols[:, 68:84] = f(inputs["hgrn_head_norm_g"])[0].reshape(16, 128).T
    cols[0:64, 84] = 1.0
    cols[64:128, 85] = 1.0
    shared = {
        "w_in_even": f(inputs["w_in_even"])[0], "w_out_even": f(inputs["w_out_even"])[0],
        "w_in_odd": f(inputs["w_in_odd"])[0], "w_out_odd": f(inputs["w_out_odd"])[0],
        "vecs": np.ascontiguousarray(vecs), "wsT": wsT, "cols": cols, "sink": f(inputs["attn_sink"]),
        "cst": _host_consts(),
    }
    maps = []
    for c in range(ncores):
        m = dict(shared)
        m["x"] = np.ascontiguousarray(x[c, :S])
        maps.append(m)
    return maps


def kernel(**inputs):
    x = np.asarray(inputs["x"])
    B, S, _ = x.shape
    prog = _get_prog(S // 128)
    maps = make_in_maps(inputs, B, S)
    res = run_bass_kernel_spmd(prog.nc, maps, core_ids=list(range(B)))
    return np.stack([np.asarray(r["out"]).reshape(S, D) for r in res.results], axis=0).astype(np.float32)
```

```python
import os
import numpy as np
import concourse.bass as bass
import concourse.mybir as mybir
from concourse.bass_utils import run_bass_kernel_spmd

F32 = mybir.dt.float32
BF16 = mybir.dt.bfloat16
AF = mybir.ActivationFunctionType
ALU = mybir.AluOpType
AX = mybir.AxisListType

KSTOP = int(os.environ.get('KSTOP', '99'))
D = 1024
EPS = 1e-6
NCORES = 8
SEQ = 8192
ARENA_WORDS = 53000


class _Op:
    __slots__ = ("eng", "fn", "reads", "writes", "dma", "deps", "alld", "sig", "need", "dur", "seg")

    def __init__(self, eng, fn, reads, writes, dma, dur, seg):
        self.eng, self.fn, self.reads, self.writes, self.dma = eng, fn, tuple(reads), tuple(writes), dma
        self.deps = ()
        self.alld = ()
        self.sig = None
        self.need = False
        self.dur = dur
        self.seg = seg


class Sched:
    ENGS = ("pe", "act", "dve", "pool", "sp")
    XLAT = 350.0

    def __init__(self):
        self.ops = []
        self.last_w = {}
        self.readers = {}
        self.seg = 0

    def add(self, eng, fn, r=(), w=(), dma=None, dur=300.0):
        ops = self.ops
        i = len(ops)
        w = tuple(w) + tuple(k for k in r if isinstance(k, tuple) and k[0] == "ps" and k not in w)
        op = _Op(eng, fn, r, w, dma, dur, self.seg)
        deps = set()
        for k in op.reads:
            j = self.last_w.get(k)
            if j is not None:
                deps.add(j)
        for k in op.writes:
            j = self.last_w.get(k)
            if j is not None:
                deps.add(j)
            deps.update(self.readers.get(k, ()))
        deps = set(j for j in deps if ops[j].seg == self.seg)
        op.alld = deps
        pr = set()
        for j in deps:
            oj = ops[j]
            if oj.dma is None and dma is None and oj.eng == eng:
                if eng == "pe":
                    continue
                if not any(k in oj.writes for k in op.reads):
                    continue
            pr.add(j)
        op.deps = pr
        for k in op.reads:
            self.readers.setdefault(k, []).append(i)
        for k in op.writes:
            self.last_w[k] = i
            self.readers[k] = []
        ops.append(op)
        return i

    def barrier(self):
        self.seg += 1

    def _sched_segment(self, idxs):
        import heapq
        ops = self.ops
        nd = {}
        succ = {}
        for i in idxs:
            d = ops[i].alld
            nd[i] = len(d)
            for j in d:
                succ.setdefault(j, []).append(i)
        free = {e: 0.0 for e in self.ENGS}
        fut = {e: [] for e in self.ENGS}
        av = {e: [] for e in self.ENGS}
        rt = {}
        fin = {}
        for i in idxs:
            if nd[i] == 0:
                heapq.heappush(fut[ops[i].eng], (0.0, i))
        order = []
        left = len(idxs)
        while left:
            best = None
            for e in self.ENGS:
                f, a = fut[e], av[e]
                while f and f[0][0] <= free[e]:
                    heapq.heappush(a, heapq.heappop(f)[1])
                if a:
                    cand = (free[e], a[0], e, True)
                elif f:
                    cand = (f[0][0], f[0][1], e, False)
                else:
                    continue
                if best is None or cand[:2] < best[:2]:
                    best = cand
            start, i, e, fa = best
            if fa:
                heapq.heappop(av[e])
            else:
                heapq.heappop(fut[e])
            op = ops[i]
            if op.dma is not None:
                free[e] = start + 120.0
            else:
                free[e] = start + op.dur
            fin[i] = start + op.dur
            order.append(i)
            left -= 1
            for sidx in succ.get(i, ()):
                so = ops[sidx]
                if so.eng == e and op.dma is None and so.dma is None:
                    lat = 60.0 if sidx in so.deps or i in so.deps else 0.0
                else:
                    lat = self.XLAT
                t = fin[i] + lat
                if rt.get(sidx, 0.0) < t:
                    rt[sidx] = t
                nd[sidx] -= 1
                if nd[sidx] == 0:
                    heapq.heappush(fut[so.eng], (rt[sidx], sidx))
        self.makespan.append(max(fin.values()) if fin else 0.0)
        return order

    def schedule(self):
        ops = self.ops
        nseg = self.seg + 1
        segs = [[] for _ in range(nseg)]
        for i, op in enumerate(ops):
            segs[op.seg].append(i)
        self.makespan = []
        order = []
        for sg in segs:
            o = self._sched_segment(sg)
            order.extend(o)
            last = {}
            for i in o:
                op = ops[i]
                if op.dma is not None:
                    last[("d", op.dma)] = i
                elif op.fn is not None:
                    last[("e", op.eng)] = i
            deps = set(last.values())
            for e in self.ENGS:
                order.append(("bar", e, deps))
        self.order = order

    def finalize(self, nc, stack):
        ops = self.ops
        self.schedule()
        for op in ops:
            for j in op.deps:
                ops[j].need = True
        for it in self.order:
            if isinstance(it, tuple):
                for j in it[2]:
                    ops[j].need = True
        sems = {}
        cnt = {}

        def sem_for(key):
            if key not in sems:
                sems[key] = stack.enter_context(nc.semaphore("s%d" % len(sems)))
                cnt[key] = 0
            return sems[key]

        for e in self.ENGS:
            sem_for(("e", e))
        for it in self.order:
            if isinstance(it, tuple):
                continue
            op = ops[it]
            if op.dma is not None:
                k = ("d", op.dma)
                sem_for(k)
                cnt[k] += 16
                op.sig = (k, cnt[k])
            elif op.need and op.fn is not None:
                k = ("e", op.eng)
                cnt[k] += 1
                op.sig = (k, cnt[k])
        self.sems = sems
        return sems

    def emit(self, eng_name, eng):
        ops = self.ops
        sems = self.sems
        known = {}
        for it in self.order:
            if isinstance(it, tuple):
                if it[1] != eng_name:
                    continue
                deps, op = it[2], None
            else:
                op = ops[it]
                if op.eng != eng_name:
                    continue
                deps = op.deps
            need = {}
            for j in deps:
                sg = ops[j].sig
                if sg is None:
                    continue
                if need.get(sg[0], 0) < sg[1]:
                    need[sg[0]] = sg[1]
            for k, v in need.items():
                if known.get(k, 0) < v:
                    eng.wait_ge(sems[k], v)
                    known[k] = v
            if op is None or op.fn is None:
                continue
            ins = op.fn(eng)
            if op.sig is not None:
                ins.then_inc(sems[op.sig[0]], 16 if op.dma is not None else 1)


class Arena:
    def __init__(self, t, words):
        self.t = t
        self.words = words
        self.off = 0

    def alloc(self, shape, dtype):
        n = 1
        for s in shape:
            n *= s
        words = n if dtype == F32 else (n + 1) // 2
        words = (words + 7) // 8 * 8
        assert self.off + words <= self.words, ("arena overflow", self.off, words)
        ap = self.t[:, self.off:self.off + words]
        self.off += words
        if dtype != F32:
            ap = ap.bitcast(dtype)
        ap = ap[:, 0:n]
        if len(shape) == 2:
            ap = ap.rearrange("p (a b) -> p a b", a=shape[0], b=shape[1])
        elif len(shape) == 3:
            ap = ap.rearrange("p (a b c) -> p a b c", a=shape[0], b=shape[1], c=shape[2])
        return ap


class Prog:
    def __init__(self, NT, upto=3):
        self.NT = NT
        self.S = NT * 128
        self.upto = upto
        self.sc = Sched()
        self.nc = bass.Bass("TRN2", target_bir_lowering=False)

    @staticmethod
    def _n(ap):
        n = 1
        for v in ap.shape[1:]:
            n *= v
        return n

    def mm(self, out, lhsT, rhs, start, stop, r, w):
        d = max(self._n(rhs) / 2.4, 104.0)
        self.sc.add("pe", lambda e: e.matmul(out, lhsT, rhs, start=start, stop=stop), r, w, dur=d)

    def tr(self, out, in_, r, w):
        ident = self.ident
        self.sc.add("pe", lambda e: e.transpose(out, in_, ident), tuple(r) + ("const",), w, dur=110.0)

    def act(self, out, in_, func, r, w, bias=None, scale=None, accum=None):
        kw = {}
        if bias is not None:
            kw["bias"] = bias
        if scale is not None:
            kw["scale"] = scale
        if accum is not None:
            kw["accum_out"] = accum
        d = 210.0 + 0.75 * self._n(in_) + (100.0 if accum is not None else 0.0)
        self.sc.add("act", lambda e: e.activation(out, in_, func, **kw), r, w, dur=d)

    def _vd(self, eng, n, k=1.35):
        return (110.0 + 2.25 * n) if eng == "pool" else (70.0 + k * n)

    def ts(self, eng, out, in0, s1, op0, r, w, s2=None, op1=None):
        d = self._vd(eng, self._n(in0))
        if op1 is None:
            self.sc.add(eng, lambda e: e.tensor_scalar(out, in0, s1, None, op0), r, w, dur=d)
        else:
            self.sc.add(eng, lambda e: e.tensor_scalar(out, in0, s1, s2, op0, op1), r, w, dur=d)

    def tt(self, eng, out, in0, in1, op, r, w):
        self.sc.add(eng, lambda e: e.tensor_tensor(out, in0, in1, op), r, w, dur=self._vd(eng, self._n(in0)))

    def stt(self, out, in0, scalar, in1, op0, op1, r, w):
        self.sc.add("dve", lambda e: e.scalar_tensor_tensor(out, in0, scalar, in1, op0, op1), r, w,
                    dur=90.0 + 2.0 * self._n(in0))

    def cp(self, eng, out, in_, r, w):
        if eng == "act":
            self.sc.add("act", lambda e: e.copy(out, in_), r, w, dur=210.0 + 0.75 * self._n(in_))
        else:
            self.sc.add(eng, lambda e: e.tensor_copy(out, in_), r, w, dur=self._vd(eng, self._n(in_), 1.0))

    def dma(self, eng, out, in_, key, r, w, cast=False):
        nbytes = 128 * self._n(out) * 4
        d = 2500.0 + nbytes / 150.0
        if cast:
            self.sc.add(eng, lambda e: e.dma_start(out=out, in_=in_, max_dma_last_dim=4096), r, w, dma=key, dur=d)
        else:
            self.sc.add(eng, lambda e: e.dma_start(out=out, in_=in_), r, w, dma=key, dur=d)

    def silu(self, out, src, tmp, r, w, tk):
        self.act(tmp, src, AF.Exp, r, [tk], scale=-1.0)
        self.act(tmp, tmp, AF.Ln, [tk], [tk], bias=1.0)
        self.act(tmp, tmp, AF.Exp, [tk], [tk], scale=-1.0)
        self.tt("dve", out, src, tmp, ALU.mult, tuple(r) + (tk,), w)

    def rms(self, x_ap, xk, n=1024):
        self.act(self.hb, x_ap, AF.Square, [xk], ["hb", "ss"], accum=self.ss)
        self.act(self.lnv, self.ss, AF.Ln, ["ss"], ["lnv"], scale=1.0 / n, bias=EPS)
        self.act(self.rstd, self.lnv, AF.Exp, ["lnv"], ["rstd"], scale=-0.5)

    def make_hT(self, x_ap, xk, gbc, gk, dst=None, dk="hT"):
        self.rms(x_ap, xk)
        self.stt(self.hb, x_ap, self.rstd, gbc, ALU.mult, ALU.mult, [xk, "rstd", gk], ["hb"])
        Tb = self.psb[0]
        for kc in range(8):
            self.tr(Tb[:, kc * 128:(kc + 1) * 128], self.hb[:, kc * 128:(kc + 1) * 128], ["hb"], [("ps", 0)])
        if dst is None:
            self.cp("act", self.hT.rearrange("p a b -> p (a b)"), Tb[:, 0:1024], [("ps", 0)], ["hT"])
        else:
            self.cp("act", dst, Tb[:, 0:1024].rearrange("p (a b) -> p a b", b=128), [("ps", 0)], [dk])

    def proj_tok(self, bank, ncols, W, col0):
        pk = ("ps", bank)
        out = self.ps[bank][:, 0:ncols]
        for kc in range(8):
            self.mm(out, self.hT[:, kc, :], W[:, kc, col0:col0 + ncols], kc == 0, kc == 7, ["hT", ("W", kc)], [pk])

    def proj_feat(self, bank, hh, W, col0):
        pk = ("ps", bank)
        out = self.ps[bank][:, hh * 128:(hh + 1) * 128]
        for kc in range(8):
            self.mm(out, W[:, kc, col0:col0 + 128], self.hT[:, kc, :], kc == 0, kc == 7, ["hT", ("W", kc)], [pk])

    def build(self):
        nc = self.nc
        NT, S = self.NT, self.S
        dt = nc.dram_tensor
        self.x_d = dt("x", [S, D], F32, kind="ExternalInput")
        self.wie_d = dt("w_in_even", [D, 5632], F32, kind="ExternalInput")
        self.woe_d = dt("w_out_even", [2048, D], F32, kind="ExternalInput")
        self.wio_d = dt("w_in_odd", [D, 10240], F32, kind="ExternalInput")
        self.woo_d = dt("w_out_odd", [2048, D], F32, kind="ExternalInput")
        self.vecs_d = dt("vecs", [5, D], F32, kind="ExternalInput")
        self.wsT_d = dt("wsT", [128, 512], F32, kind="ExternalInput")
        self.cols_d = dt("cols", [128, 128], F32, kind="ExternalInput")
        self.sink_d = dt("sink", [1, 8], F32, kind="ExternalInput")
        self.cst_d = dt("cst", [128, 3200], F32, kind="ExternalInput")
        self.out_d = dt("out", [S, D], F32, kind="ExternalOutput")
        self.x1_d = dt("x1s", [S, D], F32, kind="Internal")
        self.ob_d = dt("obs", [NT // 2, 128, 4096], F32, kind="Internal")
        self.on_d = dt("ons", [NT // 2, 128, 4096], F32, kind="Internal")

        from contextlib import ExitStack
        with ExitStack() as st:
            arena_t = st.enter_context(nc.sbuf_tensor("arena", [128, ARENA_WORDS], F32))
            self.ps = [st.enter_context(nc.psum_tensor("ps%d" % b, [128, 512], F32)) for b in range(8)]
            self.psb = [p.bitcast(BF16) for p in self.ps]
            self.A = Arena(arena_t, ARENA_WORDS)
            self.consts()
            base = self.A.off
            self.phase0()
            if self.upto >= 1 and self.upto != 12:
                self.sc.barrier()
                self.A.off = base
                self.phase_scan(fwd=False)
            if self.upto >= 2:
                self.sc.barrier()
                self.A.off = base
                self.phase_scan(fwd=True)
            if self.upto >= 3 and self.upto != 12:
                self.sc.barrier()
                self.A.off = base
                self.phase3()
            self.sc.finalize(nc, st)
            with nc.Block() as block:
                sc = self.sc

                @block.sync
                def _(e):
                    sc.emit("sp", e)

                @block.scalar
                def _(e):
                    sc.emit("act", e)

                @block.vector
                def _(e):
                    sc.emit("dve", e)

                @block.gpsimd
                def _(e):
                    sc.emit("pool", e)

                @block.tensor
                def _(e):
                    sc.emit("pe", e)
        return nc

    def consts(self):
        A = self.A
        a = A.alloc
        self.ident = a((128,), BF16)
        self.ones = a((128,), BF16)
        self.maskf = a((512,), BF16)
        self.maskb = a((512,), BF16)
        self.resf = a((512,), F32)
        self.resb = a((512,), F32)
        self.distp = a((384,), F32)
        self.cols = a((128,), F32)
        self.sinkbc = a((8,), F32)
        self.nsink = a((8,), F32)
        self.lbf = a((16,), F32)
        self.omlf = a((16,), F32)
        self.lbb = a((16,), F32)
        self.omlb = a((16,), F32)
        self.lnomlf = a((16,), F32)
        self.lnomlb = a((16,), F32)
        self.ss = a((1,), F32)
        self.lnv = a((1,), F32)
        self.rstd = a((1,), F32)
        self.hb = a((1024,), BF16)
        self.hT = a((8, 128), BF16)
        c = self.cst_d
        p = "pool"
        k = "const"
        self.dma(p, self.ident, c[:, 0:128], "c0", [], [k], cast=True)
        self.dma(p, self.ones, c[:, 128:256], "c1", [], [k], cast=True)
        self.dma(p, self.maskf, c[:, 256:768], "c2", [], [k], cast=True)
        self.dma(p, self.maskb, c[:, 768:1280], "c3", [], [k], cast=True)
        self.dma("sp", self.resf, c[:, 1280:1792], "c4", [], [k])
        self.dma("sp", self.resb, c[:, 1792:2304], "c5", [], [k])
        self.dma("sp", self.distp, c[:, 2304:2688], "c6", [], [k])
        self.dma("sp", self.cols, self.cols_d[:, :], "c7", [], [k])
        self.dma("sp", self.sinkbc, self.sink_d[0:1, :].partition_broadcast(128), "c8", [], [k])
        self.ts("dve", self.nsink, self.sinkbc, -1.0, ALU.mult, [k], ["const2"])
        for (lb, oml, lno, o) in ((self.lbf, self.omlf, self.lnomlf, 4), (self.lbb, self.omlb, self.lnomlb, 36)):
            self.tt("dve", lb, self.cols[:, o:o + 16], self.cols[:, o + 16:o + 32], ALU.subtract, [k], ["const2"])
            self.act(lb, lb, AF.Exp, ["const2"], ["const2"])
            self.ts("dve", lb, lb, 1.0, ALU.add, ["const2"], ["const2"])
            self.sc.add("dve", (lambda lb: (lambda e: e.reciprocal(lb, lb)))(lb), ["const2"], ["const2"])
            self.ts("dve", oml, lb, -1.0, ALU.mult, ["const2"], ["const2"], 1.0, ALU.add)
            self.act(lno, oml, AF.Ln, ["const2"], ["const2"])
        self.bs = self.cols[:, 0:4]
        self.hg = self.cols[:, 68:84]
        self.cmask = self.cols[:, 84:86]

    def load_vec(self, i, key):
        t = self.A.alloc((1024,), F32)
        self.dma("sp", t, self.vecs_d[i:i + 1, :].partition_broadcast(128), key, [], [key])
        return t

    def phase0(self):
        A, NT = self.A, self.NT
        a = A.alloc
        ps, psb = self.ps, self.psb
        W = a((8, 5632), BF16)
        WO = a((16, 1024), BF16)
        gE = self.load_vec(0, "gE")
        lng = self.load_vec(1, "lng")
        lnb = self.load_vec(2, "lnb")
        wsT = a((4, 128), BF16)
        self.dma("pool", wsT.rearrange("p a b -> p (a b)"), self.wsT_d[:, :], "wsT", [], ["wsT"], cast=True)
        wv = self.wie_d.ap().rearrange("(kc p) n -> p kc n", p=128)
        for kc in range(8):
            self.dma("pool", W[:, kc, :], wv[:, kc, :], ("W", kc), [], [("W", kc)], cast=True)
        self.dma("pool", WO, self.woe_d.ap().rearrange("(kc p) n -> p kc n", p=128), "WO", [], ["WO"], cast=True)
        xs = [a((1024,), F32) for _ in range(3)]
        qT = [a((8, 128), BF16) for _ in range(2)]
        kT = [a((2, 128), BF16) for _ in range(3)]
        vb = [a((256,), BF16) for _ in range(3)]
        szb = [a((1024,), BF16) for _ in range(2)]
        cat = [a((2048,), BF16) for _ in range(2)]
        catT = a((16, 128), BF16)
        vh = a((1024,), F32)
        vn = a((1024,), BF16)
        mp = vh
        sza = a((1024,), F32)
        tmpE = [a((512,), F32) for _ in range(2)]
        stats = a((2, 6), F32)
        mv = a((2,), F32)
        lnv2 = a((1,), F32)
        rstd2 = a((1,), F32)
        ssb = [a((384,), F32) for _ in range(2)]
        pp = [a((384,), BF16) for _ in range(2)]
        pT = [a((384,), BF16) for _ in range(2)]
        nrm = a((8,), F32)
        negm = a((8,), F32)
        rsum = a((8,), F32)
        t4 = a((4,), F32)
        rinv = a((4,), F32)
        dst = self.x1_d if self.upto >= 1 else self.out_d

        def load_x(n):
            sl = n % 3
            self.dma("sp", xs[sl], self.x_d[n * 128:(n + 1) * 128, :], ("xs", sl), [], [("xs", sl)])

        load_x(0)
        for n in range(NT + 1):
            if n + 1 < NT:
                load_x(n + 1)
            if n < NT:
                sl, par, s3 = n % 3, n % 2, n % 3
                xk = ("xs", sl)
                self.make_hT(xs[sl], xk, gE, "gE")
                self.proj_tok(1, 512, W, 1024)
                self.proj_tok(2, 512, W, 1536)
                for h in range(8):
                    self.proj_feat(3 + h // 4, h % 4, W, 3072 + h * 128)
                for b in range(2):
                    self.sc.add("dve", (lambda o, i: (lambda e: e.bn_stats(o, i)))(stats[:, b, :], ps[1 + b][:, :]),
                                [("ps", 1 + b)], ["stats"], dur=700.0)
                self.sc.add("dve", lambda e: e.bn_aggr(mv, stats.rearrange("p a b -> p (a b)")), ["stats"], ["mv"])
                self.act(lnv2, mv[:, 1:2], AF.Ln, ["mv"], ["lnv2"], bias=EPS)
                self.act(rstd2, lnv2, AF.Exp, ["lnv2"], ["rstd2"], scale=-0.5)
                for b in range(2):
                    self.ts("dve", vh[:, b * 512:(b + 1) * 512], ps[1 + b][:, :], mv[:, 0:1], ALU.subtract,
                            [("ps", 1 + b), "mv", "rstd2"], [("vm", b)], rstd2, ALU.mult)
                for b in range(2):
                    sl_ = slice(b * 512, (b + 1) * 512)
                    self.tt("pool", vh[:, sl_], vh[:, sl_], lng[:, sl_], ALU.mult, [("vm", b), "lng"], [("vm", b)])
                    self.tt("pool", vn[:, sl_], vh[:, sl_], lnb[:, sl_], ALU.add, [("vm", b), "lnb"], [("vn", b)])
                for g in range(2):
                    self.act(qT[par][:, 4 * g:4 * g + 4, :].rearrange("p a b -> p (a b)"), ps[3 + g][:, :], AF.Identity,
                             [("ps", 3 + g)], [("qT", par)], scale=128.0 ** -0.5)
                for hk in range(2):
                    self.proj_feat(1, hk, W, 4096 + hk * 128)
                self.proj_tok(2, 256, W, 4352)
                self.cp("dve", kT[s3].rearrange("p a b -> p (a b)"), ps[1][:, 0:256], [("ps", 1)], [("kT", s3)])
                self.cp("dve", vb[s3], ps[2][:, 0:256], [("ps", 2)], [("vb", s3)])
                self.proj_tok(3, 512, W, 2048)
                self.proj_tok(4, 512, W, 2560)
                for b in range(2):
                    self.silu(sza[:, b * 512:(b + 1) * 512], ps[3 + b][:, :], tmpE[b], [("ps", 3 + b)], [("sza", b)],
                              ("tmpE", b))
                for g in range(4):
                    self.mm(ps[1 + g // 2][:, (g % 2) * 256:(g % 2) * 256 + 256], wsT[:, g, :],
                            vn[:, g * 256:(g + 1) * 256], True, True, ["wsT", ("vn", g // 2)], [("ps", 1 + g // 2)])
                for g in range(4):
                    self.act(mp[:, g * 256:(g + 1) * 256], ps[1 + g // 2][:, (g % 2) * 256:(g % 2) * 256 + 256],
                             AF.Identity, [("ps", 1 + g // 2), "const"], [("vm", g // 2)], bias=self.bs[:, g:g + 1])
                self.proj_tok(3, 512, W, 0)
                self.proj_tok(4, 512, W, 512)
                for b in range(2):
                    sl_ = slice(b * 512, (b + 1) * 512)
                    self.tt("dve", mp[:, sl_], ps[3 + b][:, :], mp[:, sl_], ALU.mult, [("ps", 3 + b), ("vm", b)],
                            [("vm", b)])
                    self.tt("pool", cat[par][:, sl_], mp[:, sl_], sza[:, sl_], ALU.mult, [("vm", b), ("sza", b)],
                            [("cat", par, b)])
                self.proj_tok(1, 512, W, 4608)
                self.proj_tok(2, 512, W, 5120)
                for b in range(2):
                    self.silu(szb[par][:, b * 512:(b + 1) * 512], ps[1 + b][:, :], tmpE[b], [("ps", 1 + b)],
                              [("szb", par, b)], ("tmpE", b))
            if n >= 1:
                m = n - 1
                pm = m % 2
                js = [j for j in (-1, 0, 1) if 0 <= m + j < NT]
                c0, c1 = (js[0] + 1) * 128, (js[-1] + 2) * 128
                Sb, PTb, Ob = ps[5], psb[6], ps[7]
                for h in range(8):
                    kvh, hh, hs = h // 4, h % 4, h % 2
                    for j in js:
                        self.mm(Sb[:, (j + 1) * 128:(j + 2) * 128], qT[pm][:, h, :], kT[(m + j) % 3][:, kvh, :], True, True,
                                [("qT", pm), ("kT", (m + j) % 3)], [("ps", 5)])
                    self.stt(ssb[hs][:, c0:c1], self.distp[:, c0:c1], -(2.0 ** -(h + 1)), Sb[:, c0:c1], ALU.mult, ALU.add,
                             ["const", ("ps", 5)], [("ssb", hs)])
                    self.sc.add("dve", (lambda o, i: (lambda e: e.tensor_reduce(o, i, AX.X, ALU.max, negate=True)))(
                        nrm[:, h:h + 1], ssb[hs][:, c0:c1]), [("ssb", hs)], [("nrm", h)], dur=550.0)
                    self.tt("dve", negm[:, h:h + 1], nrm[:, h:h + 1], self.nsink[:, h:h + 1], ALU.min,
                            [("nrm", h), "const2"], [("negm", h)])
                    self.act(pp[hs][:, c0:c1], ssb[hs][:, c0:c1], AF.Exp, [("ssb", hs), ("negm", h)],
                             [("pp", hs), ("rsum", h)], bias=negm[:, h:h + 1], accum=rsum[:, h:h + 1])
                    for j in js:
                        cs = slice((j + 1) * 128, (j + 2) * 128)
                        self.tr(PTb[:, cs], pp[hs][:, cs], [("pp", hs)], [("ps", 6)])
                    self.cp("act", pT[hs][:, c0:c1], PTb[:, c0:c1], [("ps", 6)], [("pT", hs)])
                    for j in js:
                        cs = slice((j + 1) * 128, (j + 2) * 128)
                        self.mm(Ob[:, hh * 128:(hh + 1) * 128], pT[hs][:, cs], vb[(m + j) % 3][:, kvh * 128:(kvh + 1) * 128],
                                j == js[0], j == js[-1], [("pT", hs), ("vb", (m + j) % 3)], [("ps", 7)])
                    if hh == 3:
                        g4 = slice(h - 3, h + 1)
                        self.tt("dve", t4, self.sinkbc[:, g4], negm[:, g4], ALU.add,
                                ["const"] + [("negm", q) for q in range(h - 3, h + 1)], ["t4"])
                        self.act(t4, t4, AF.Exp, ["t4"], ["t4"])
                        self.tt("dve", t4, t4, rsum[:, g4], ALU.add, ["t4"] + [("rsum", q) for q in range(h - 3, h + 1)],
                                ["t4"])
                        self.sc.add("dve", lambda e: e.reciprocal(rinv, t4), ["t4"], ["rinv"])
                        for q in range(4):
                            hq = h - 3 + q
                            self.stt(cat[pm][:, 1024 + hq * 128:1024 + (hq + 1) * 128], Ob[:, q * 128:(q + 1) * 128],
                                     rinv[:, q:q + 1], szb[pm][:, hq * 128:(hq + 1) * 128], ALU.mult, ALU.mult,
                                     [("ps", 7), "rinv", ("szb", pm, hq // 4)], [("cat", pm, 2 + hq // 4)])
                for half in range(2):
                    for kc in range(8):
                        c = half * 8 + kc
                        self.tr(psb[0][:, kc * 128:(kc + 1) * 128], cat[pm][:, c * 128:(c + 1) * 128],
                                [("cat", pm, c // 4)], [("ps", 0)])
                    self.cp("act", catT[:, half * 8:half * 8 + 8, :].rearrange("p a b -> p (a b)"), psb[0][:, 0:1024],
                            [("ps", 0)], [("catT", half)])
                for half in range(2):
                    for kc in range(16):
                        self.mm(ps[3 + half][:, :], catT[:, kc, :], WO[:, kc, half * 512:(half + 1) * 512], kc == 0, kc == 15,
                                [("catT", kc // 8), "WO"], [("ps", 3 + half)])
                xo = xs[m % 3]
                for half in range(2):
                    sl_ = slice(half * 512, (half + 1) * 512)
                    self.tt("dve", xo[:, sl_], ps[3 + half][:, :], xs[m % 3][:, sl_], ALU.add,
                            [("ps", 3 + half), ("xs", m % 3)], [("xs", m % 3)])
                self.dma("sp", dst[m * 128:(m + 1) * 128, :], xo, ("st", m % 3), [("xs", m % 3)], [("x1d", m)])

    def phase_scan(self, fwd):
        A, NT = self.A, self.NT
        NM = NT // 2
        a = A.alloc
        ps, psb = self.ps, self.psb
        W = a((8, 6144), BF16)
        wv = self.wio_d.ap().rearrange("(kc p) n -> p kc n", p=128)
        fcol = 2048 if fwd else 4096
        for bi, c0 in enumerate((6144, 0, fcol)):
            bo = (2, 0, 1)[bi]
            for kc in range(8):
                self.dma("pool", W[:, kc, bo * 2048:(bo + 1) * 2048], wv[:, kc, c0:c0 + 2048], ("W", kc), [],
                         [("W", kc)], cast=True)
        gO = self.load_vec(3, "gO")
        xs = [a((1024,), F32) for _ in range(4)]
        hT2 = a((8, 256), BF16)
        itok = [[a((2048,), BF16) for _ in range(2)] for _ in range(2)]
        st32 = a((16, 128), F32)
        stbf = a((16, 128), BF16)
        E = a((512,), F32)
        L1 = a((512,), F32)
        L2 = a((512,), F32)
        bc = a((512,), F32)
        wk = a((512,), F32)
        Eq = a((512,), F32)
        w2 = a((512,), F32)
        dn = [a((8,), F32) for _ in range(2)]
        qt = [a((512,), BF16) for _ in range(2)]
        kt = [a((512,), BF16) for _ in range(2)]
        kend = a((512,), BF16)
        kttok = [a((512,), BF16) for _ in range(2)]
        aT = a((512,), BF16)
        osb = [a((512,), F32) for _ in range(2)]
        obl = [a((512,), F32) for _ in range(2)]
        osq = a((512,), BF16)
        lnr = a((512,), F32)
        lb, lnoml = (self.lbf, self.lnomlf) if fwd else (self.lbb, self.lnomlb)
        mask = self.maskf if fwd else self.maskb
        res = self.resf if fwd else self.resb
        self.sc.add("pool", lambda e: e.memset(st32.rearrange("p a b -> p (a b)"), 0.0), [],
                    [("st32", h) for h in range(16)], dur=1200.0)
        self.sc.add("pool", lambda e: e.memset(stbf.rearrange("p a b -> p (a b)"), 0.0), [],
                    [("stbf", h) for h in range(16)], dur=700.0)
        morder = list(range(NM)) if fwd else list(range(NM - 1, -1, -1))
        c4order = (0, 1, 2, 3) if fwd else (3, 2, 1, 0)

        def load_x(idx):
            M = morder[idx]
            for j in range(2):
                sl = (idx % 2) * 2 + j
                n = 2 * M + j
                self.dma("sp", xs[sl], self.x1_d[n * 128:(n + 1) * 128, :], ("xs", sl), [("x1d", n)], [("xs", sl)])

        def rev(ap):
            return ap if fwd else ap[:, ::-1]

        def prologue(idx):
            if idx + 1 < NM:
                load_x(idx + 1)
            mp = idx % 2
            for j in range(2):
                sl = mp * 2 + j
                self.make_hT(xs[sl], ("xs", sl), gO, "gO", dst=hT2[:, :, j * 128:(j + 1) * 128], dk=("hT2", j))
            for j in range(2):
                for blk in range(4):
                    bank = (3, 5)[blk % 2]
                    pk = ("ps", bank)
                    for kc in range(8):
                        self.mm(ps[bank][:, :], hT2[:, kc, j * 128:(j + 1) * 128],
                                W[:, kc, 4096 + blk * 512:4096 + (blk + 1) * 512], kc == 0, kc == 7,
                                [("hT2", j), ("W", kc)], [pk])
                    cs = slice(blk * 512, (blk + 1) * 512)
                    self.cp("act", itok[mp][j][:, cs], ps[bank][:, :], [pk], [("itok", mp, j, blk)])

        def projf(bank, hh, col0):
            pk = ("ps", bank)
            out = ps[bank][:, hh * 256:(hh + 1) * 256]
            for kc in range(8):
                self.mm(out, W[:, kc, col0:col0 + 128], hT2[:, kc, :], kc == 0, kc == 7,
                        [("hT2", 0), ("hT2", 1), ("W", kc)], [pk])

        def stageA(idx, G):
            p = (idx * 8 + G) % 2
            qb, fb = (1, 2) if p == 0 else (6, 7)
            for hh in range(2):
                projf(qb, hh, (2 * G + hh) * 128)
            for hh in range(2):
                projf(fb, hh, 2048 + (2 * G + hh) * 128)
            Q, Fp = ps[qb], ps[fb]
            self.act(E, Fp[:, :], AF.Exp, [("ps", fb)], ["E"], scale=-1.0)
            self.act(L1, E, AF.Ln, ["E"], ["L1"], bias=1.0)
            for hh in range(2):
                h = 2 * G + hh
                cs = slice(hh * 256, (hh + 1) * 256)
                self.act(L2[:, cs], E[:, cs], AF.Ln, ["E", "const2"], ["L2"], scale=lb[:, h:h + 1], bias=1.0)
            self.tt("pool", L2, L2, L1, ALU.subtract, ["L2", "L1"], ["L2"])
            self.sc.add("dve", (lambda: (lambda e: e.tensor_tensor_scan(rev(bc), rev(res), rev(L2), 0.0, ALU.mult,
                                                                         ALU.add)))(), ["L2", "const"], ["bc"], dur=1250.0)
            bcv = bc.rearrange("p (a b) -> p a b", b=64)
            self.act(dn[p], bcv[:, :, 63] if fwd else bcv[:, :, 0], AF.Exp, ["bc"], [("dn", p)])
            self.act(Eq, Q[:, :], AF.Exp, [("ps", qb)], ["Eq"], scale=-1.0)
            self.act(Eq, Eq, AF.Ln, ["Eq"], ["Eq"], bias=1.0)
            self.tt("dve", wk, Fp[:, :], L1, ALU.add, [("ps", fb), "L1"], ["wk"])
            self.tt("dve", wk, wk, bc, ALU.add, ["wk", "bc"], ["wk"])
            for hh in range(2):
                h = 2 * G + hh
                cs = slice(hh * 256, (hh + 1) * 256)
                self.act(kt[p][:, cs], wk[:, cs], AF.Exp, ["wk", "const2"], [("kt", p)], scale=-1.0,
                         bias=lnoml[:, h:h + 1])
            self.tt("pool", w2, bc, Eq, ALU.subtract, ["bc", "Eq"], ["w2"])
            self.act(w2, w2, AF.Exp, ["w2"], ["w2"])
            self.tt("dve", qt[p], Q[:, :], w2, ALU.mult, [("ps", qb), "w2"], [("qt", p)])
            self.tt("pool", kend.rearrange("p (a b) -> p a b", b=64), kt[p].rearrange("p (a b) -> p a b", b=64),
                    dn[p].unsqueeze(2).to_broadcast([128, 8, 64]), ALU.mult, [("kt", p), ("dn", p)], ["kend"])
            for q4 in range(4):
                cs = slice(q4 * 128, (q4 + 1) * 128)
                self.tr(psb[0][:, cs], kend[:, cs], ["kend"], [("ps", 0)])
            self.cp("act", kttok[p], psb[0][:, 0:512], [("ps", 0)], [("kttok", p)])

        def stageB(idx, G):
            M = morder[idx]
            p = (idx * 8 + G) % 2
            mp = idx % 2
            for q4 in range(4):
                cs = slice(q4 * 128, (q4 + 1) * 128)
                self.mm(ps[3][:, cs], kt[p][:, cs], qt[p][:, cs], True, True, [("kt", p), ("qt", p)], [("ps", 3)])
            self.tt("dve", aT, ps[3][:, :], mask, ALU.mult, [("ps", 3), "const"], ["aT"])
            O = ps[4]
            for q4 in range(4):
                hh, j = q4 // 2, q4 % 2
                h = 2 * G + hh
                cs = slice(q4 * 128, (q4 + 1) * 128)
                self.mm(O[:, cs], itok[mp][j][:, h * 128:(h + 1) * 128], aT[:, cs], q4 == 0, False,
                        [("itok", mp, j, h // 4), "aT"], [("ps", 4)])
            for ci, c4 in enumerate(c4order):
                j, c = c4 // 2, c4 % 2
                for hh in range(2):
                    h = 2 * G + hh
                    cs = slice(hh * 256 + c4 * 64, hh * 256 + c4 * 64 + 64)
                    self.mm(O[:, cs], stbf[:, h, :], qt[p][:, cs], False, ci == 3 and hh == 1,
                            [("stbf", h), ("qt", p)], [("ps", 4)])
                for hh in range(2):
                    h = 2 * G + hh
                    q4 = hh * 2 + j
                    self.mm(ps[5][:, hh * 128:(hh + 1) * 128], kttok[p][c * 64:(c + 1) * 64, q4 * 128:(q4 + 1) * 128],
                            itok[mp][j][c * 64:(c + 1) * 64, h * 128:(h + 1) * 128], True, True,
                            [("kttok", p), ("itok", mp, j, h // 4)], [("ps", 5)])
                for hh in range(2):
                    h = 2 * G + hh
                    dcol = dn[p][:, hh * 4 + c4:hh * 4 + c4 + 1]
                    self.stt(st32[:, h, :], st32[:, h, :], dcol, ps[5][:, hh * 128:(hh + 1) * 128], ALU.mult, ALU.add,
                             [("st32", h), ("dn", p), ("ps", 5)], [("st32", h)])
                self.cp("act", stbf[:, 2 * G:2 * G + 2, :].rearrange("p a b -> p (a b)"),
                        st32[:, 2 * G:2 * G + 2, :].rearrange("p a b -> p (a b)"),
                        [("st32", 2 * G), ("st32", 2 * G + 1)], [("stbf", 2 * G), ("stbf", 2 * G + 1)])
            if not fwd:
                ob_ = osb[G % 2]
                self.cp("act", ob_, O[:, :], [("ps", 4)], [("osb", G % 2)])
                self.dma("sp", self.ob_d[M, :, G * 512:(G + 1) * 512], ob_, ("st", G % 2), [("osb", G % 2)],
                         [("obd", M, G)])
            else:
                l_ = obl[G % 2]
                o_ = osb[G % 2]
                self.dma("sp", l_, self.ob_d[M, :, G * 512:(G + 1) * 512], ("ld", G % 2), [("obd", M, G)],
                         [("obl", G % 2)])
                self.tt("dve", l_, O[:, :], l_, ALU.add, [("ps", 4), ("obl", G % 2)], [("obl", G % 2)])
                self.act(osq, l_, AF.Square, [("obl", G % 2)], ["osq"])
                self.mm(ps[3][:, :], self.ones, osq, True, True, ["osq", "const"], [("ps", 3)])
                self.act(lnr, ps[3][:, :], AF.Ln, [("ps", 3)], ["lnr"], scale=1.0 / 128, bias=EPS)
                self.act(lnr, lnr, AF.Exp, ["lnr"], ["lnr"], scale=-0.5)
                for hh in range(2):
                    h = 2 * G + hh
                    cs = slice(hh * 256, (hh + 1) * 256)
                    self.stt(o_[:, cs], l_[:, cs], self.hg[:, h:h + 1], lnr[:, cs], ALU.mult, ALU.mult,
                             [("obl", G % 2), "const", "lnr"], [("osb", G % 2)])
                self.dma("sp", self.on_d[M, :, G * 512:(G + 1) * 512], o_, ("st", G % 2), [("osb", G % 2)],
                         [("ond", M, G)])

        load_x(0)
        items = [(idx, G) for idx in range(NM) for G in range(8)]
        for k, (idx, G) in enumerate(items):
            if G == 0:
                prologue(idx)
            stageA(idx, G)
            if k >= 1:
                stageB(*items[k - 1])
        stageB(*items[-1])

    def phase3(self):
        A, NT = self.A, self.NT
        NM = NT // 2
        a = A.alloc
        ps, psb = self.ps, self.psb
        W = a((8, 2048), BF16)
        WO = a((16, 1024), BF16)
        wv = self.wio_d.ap().rearrange("(kc p) n -> p kc n", p=128)
        for kc in range(8):
            self.dma("pool", W[:, kc, :], wv[:, kc, 8192:10240], ("W", kc), [], [("W", kc)], cast=True)
        self.dma("pool", WO, self.woo_d.ap().rearrange("(kc p) n -> p kc n", p=128), "WO", [], ["WO"], cast=True)
        gO = self.load_vec(3, "gO")
        gF = self.load_vec(4, "gF")
        xs = [a((1024,), F32) for _ in range(4)]
        hT2 = a((8, 256), BF16)
        onl = [a((4096,), F32) for _ in range(2)]
        tmpE = [a((512,), F32) for _ in range(2)]
        szt = [a((512,), F32) for _ in range(2)]
        yT = a((16, 256), BF16)
        x2 = [a((1024,), F32) for _ in range(2)]

        def load(M):
            mp = M % 2
            for j in range(2):
                n = 2 * M + j
                sl = mp * 2 + j
                self.dma("sp", xs[sl], self.x1_d[n * 128:(n + 1) * 128, :], ("xs", sl), [("x1d", n)], [("xs", sl)])
            for hf in range(2):
                self.dma("sp", onl[mp][:, hf * 2048:(hf + 1) * 2048], self.on_d[M, :, hf * 2048:(hf + 1) * 2048],
                         ("ld", mp, hf), [("ond", M, g) for g in range(4 * hf, 4 * hf + 4)], [("onl", mp, hf)])

        def stA(M, G):
            mp = M % 2
            if G == 0:
                for j in range(2):
                    sl = mp * 2 + j
                    self.make_hT(xs[sl], ("xs", sl), gO, "gO", dst=hT2[:, :, j * 128:(j + 1) * 128], dk=("hT2", j))
            bank = 1 + G % 2
            pk = ("ps", bank)
            for hh in range(2):
                col0 = (2 * G + hh) * 128
                for kc in range(8):
                    self.mm(ps[bank][:, hh * 256:(hh + 1) * 256], W[:, kc, col0:col0 + 128], hT2[:, kc, :], kc == 0,
                            kc == 7, [("hT2", 0), ("hT2", 1), ("W", kc)], [pk])

        def stB(M, G):
            mp = M % 2
            bank = 1 + G % 2
            self.silu(szt[G % 2], ps[bank][:, :], tmpE[G % 2], [("ps", bank)], [("szt", G % 2)], ("tmpE", G % 2))
            self.tt("pool", yT[:, 2 * G:2 * G + 2, :].rearrange("p a b -> p (a b)"), szt[G % 2],
                    onl[mp][:, G * 512:(G + 1) * 512], ALU.mult, [("szt", G % 2), ("onl", mp, G // 4)], [("yT", G)])
            for hh in range(2):
                h = 2 * G + hh
                for j in range(2):
                    for half in range(2):
                        bk = 3 + 2 * j + half
                        self.mm(ps[bk][:, :], yT[:, h, j * 128:(j + 1) * 128], WO[:, h, half * 512:(half + 1) * 512],
                                h == 0, h == 15, [("yT", G), "WO"], [("ps", bk)])
            if G == 7:
                for j in range(2):
                    n = 2 * M + j
                    sl = mp * 2 + j
                    xo = x2[j]
                    for half in range(2):
                        sl_ = slice(half * 512, (half + 1) * 512)
                        bk = 3 + 2 * j + half
                        self.tt("dve", xo[:, sl_], ps[bk][:, :], xs[sl][:, sl_], ALU.add,
                                [("ps", bk), ("xs", sl)], [("x2", j)])
                    self.rms(xo, ("x2", j))
                    self.stt(xo, xo, self.rstd, gF, ALU.mult, ALU.mult, [("x2", j), "rstd", "gF"], [("x2", j)])
                    self.dma("sp", self.out_d[n * 128:(n + 1) * 128, :], xo, ("st", j), [("x2", j)], [("outd", n)])
                if M + 2 < NM:
                    load(M + 2)

        load(0)
        if NM > 1:
            load(1)
        items = [(M, G) for M in range(NM) for G in range(8)]
        for k, it in enumerate(items):
            stA(*it)
            if k >= 1:
                stB(*items[k - 1])
        stB(*items[-1])


def _host_consts():
    c = np.zeros((128, 3200), np.float32)
    c[:, 0:128] = np.eye(128, dtype=np.float32)
    c[:, 128:256] = 1.0
    s = np.arange(128)[:, None]
    t = np.arange(128)[None, :]
    same = (s // 64) == (t // 64)
    mf = (same & (s <= t)).astype(np.float32)
    mb = (same & (s >= t)).astype(np.float32)
    c[:, 256:768] = np.tile(mf, (1, 4))
    c[:, 768:1280] = np.tile(mb, (1, 4))
    col = np.arange(512)
    c[:, 1280:1792] = (col % 64 != 0).astype(np.float32)[None, :]
    c[:, 1792:2304] = (col % 64 != 63).astype(np.float32)[None, :]
    qi = np.arange(128)[:, None]
    kj = np.arange(384)[None, :] - 128
    dist = np.abs(kj - qi).astype(np.float32)
    c[:, 2304:2688] = np.where(dist <= 128, dist, 1.0e9)
    return c


_PROG_CACHE = {}


def _get_prog(NT, upto=3):
    key = (NT, upto)
    if key not in _PROG_CACHE:
        p = Prog(NT, upto)
        p.build()
        _PROG_CACHE[key] = p
    return _PROG_CACHE[key]


def make_in_maps(inputs, ncores, S):
    f = lambda v: np.ascontiguousarray(np.asarray(v, dtype=np.float32))
    x = f(inputs["x"])
    vecs = np.stack([f(inputs["norm_g_even"])[0], f(inputs["gmlp_ln_g"])[0], f(inputs["gmlp_ln_b"])[0],
                     f(inputs["norm_g_odd"])[0], f(inputs["final_norm_g"])], axis=0)
    ws = f(inputs["gmlp_w_s"])[0]
    wsT = np.ascontiguousarray(ws.transpose(2, 0, 1).reshape(128, 512))
    cols = np.zeros((128, 128), np.float32)
    cols[:, 0:4] = f(inputs["gmlp_b_s"])[0].T
    gf = f(inputs["hgrn_gamma_fwd"]).reshape(2, 16, 128)
    gb = f(inputs["hgrn_gamma_bwd"]).reshape(2, 16, 128)
    cols[:, 4:20] = gf[0].T
    cols[:, 20:36] = gf[1].T
    cols[:, 36:52] = gb[0].T
    cols[:, 52:68] = gb[1].T
    cols[:, 68:84] = f(inputs["hgrn_head_norm_g"])[0].reshape(16, 128).T
    cols[0:64, 84] = 1.0
    cols[64:128, 85] = 1.0
    shared = {
        "w_in_even": f(inputs["w_in_even"])[0], "w_out_even": f(inputs["w_out_even"])[0],
        "w_in_odd": f(inputs["w_in_odd"])[0], "w_out_odd": f(inputs["w_out_odd"])[0],
        "vecs": np.ascontiguousarray(vecs), "wsT": wsT, "cols": cols, "sink": f(inputs["attn_sink"]),
        "cst": _host_consts(),
    }
    maps = []
    for c in range(ncores):
        m = dict(shared)
        m["x"] = np.ascontiguousarray(x[c, :S])
        maps.append(m)
    return maps


def kernel(**inputs):
    x = np.asarray(inputs["x"])
    B, S, _ = x.shape
    prog = _get_prog(S // 128)
    maps = make_in_maps(inputs, B, S)
    res = run_bass_kernel_spmd(prog.nc, maps, core_ids=list(range(B)))
    return np.stack([np.asarray(r["out"]).reshape(S, D) for r in res.results], axis=0).astype(np.float32)
```

```python
import os
import numpy as np
import concourse.bass as bass
import concourse.mybir as mybir
from concourse.bass_utils import run_bass_kernel_spmd

F32 = mybir.dt.float32
BF16 = mybir.dt.bfloat16
AF = mybir.ActivationFunctionType
ALU = mybir.AluOpType
AX = mybir.AxisListType

KSTOP = int(os.environ.get('KSTOP', '99'))
D = 1024
EPS = 1e-6
NCORES = 8
SEQ = 8192
ARENA_WORDS = 53000


class _Op:
    __slots__ = ("eng", "fn", "reads", "writes", "dma", "deps", "alld", "sig", "need", "dur", "seg")

    def __init__(self, eng, fn, reads, writes, dma, dur, seg):
        self.eng, self.fn, self.reads, self.writes, self.dma = eng, fn, tuple(reads), tuple(writes), dma
        self.deps = ()
        self.alld = ()
        self.sig = None
        self.need = False
        self.dur = dur
        self.seg = seg


class Sched:
    ENGS = ("pe", "act", "dve", "pool", "sp")
    XLAT = 350.0

    def __init__(self):
        self.ops = []
        self.last_w = {}
        self.readers = {}
        self.seg = 0

    def add(self, eng, fn, r=(), w=(), dma=None, dur=300.0):
        ops = self.ops
        i = len(ops)
        w = tuple(w) + tuple(k for k in r if isinstance(k, tuple) and k[0] == "ps" and k not in w)
        op = _Op(eng, fn, r, w, dma, dur, self.seg)
        deps = set()
        for k in op.reads:
            j = self.last_w.get(k)
            if j is not None:
                deps.add(j)
        for k in op.writes:
            j = self.last_w.get(k)
            if j is not None:
                deps.add(j)
            deps.update(self.readers.get(k, ()))
        deps = set(j for j in deps if ops[j].seg == self.seg)
        op.alld = deps
        pr = set()
        for j in deps:
            oj = ops[j]
            if oj.dma is None and dma is None and oj.eng == eng:
                if eng == "pe":
                    continue
                if not any(k in oj.writes for k in op.reads):
                    continue
            pr.add(j)
        op.deps = pr
        for k in op.reads:
            self.readers.setdefault(k, []).append(i)
        for k in op.writes:
            self.last_w[k] = i
            self.readers[k] = []
        ops.append(op)
        return i

    def barrier(self):
        self.seg += 1

    def _sched_segment(self, idxs):
        import heapq
        ops = self.ops
        nd = {}
        succ = {}
        for i in idxs:
            d = ops[i].alld
            nd[i] = len(d)
            for j in d:
                succ.setdefault(j, []).append(i)
        free = {e: 0.0 for e in self.ENGS}
        fut = {e: [] for e in self.ENGS}
        av = {e: [] for e in self.ENGS}
        rt = {}
        fin = {}
        for i in idxs:
            if nd[i] == 0:
                heapq.heappush(fut[ops[i].eng], (0.0, i))
        order = []
        left = len(idxs)
        while left:
            best = None
            for e in self.ENGS:
                f, a = fut[e], av[e]
                while f and f[0][0] <= free[e]:
                    heapq.heappush(a, heapq.heappop(f)[1])
                if a:
                    cand = (free[e], a[0], e, True)
                elif f:
                    cand = (f[0][0], f[0][1], e, False)
                else:
                    continue
                if best is None or cand[:2] < best[:2]:
                    best = cand
            start, i, e, fa = best
            if fa:
                heapq.heappop(av[e])
            else:
                heapq.heappop(fut[e])
            op = ops[i]
            if op.dma is not None:
                free[e] = start + 120.0
            else:
                free[e] = start + op.dur
            fin[i] = start + op.dur
            order.append(i)
            left -= 1
            for sidx in succ.get(i, ()):
                so = ops[sidx]
                if so.eng == e and op.dma is None and so.dma is None:
                    lat = 60.0 if sidx in so.deps or i in so.deps else 0.0
                else:
                    lat = self.XLAT
                t = fin[i] + lat
                if rt.get(sidx, 0.0) < t:
                    rt[sidx] = t
                nd[sidx] -= 1
                if nd[sidx] == 0:
                    heapq.heappush(fut[so.eng], (rt[sidx], sidx))
        self.makespan.append(max(fin.values()) if fin else 0.0)
        return order

    def schedule(self):
        ops = self.ops
        nseg = self.seg + 1
        segs = [[] for _ in range(nseg)]
        for i, op in enumerate(ops):
            segs[op.seg].append(i)
        self.makespan = []
        order = []
        for sg in segs:
            o = self._sched_segment(sg)
            order.extend(o)
            last = {}
            for i in o:
                op = ops[i]
                if op.dma is not None:
                    last[("d", op.dma)] = i
                elif op.fn is not None:
                    last[("e", op.eng)] = i
            deps = set(last.values())
            for e in self.ENGS:
                order.append(("bar", e, deps))
        self.order = order

    def finalize(self, nc, stack):
        ops = self.ops
        self.schedule()
        for op in ops:
            for j in op.deps:
                ops[j].need = True
        for it in self.order:
            if isinstance(it, tuple):
                for j in it[2]:
                    ops[j].need = True
        sems = {}
        cnt = {}

        def sem_for(key):
            if key not in sems:
                sems[key] = stack.enter_context(nc.semaphore("s%d" % len(sems)))
                cnt[key] = 0
            return sems[key]

        for e in self.ENGS:
            sem_for(("e", e))
        for it in self.order:
            if isinstance(it, tuple):
                continue
            op = ops[it]
            if op.dma is not None:
                k = ("d", op.dma)
                sem_for(k)
                cnt[k] += 16
                op.sig = (k, cnt[k])
            elif op.need and op.fn is not None:
                k = ("e", op.eng)
                cnt[k] += 1
                op.sig = (k, cnt[k])
        self.sems = sems
        return sems

    def emit(self, eng_name, eng):
        ops = self.ops
        sems = self.sems
        known = {}
        for it in self.order:
            if isinstance(it, tuple):
                if it[1] != eng_name:
                    continue
                deps, op = it[2], None
            else:
                op = ops[it]
                if op.eng != eng_name:
                    continue
                deps = op.deps
            need = {}
            for j in deps:
                sg = ops[j].sig
                if sg is None:
                    continue
                if need.get(sg[0], 0) < sg[1]:
                    need[sg[0]] = sg[1]
            for k, v in need.items():
                if known.get(k, 0) < v:
                    eng.wait_ge(sems[k], v)
                    known[k] = v
            if op is None or op.fn is None:
                continue
            ins = op.fn(eng)
            if op.sig is not None:
                ins.then_inc(sems[op.sig[0]], 16 if op.dma is not None else 1)


class Arena:
    def __init__(self, t, words):
        self.t = t
        self.words = words
        self.off = 0

    def alloc(self, shape, dtype):
        n = 1
        for s in shape:
            n *= s
        words = n if dtype == F32 else (n + 1) // 2
        words = (words + 7) // 8 * 8
        assert self.off + words <= self.words, ("arena overflow", self.off, words)
        ap = self.t[:, self.off:self.off + words]
        self.off += words
        if dtype != F32:
            ap = ap.bitcast(dtype)
        ap = ap[:, 0:n]
        if len(shape) == 2:
            ap = ap.rearrange("p (a b) -> p a b", a=shape[0], b=shape[1])
        elif len(shape) == 3:
            ap = ap.rearrange("p (a b c) -> p a b c", a=shape[0], b=shape[1], c=shape[2])
        return ap


class Prog:
    def __init__(self, NT, upto=3):
        self.NT = NT
        self.S = NT * 128
        self.upto = upto
        self.sc = Sched()
        self.nc = bass.Bass("TRN2", target_bir_lowering=False)

    @staticmethod
    def _n(ap):
        n = 1
        for v in ap.shape[1:]:
            n *= v
        return n

    def mm(self, out, lhsT, rhs, start, stop, r, w):
        d = max(self._n(rhs) / 2.4, 104.0)
        self.sc.add("pe", lambda e: e.matmul(out, lhsT, rhs, start=start, stop=stop), r, w, dur=d)

    def tr(self, out, in_, r, w):
        ident = self.ident
        self.sc.add("pe", lambda e: e.transpose(out, in_, ident), tuple(r) + ("const",), w, dur=110.0)

    def act(self, out, in_, func, r, w, bias=None, scale=None, accum=None):
        kw = {}
        if bias is not None:
            kw["bias"] = bias
        if scale is not None:
            kw["scale"] = scale
        if accum is not None:
            kw["accum_out"] = accum
        d = 210.0 + 0.75 * self._n(in_) + (100.0 if accum is not None else 0.0)
        self.sc.add("act", lambda e: e.activation(out, in_, func, **kw), r, w, dur=d)

    def _vd(self, eng, n, k=1.35):
        return (110.0 + 2.25 * n) if eng == "pool" else (70.0 + k * n)

    def ts(self, eng, out, in0, s1, op0, r, w, s2=None, op1=None):
        d = self._vd(eng, self._n(in0))
        if op1 is None:
            self.sc.add(eng, lambda e: e.tensor_scalar(out, in0, s1, None, op0), r, w, dur=d)
        else:
            self.sc.add(eng, lambda e: e.tensor_scalar(out, in0, s1, s2, op0, op1), r, w, dur=d)

    def tt(self, eng, out, in0, in1, op, r, w):
        self.sc.add(eng, lambda e: e.tensor_tensor(out, in0, in1, op), r, w, dur=self._vd(eng, self._n(in0)))

    def stt(self, out, in0, scalar, in1, op0, op1, r, w):
        self.sc.add("dve", lambda e: e.scalar_tensor_tensor(out, in0, scalar, in1, op0, op1), r, w,
                    dur=90.0 + 2.0 * self._n(in0))

    def cp(self, eng, out, in_, r, w):
        if eng == "act":
            self.sc.add("act", lambda e: e.copy(out, in_), r, w, dur=210.0 + 0.75 * self._n(in_))
        else:
            self.sc.add(eng, lambda e: e.tensor_copy(out, in_), r, w, dur=self._vd(eng, self._n(in_), 1.0))

    def dma(self, eng, out, in_, key, r, w, cast=False):
        nbytes = 128 * self._n(out) * 4
        d = 2500.0 + nbytes / 150.0
        if cast:
            self.sc.add(eng, lambda e: e.dma_start(out=out, in_=in_, max_dma_last_dim=4096), r, w, dma=key, dur=d)
        else:
            self.sc.add(eng, lambda e: e.dma_start(out=out, in_=in_), r, w, dma=key, dur=d)

    def silu(self, out, src, tmp, r, w, tk):
        self.act(tmp, src, AF.Exp, r, [tk], scale=-1.0)
        self.act(tmp, tmp, AF.Ln, [tk], [tk], bias=1.0)
        self.act(tmp, tmp, AF.Exp, [tk], [tk], scale=-1.0)
        self.tt("dve", out, src, tmp, ALU.mult, tuple(r) + (tk,), w)

    def rms(self, x_ap, xk, n=1024):
        self.act(self.hb, x_ap, AF.Square, [xk], ["hb", "ss"], accum=self.ss)
        self.act(self.lnv, self.ss, AF.Ln, ["ss"], ["lnv"], scale=1.0 / n, bias=EPS)
        self.act(self.rstd, self.lnv, AF.Exp, ["lnv"], ["rstd"], scale=-0.5)

    def make_hT(self, x_ap, xk, gbc, gk, dst=None, dk="hT"):
        self.rms(x_ap, xk)
        self.stt(self.hb, x_ap, self.rstd, gbc, ALU.mult, ALU.mult, [xk, "rstd", gk], ["hb"])
        Tb = self.psb[0]
        for kc in range(8):
            self.tr(Tb[:, kc * 128:(kc + 1) * 128], self.hb[:, kc * 128:(kc + 1) * 128], ["hb"], [("ps", 0)])
        if dst is None:
            self.cp("act", self.hT.rearrange("p a b -> p (a b)"), Tb[:, 0:1024], [("ps", 0)], ["hT"])
        else:
            self.cp("act", dst, Tb[:, 0:1024].rearrange("p (a b) -> p a b", b=128), [("ps", 0)], [dk])

    def proj_tok(self, bank, ncols, W, col0):
        pk = ("ps", bank)
        out = self.ps[bank][:, 0:ncols]
        for kc in range(8):
            self.mm(out, self.hT[:, kc, :], W[:, kc, col0:col0 + ncols], kc == 0, kc == 7, ["hT", ("W", kc)], [pk])

    def proj_feat(self, bank, hh, W, col0):
        pk = ("ps", bank)
        out = self.ps[bank][:, hh * 128:(hh + 1) * 128]
        for kc in range(8):
            self.mm(out, W[:, kc, col0:col0 + 128], self.hT[:, kc, :], kc == 0, kc == 7, ["hT", ("W", kc)], [pk])

    def build(self):
        nc = self.nc
        NT, S = self.NT, self.S
        dt = nc.dram_tensor
        self.x_d = dt("x", [S, D], F32, kind="ExternalInput")
        self.wie_d = dt("w_in_even", [D, 5632], F32, kind="ExternalInput")
        self.woe_d = dt("w_out_even", [2048, D], F32, kind="ExternalInput")
        self.wio_d = dt("w_in_odd", [D, 10240], F32, kind="ExternalInput")
        self.woo_d = dt("w_out_odd", [2048, D], F32, kind="ExternalInput")
        self.vecs_d = dt("vecs", [5, D], F32, kind="ExternalInput")
        self.wsT_d = dt("wsT", [128, 512], F32, kind="ExternalInput")
        self.cols_d = dt("cols", [128, 128], F32, kind="ExternalInput")
        self.sink_d = dt("sink", [1, 8], F32, kind="ExternalInput")
        self.cst_d = dt("cst", [128, 3200], F32, kind="ExternalInput")
        self.out_d = dt("out", [S, D], F32, kind="ExternalOutput")
        self.x1_d = dt("x1s", [S, D], F32, kind="Internal")
        self.ob_d = dt("obs", [NT // 2, 128, 4096], F32, kind="Internal")
        self.on_d = dt("ons", [NT // 2, 128, 4096], F32, kind="Internal")
        self.qs_d = dt("qss", [NT // 2, 128, 4096], F32, kind="Internal")
        self.it_d = dt("its", [NT, 128, 2048], BF16, kind="Internal")

        from contextlib import ExitStack
        with ExitStack() as st:
            arena_t = st.enter_context(nc.sbuf_tensor("arena", [128, ARENA_WORDS], F32))
            self.ps = [st.enter_context(nc.psum_tensor("ps%d" % b, [128, 512], F32)) for b in range(8)]
            self.psb = [p.bitcast(BF16) for p in self.ps]
            self.A = Arena(arena_t, ARENA_WORDS)
            self.consts()
            base = self.A.off
            self.phase0()
            if self.upto >= 1 and self.upto != 12:
                self.sc.barrier()
                self.A.off = base
                self.phase_scan(fwd=False)
            if self.upto >= 2:
                self.sc.barrier()
                self.A.off = base
                self.phase_scan(fwd=True)
            if self.upto >= 3 and self.upto != 12:
                self.sc.barrier()
                self.A.off = base
                self.phase3()
            self.sc.finalize(nc, st)
            with nc.Block() as block:
                sc = self.sc

                @block.sync
                def _(e):
                    sc.emit("sp", e)

                @block.scalar
                def _(e):
                    sc.emit("act", e)

                @block.vector
                def _(e):
                    sc.emit("dve", e)

                @block.gpsimd
                def _(e):
                    sc.emit("pool", e)

                @block.tensor
                def _(e):
                    sc.emit("pe", e)
        return nc

    def consts(self):
        A = self.A
        a = A.alloc
        self.ident = a((128,), BF16)
        self.ones = a((128,), BF16)
        self.maskf = a((512,), BF16)
        self.maskb = a((512,), BF16)
        self.resf = a((512,), F32)
        self.resb = a((512,), F32)
        self.distp = a((384,), F32)
        self.cols = a((128,), F32)
        self.sinkbc = a((8,), F32)
        self.nsink = a((8,), F32)
        self.lbf = a((16,), F32)
        self.omlf = a((16,), F32)
        self.lbb = a((16,), F32)
        self.omlb = a((16,), F32)
        self.lnomlf = a((16,), F32)
        self.lnomlb = a((16,), F32)
        self.ss = a((1,), F32)
        self.lnv = a((1,), F32)
        self.rstd = a((1,), F32)
        self.hb = a((1024,), BF16)
        self.hT = a((8, 128), BF16)
        c = self.cst_d
        p = "pool"
        k = "const"
        self.dma(p, self.ident, c[:, 0:128], "c0", [], [k], cast=True)
        self.dma(p, self.ones, c[:, 128:256], "c1", [], [k], cast=True)
        self.dma(p, self.maskf, c[:, 256:768], "c2", [], [k], cast=True)
        self.dma(p, self.maskb, c[:, 768:1280], "c3", [], [k], cast=True)
        self.dma("sp", self.resf, c[:, 1280:1792], "c4", [], [k])
        self.dma("sp", self.resb, c[:, 1792:2304], "c5", [], [k])
        self.dma("sp", self.distp, c[:, 2304:2688], "c6", [], [k])
        self.dma("sp", self.cols, self.cols_d[:, :], "c7", [], [k])
        self.dma("sp", self.sinkbc, self.sink_d[0:1, :].partition_broadcast(128), "c8", [], [k])
        self.ts("dve", self.nsink, self.sinkbc, -1.0, ALU.mult, [k], ["const2"])
        for (lb, oml, lno, o) in ((self.lbf, self.omlf, self.lnomlf, 4), (self.lbb, self.omlb, self.lnomlb, 36)):
            self.tt("dve", lb, self.cols[:, o:o + 16], self.cols[:, o + 16:o + 32], ALU.subtract, [k], ["const2"])
            self.act(lb, lb, AF.Exp, ["const2"], ["const2"])
            self.ts("dve", lb, lb, 1.0, ALU.add, ["const2"], ["const2"])
            self.sc.add("dve", (lambda lb: (lambda e: e.reciprocal(lb, lb)))(lb), ["const2"], ["const2"])
            self.ts("dve", oml, lb, -1.0, ALU.mult, ["const2"], ["const2"], 1.0, ALU.add)
            self.act(lno, oml, AF.Ln, ["const2"], ["const2"])
        self.bs = self.cols[:, 0:4]
        self.hg = self.cols[:, 68:84]
        self.cmask = self.cols[:, 84:86]

    def load_vec(self, i, key):
        t = self.A.alloc((1024,), F32)
        self.dma("sp", t, self.vecs_d[i:i + 1, :].partition_broadcast(128), key, [], [key])
        return t

    def phase0(self):
        A, NT = self.A, self.NT
        a = A.alloc
        ps, psb = self.ps, self.psb
        W = a((8, 5632), BF16)
        WO = a((16, 1024), BF16)
        gE = self.load_vec(0, "gE")
        lng = self.load_vec(1, "lng")
        lnb = self.load_vec(2, "lnb")
        wsT = a((4, 128), BF16)
        self.dma("pool", wsT.rearrange("p a b -> p (a b)"), self.wsT_d[:, :], "wsT", [], ["wsT"], cast=True)
        wv = self.wie_d.ap().rearrange("(kc p) n -> p kc n", p=128)
        for kc in range(8):
            self.dma("pool", W[:, kc, :], wv[:, kc, :], ("W", kc), [], [("W", kc)], cast=True)
        self.dma("pool", WO, self.woe_d.ap().rearrange("(kc p) n -> p kc n", p=128), "WO", [], ["WO"], cast=True)
        xs = [a((1024,), F32) for _ in range(3)]
        qT = [a((8, 128), BF16) for _ in range(2)]
        kT = [a((2, 128), BF16) for _ in range(3)]
        vb = [a((256,), BF16) for _ in range(3)]
        szb = [a((1024,), BF16) for _ in range(2)]
        cat = [a((2048,), BF16) for _ in range(2)]
        catT = a((16, 128), BF16)
        vh = a((1024,), F32)
        vn = a((1024,), BF16)
        mp = vh
        sza = a((1024,), F32)
        tmpE = [a((512,), F32) for _ in range(2)]
        stats = a((2, 6), F32)
        mv = a((2,), F32)
        lnv2 = a((1,), F32)
        rstd2 = a((1,), F32)
        ssb = [a((384,), F32) for _ in range(2)]
        pp = [a((384,), BF16) for _ in range(2)]
        pT = [a((384,), BF16) for _ in range(2)]
        nrm = a((8,), F32)
        negm = a((8,), F32)
        rsum = a((8,), F32)
        t4 = a((4,), F32)
        rinv = a((4,), F32)
        dst = self.x1_d if self.upto >= 1 else self.out_d

        def load_x(n):
            sl = n % 3
            self.dma("sp", xs[sl], self.x_d[n * 128:(n + 1) * 128, :], ("xs", sl), [], [("xs", sl)])

        load_x(0)
        for n in range(NT + 1):
            if n + 1 < NT:
                load_x(n + 1)
            if n < NT:
                sl, par, s3 = n % 3, n % 2, n % 3
                xk = ("xs", sl)
                self.make_hT(xs[sl], xk, gE, "gE")
                self.proj_tok(1, 512, W, 1024)
                self.proj_tok(2, 512, W, 1536)
                for h in range(8):
                    self.proj_feat(3 + h // 4, h % 4, W, 3072 + h * 128)
                for b in range(2):
                    self.sc.add("dve", (lambda o, i: (lambda e: e.bn_stats(o, i)))(stats[:, b, :], ps[1 + b][:, :]),
                                [("ps", 1 + b)], ["stats"], dur=700.0)
                self.sc.add("dve", lambda e: e.bn_aggr(mv, stats.rearrange("p a b -> p (a b)")), ["stats"], ["mv"])
                self.act(lnv2, mv[:, 1:2], AF.Ln, ["mv"], ["lnv2"], bias=EPS)
                self.act(rstd2, lnv2, AF.Exp, ["lnv2"], ["rstd2"], scale=-0.5)
                for b in range(2):
                    self.ts("dve", vh[:, b * 512:(b + 1) * 512], ps[1 + b][:, :], mv[:, 0:1], ALU.subtract,
                            [("ps", 1 + b), "mv", "rstd2"], [("vm", b)], rstd2, ALU.mult)
                for b in range(2):
                    sl_ = slice(b * 512, (b + 1) * 512)
                    self.tt("pool", vh[:, sl_], vh[:, sl_], lng[:, sl_], ALU.mult, [("vm", b), "lng"], [("vm", b)])
                    self.tt("pool", vn[:, sl_], vh[:, sl_], lnb[:, sl_], ALU.add, [("vm", b), "lnb"], [("vn", b)])
                for g in range(2):
                    self.act(qT[par][:, 4 * g:4 * g + 4, :].rearrange("p a b -> p (a b)"), ps[3 + g][:, :], AF.Identity,
                             [("ps", 3 + g)], [("qT", par)], scale=128.0 ** -0.5)
                for hk in range(2):
                    self.proj_feat(1, hk, W, 4096 + hk * 128)
                self.proj_tok(2, 256, W, 4352)
                self.cp("dve", kT[s3].rearrange("p a b -> p (a b)"), ps[1][:, 0:256], [("ps", 1)], [("kT", s3)])
                self.cp("dve", vb[s3], ps[2][:, 0:256], [("ps", 2)], [("vb", s3)])
                self.proj_tok(3, 512, W, 2048)
                self.proj_tok(4, 512, W, 2560)
                for b in range(2):
                    self.silu(sza[:, b * 512:(b + 1) * 512], ps[3 + b][:, :], tmpE[b], [("ps", 3 + b)], [("sza", b)],
                              ("tmpE", b))
                for g in range(4):
                    self.mm(ps[1 + g // 2][:, (g % 2) * 256:(g % 2) * 256 + 256], wsT[:, g, :],
                            vn[:, g * 256:(g + 1) * 256], True, True, ["wsT", ("vn", g // 2)], [("ps", 1 + g // 2)])
                for g in range(4):
                    self.act(mp[:, g * 256:(g + 1) * 256], ps[1 + g // 2][:, (g % 2) * 256:(g % 2) * 256 + 256],
                             AF.Identity, [("ps", 1 + g // 2), "const"], [("vm", g // 2)], bias=self.bs[:, g:g + 1])
                self.proj_tok(3, 512, W, 0)
                self.proj_tok(4, 512, W, 512)
                for b in range(2):
                    sl_ = slice(b * 512, (b + 1) * 512)
                    self.tt("dve", mp[:, sl_], ps[3 + b][:, :], mp[:, sl_], ALU.mult, [("ps", 3 + b), ("vm", b)],
                            [("vm", b)])
                    self.tt("pool", cat[par][:, sl_], mp[:, sl_], sza[:, sl_], ALU.mult, [("vm", b), ("sza", b)],
                            [("cat", par, b)])
                self.proj_tok(1, 512, W, 4608)
                self.proj_tok(2, 512, W, 5120)
                for b in range(2):
                    self.silu(szb[par][:, b * 512:(b + 1) * 512], ps[1 + b][:, :], tmpE[b], [("ps", 1 + b)],
                              [("szb", par, b)], ("tmpE", b))
            if n >= 1:
                m = n - 1
                pm = m % 2
                js = [j for j in (-1, 0, 1) if 0 <= m + j < NT]
                c0, c1 = (js[0] + 1) * 128, (js[-1] + 2) * 128
                Sb, PTb, Ob = ps[5], psb[6], ps[7]
                for h in range(8):
                    kvh, hh, hs = h // 4, h % 4, h % 2
                    for j in js:
                        self.mm(Sb[:, (j + 1) * 128:(j + 2) * 128], qT[pm][:, h, :], kT[(m + j) % 3][:, kvh, :], True, True,
                                [("qT", pm), ("kT", (m + j) % 3)], [("ps", 5)])
                    self.stt(ssb[hs][:, c0:c1], self.distp[:, c0:c1], -(2.0 ** -(h + 1)), Sb[:, c0:c1], ALU.mult, ALU.add,
                             ["const", ("ps", 5)], [("ssb", hs)])
                    self.sc.add("dve", (lambda o, i: (lambda e: e.tensor_reduce(o, i, AX.X, ALU.max, negate=True)))(
                        nrm[:, h:h + 1], ssb[hs][:, c0:c1]), [("ssb", hs)], [("nrm", h)], dur=550.0)
                    self.tt("dve", negm[:, h:h + 1], nrm[:, h:h + 1], self.nsink[:, h:h + 1], ALU.min,
                            [("nrm", h), "const2"], [("negm", h)])
                    self.act(pp[hs][:, c0:c1], ssb[hs][:, c0:c1], AF.Exp, [("ssb", hs), ("negm", h)],
                             [("pp", hs), ("rsum", h)], bias=negm[:, h:h + 1], accum=rsum[:, h:h + 1])
                    for j in js:
                        cs = slice((j + 1) * 128, (j + 2) * 128)
                        self.tr(PTb[:, cs], pp[hs][:, cs], [("pp", hs)], [("ps", 6)])
                    self.cp("act", pT[hs][:, c0:c1], PTb[:, c0:c1], [("ps", 6)], [("pT", hs)])
                    for j in js:
                        cs = slice((j + 1) * 128, (j + 2) * 128)
                        self.mm(Ob[:, hh * 128:(hh + 1) * 128], pT[hs][:, cs], vb[(m + j) % 3][:, kvh * 128:(kvh + 1) * 128],
                                j == js[0], j == js[-1], [("pT", hs), ("vb", (m + j) % 3)], [("ps", 7)])
                    if hh == 3:
                        g4 = slice(h - 3, h + 1)
                        self.tt("dve", t4, self.sinkbc[:, g4], negm[:, g4], ALU.add,
                                ["const"] + [("negm", q) for q in range(h - 3, h + 1)], ["t4"])
                        self.act(t4, t4, AF.Exp, ["t4"], ["t4"])
                        self.tt("dve", t4, t4, rsum[:, g4], ALU.add, ["t4"] + [("rsum", q) for q in range(h - 3, h + 1)],
                                ["t4"])
                        self.sc.add("dve", lambda e: e.reciprocal(rinv, t4), ["t4"], ["rinv"])
                        for q in range(4):
                            hq = h - 3 + q
                            self.stt(cat[pm][:, 1024 + hq * 128:1024 + (hq + 1) * 128], Ob[:, q * 128:(q + 1) * 128],
                                     rinv[:, q:q + 1], szb[pm][:, hq * 128:(hq + 1) * 128], ALU.mult, ALU.mult,
                                     [("ps", 7), "rinv", ("szb", pm, hq // 4)], [("cat", pm, 2 + hq // 4)])
                for half in range(2):
                    for kc in range(8):
                        c = half * 8 + kc
                        self.tr(psb[0][:, kc * 128:(kc + 1) * 128], cat[pm][:, c * 128:(c + 1) * 128],
                                [("cat", pm, c // 4)], [("ps", 0)])
                    self.cp("act", catT[:, half * 8:half * 8 + 8, :].rearrange("p a b -> p (a b)"), psb[0][:, 0:1024],
                            [("ps", 0)], [("catT", half)])
                for half in range(2):
                    for kc in range(16):
                        self.mm(ps[3 + half][:, :], catT[:, kc, :], WO[:, kc, half * 512:(half + 1) * 512], kc == 0, kc == 15,
                                [("catT", kc // 8), "WO"], [("ps", 3 + half)])
                xo = xs[m % 3]
                for half in range(2):
                    sl_ = slice(half * 512, (half + 1) * 512)
                    self.tt("dve", xo[:, sl_], ps[3 + half][:, :], xs[m % 3][:, sl_], ALU.add,
                            [("ps", 3 + half), ("xs", m % 3)], [("xs", m % 3)])
                self.dma("sp", dst[m * 128:(m + 1) * 128, :], xo, ("st", m % 3), [("xs", m % 3)], [("x1d", m)])

    def phase_scan(self, fwd):
        A, NT = self.A, self.NT
        NM = NT // 2
        a = A.alloc
        ps, psb = self.ps, self.psb
        W = a((8, 6144), BF16)
        wv = self.wio_d.ap().rearrange("(kc p) n -> p kc n", p=128)
        fcol = 2048 if fwd else 4096
        for bi, c0 in enumerate((6144, 0, fcol)):
            bo = (2, 0, 1)[bi]
            if fwd and bo != 1:
                continue
            for kc in range(8):
                self.dma("pool", W[:, kc, bo * 2048:(bo + 1) * 2048], wv[:, kc, c0:c0 + 2048], ("W", kc), [],
                         [("W", kc)], cast=True)
        gO = self.load_vec(3, "gO")
        xs = [a((1024,), F32) for _ in range(4)]
        hT2 = a((8, 256), BF16)
        itok = [[a((2048,), BF16) for _ in range(2)] for _ in range(2)]
        st32 = a((16, 128), F32)
        stbf = a((16, 128), BF16)
        E = a((512,), F32)
        L1 = a((512,), F32)
        L2 = a((512,), F32)
        bc = a((512,), F32)
        wk = a((512,), F32)
        Eq = a((512,), F32)
        w2 = a((512,), F32)
        sg = a((512,), F32)
        qsb = [a((512,), F32) for _ in range(2)]
        dn = [a((8,), F32) for _ in range(2)]
        qt = [a((512,), BF16) for _ in range(2)]
        kt = [a((512,), BF16) for _ in range(2)]
        kend = a((512,), BF16)
        kttok = [a((512,), BF16) for _ in range(2)]
        aT = a((512,), BF16)
        osb = [a((512,), F32) for _ in range(2)]
        obl = [a((512,), F32) for _ in range(2)]
        osq = a((512,), BF16)
        lnr = a((512,), F32)
        lb, lnoml = (self.lbf, self.lnomlf) if fwd else (self.lbb, self.lnomlb)
        mask = self.maskf if fwd else self.maskb
        res = self.resf if fwd else self.resb
        self.sc.add("pool", lambda e: e.memset(st32.rearrange("p a b -> p (a b)"), 0.0), [],
                    [("st32", h) for h in range(16)], dur=1200.0)
        self.sc.add("pool", lambda e: e.memset(stbf.rearrange("p a b -> p (a b)"), 0.0), [],
                    [("stbf", h) for h in range(16)], dur=700.0)
        morder = list(range(NM)) if fwd else list(range(NM - 1, -1, -1))
        c4order = (0, 1, 2, 3) if fwd else (3, 2, 1, 0)

        def load_x(idx):
            M = morder[idx]
            for j in range(2):
                sl = (idx % 2) * 2 + j
                n = 2 * M + j
                self.dma("sp", xs[sl], self.x1_d[n * 128:(n + 1) * 128, :], ("xs", sl), [("x1d", n)], [("xs", sl)])

        def rev(ap):
            return ap if fwd else ap[:, ::-1]

        def prologue(idx):
            if idx + 1 < NM:
                load_x(idx + 1)
            mp = idx % 2
            for j in range(2):
                sl = mp * 2 + j
                self.make_hT(xs[sl], ("xs", sl), gO, "gO", dst=hT2[:, :, j * 128:(j + 1) * 128], dk=("hT2", j))
            for j in range(2):
                n = 2 * morder[idx] + j
                ik = [("itok", mp, j, blk) for blk in range(4)]
                if fwd:
                    self.dma("sp", itok[mp][j], self.it_d[n, :, :], ("ldi", mp, j), [("itd", n)], ik)
                    continue
                for blk in range(4):
                    bank = (3, 5)[blk % 2]
                    pk = ("ps", bank)
                    for kc in range(8):
                        self.mm(ps[bank][:, :], hT2[:, kc, j * 128:(j + 1) * 128],
                                W[:, kc, 4096 + blk * 512:4096 + (blk + 1) * 512], kc == 0, kc == 7,
                                [("hT2", j), ("W", kc)], [pk])
                    cs = slice(blk * 512, (blk + 1) * 512)
                    self.cp("act", itok[mp][j][:, cs], ps[bank][:, :], [pk], [("itok", mp, j, blk)])
                self.dma("sp", self.it_d[n, :, :], itok[mp][j], ("sti", mp, j), ik, [("itd", n)])

        def projf(bank, hh, col0):
            pk = ("ps", bank)
            out = ps[bank][:, hh * 256:(hh + 1) * 256]
            for kc in range(8):
                self.mm(out, W[:, kc, col0:col0 + 128], hT2[:, kc, :], kc == 0, kc == 7,
                        [("hT2", 0), ("hT2", 1), ("W", kc)], [pk])

        def stageA(idx, G):
            p = (idx * 8 + G) % 2
            qb, fb = (1, 2) if p == 0 else (6, 7)
            M = morder[idx]
            if not fwd:
                for hh in range(2):
                    projf(qb, hh, (2 * G + hh) * 128)
            for hh in range(2):
                projf(fb, hh, 2048 + (2 * G + hh) * 128)
            Q, Fp = ps[qb], ps[fb]
            self.act(E, Fp[:, :], AF.Exp, [("ps", fb)], ["E"], scale=-1.0)
            self.act(L1, E, AF.Ln, ["E"], ["L1"], bias=1.0)
            for hh in range(2):
                h = 2 * G + hh
                cs = slice(hh * 256, (hh + 1) * 256)
                self.act(L2[:, cs], E[:, cs], AF.Ln, ["E", "const2"], ["L2"], scale=lb[:, h:h + 1], bias=1.0)
            self.tt("pool", L2, L2, L1, ALU.subtract, ["L2", "L1"], ["L2"])
            self.sc.add("dve", (lambda: (lambda e: e.tensor_tensor_scan(rev(bc), rev(res), rev(L2), 0.0, ALU.mult,
                                                                         ALU.add)))(), ["L2", "const"], ["bc"], dur=1250.0)
            bcv = bc.rearrange("p (a b) -> p a b", b=64)
            self.act(dn[p], bcv[:, :, 63] if fwd else bcv[:, :, 0], AF.Exp, ["bc"], [("dn", p)])
            if not fwd:
                self.act(Eq, Q[:, :], AF.Exp, [("ps", qb)], ["Eq"], scale=-1.0)
                self.act(Eq, Eq, AF.Ln, ["Eq"], ["Eq"], bias=1.0)
                self.act(sg, Eq, AF.Exp, ["Eq"], ["sg"], scale=-1.0)
                self.tt("dve", qsb[p], Q[:, :], sg, ALU.mult, [("ps", qb), "sg"], [("qsb", p)])
                self.dma("sp", self.qs_d[M, :, G * 512:(G + 1) * 512], qsb[p], ("stq", p), [("qsb", p)],
                         [("qsd", M, G)])
            else:
                self.dma("sp", qsb[p], self.qs_d[M, :, G * 512:(G + 1) * 512], ("ldq", p), [("qsd", M, G)],
                         [("qsb", p)])
            self.tt("dve", wk, Fp[:, :], L1, ALU.add, [("ps", fb), "L1"], ["wk"])
            self.tt("dve", wk, wk, bc, ALU.add, ["wk", "bc"], ["wk"])
            for hh in range(2):
                h = 2 * G + hh
                cs = slice(hh * 256, (hh + 1) * 256)
                self.act(kt[p][:, cs], wk[:, cs], AF.Exp, ["wk", "const2"], [("kt", p)], scale=-1.0,
                         bias=lnoml[:, h:h + 1])
            self.act(w2, bc, AF.Exp, ["bc"], ["w2"])
            self.tt("pool", qt[p], qsb[p], w2, ALU.mult, [("qsb", p), "w2"], [("qt", p)])
            self.tt("pool", kend.rearrange("p (a b) -> p a b", b=64), kt[p].rearrange("p (a b) -> p a b", b=64),
                    dn[p].unsqueeze(2).to_broadcast([128, 8, 64]), ALU.mult, [("kt", p), ("dn", p)], ["kend"])
            for q4 in range(4):
                cs = slice(q4 * 128, (q4 + 1) * 128)
                self.tr(psb[0][:, cs], kend[:, cs], ["kend"], [("ps", 0)])
            self.cp("act", kttok[p], psb[0][:, 0:512], [("ps", 0)], [("kttok", p)])

        def stageB(idx, G):
            M = morder[idx]
            p = (idx * 8 + G) % 2
            mp = idx % 2
            for q4 in range(4):
                cs = slice(q4 * 128, (q4 + 1) * 128)
                self.mm(ps[3][:, cs], kt[p][:, cs], qt[p][:, cs], True, True, [("kt", p), ("qt", p)], [("ps", 3)])
            self.tt("dve", aT, ps[3][:, :], mask, ALU.mult, [("ps", 3), "const"], ["aT"])
            O = ps[4]
            for q4 in range(4):
                hh, j = q4 // 2, q4 % 2
                h = 2 * G + hh
                cs = slice(q4 * 128, (q4 + 1) * 128)
                self.mm(O[:, cs], itok[mp][j][:, h * 128:(h + 1) * 128], aT[:, cs], q4 == 0, False,
                        [("itok", mp, j, h // 4), "aT"], [("ps", 4)])
            for ci, c4 in enumerate(c4order):
                j, c = c4 // 2, c4 % 2
                for hh in range(2):
                    h = 2 * G + hh
                    cs = slice(hh * 256 + c4 * 64, hh * 256 + c4 * 64 + 64)
                    self.mm(O[:, cs], stbf[:, h, :], qt[p][:, cs], False, ci == 3 and hh == 1,
                            [("stbf", h), ("qt", p)], [("ps", 4)])
                for hh in range(2):
                    h = 2 * G + hh
                    q4 = hh * 2 + j
                    self.mm(ps[5][:, hh * 128:(hh + 1) * 128], kttok[p][c * 64:(c + 1) * 64, q4 * 128:(q4 + 1) * 128],
                            itok[mp][j][c * 64:(c + 1) * 64, h * 128:(h + 1) * 128], True, True,
                            [("kttok", p), ("itok", mp, j, h // 4)], [("ps", 5)])
                for hh in range(2):
                    h = 2 * G + hh
                    dcol = dn[p][:, hh * 4 + c4:hh * 4 + c4 + 1]
                    self.stt(st32[:, h, :], st32[:, h, :], dcol, ps[5][:, hh * 128:(hh + 1) * 128], ALU.mult, ALU.add,
                             [("st32", h), ("dn", p), ("ps", 5)], [("st32", h)])
                self.cp("act", stbf[:, 2 * G:2 * G + 2, :].rearrange("p a b -> p (a b)"),
                        st32[:, 2 * G:2 * G + 2, :].rearrange("p a b -> p (a b)"),
                        [("st32", 2 * G), ("st32", 2 * G + 1)], [("stbf", 2 * G), ("stbf", 2 * G + 1)])
            if not fwd:
                ob_ = osb[G % 2]
                self.cp("act", ob_, O[:, :], [("ps", 4)], [("osb", G % 2)])
                self.dma("sp", self.ob_d[M, :, G * 512:(G + 1) * 512], ob_, ("st", G % 2), [("osb", G % 2)],
                         [("obd", M, G)])
            else:
                l_ = obl[G % 2]
                o_ = osb[G % 2]
                self.dma("sp", l_, self.ob_d[M, :, G * 512:(G + 1) * 512], ("ld", G % 2), [("obd", M, G)],
                         [("obl", G % 2)])
                self.tt("dve", l_, O[:, :], l_, ALU.add, [("ps", 4), ("obl", G % 2)], [("obl", G % 2)])
                self.act(osq, l_, AF.Square, [("obl", G % 2)], ["osq"])
                self.mm(ps[3][:, :], self.ones, osq, True, True, ["osq", "const"], [("ps", 3)])
                self.act(lnr, ps[3][:, :], AF.Ln, [("ps", 3)], ["lnr"], scale=1.0 / 128, bias=EPS)
                self.act(lnr, lnr, AF.Exp, ["lnr"], ["lnr"], scale=-0.5)
                for hh in range(2):
                    h = 2 * G + hh
                    cs = slice(hh * 256, (hh + 1) * 256)
                    self.stt(o_[:, cs], l_[:, cs], self.hg[:, h:h + 1], lnr[:, cs], ALU.mult, ALU.mult,
                             [("obl", G % 2), "const", "lnr"], [("osb", G % 2)])
                self.dma("sp", self.on_d[M, :, G * 512:(G + 1) * 512], o_, ("st", G % 2), [("osb", G % 2)],
                         [("ond", M, G)])

        load_x(0)
        items = [(idx, G) for idx in range(NM) for G in range(8)]
        for k, (idx, G) in enumerate(items):
            if G == 0:
                prologue(idx)
            stageA(idx, G)
            if k >= 1:
                stageB(*items[k - 1])
        stageB(*items[-1])

    def phase3(self):
        A, NT = self.A, self.NT
        NM = NT // 2
        a = A.alloc
        ps, psb = self.ps, self.psb
        W = a((8, 2048), BF16)
        WO = a((16, 1024), BF16)
        wv = self.wio_d.ap().rearrange("(kc p) n -> p kc n", p=128)
        for kc in range(8):
            self.dma("pool", W[:, kc, :], wv[:, kc, 8192:10240], ("W", kc), [], [("W", kc)], cast=True)
        self.dma("pool", WO, self.woo_d.ap().rearrange("(kc p) n -> p kc n", p=128), "WO", [], ["WO"], cast=True)
        gO = self.load_vec(3, "gO")
        gF = self.load_vec(4, "gF")
        xs = [a((1024,), F32) for _ in range(4)]
        hT2 = a((8, 256), BF16)
        onl = [a((4096,), F32) for _ in range(2)]
        tmpE = [a((512,), F32) for _ in range(2)]
        szt = [a((512,), F32) for _ in range(2)]
        yT = a((16, 256), BF16)
        x2 = [a((1024,), F32) for _ in range(2)]

        def load(M):
            mp = M % 2
            for j in range(2):
                n = 2 * M + j
                sl = mp * 2 + j
                self.dma("sp", xs[sl], self.x1_d[n * 128:(n + 1) * 128, :], ("xs", sl), [("x1d", n)], [("xs", sl)])
            for hf in range(2):
                self.dma("sp", onl[mp][:, hf * 2048:(hf + 1) * 2048], self.on_d[M, :, hf * 2048:(hf + 1) * 2048],
                         ("ld", mp, hf), [("ond", M, g) for g in range(4 * hf, 4 * hf + 4)], [("onl", mp, hf)])

        def stA(M, G):
            mp = M % 2
            if G == 0:
                for j in range(2):
                    sl = mp * 2 + j
                    self.make_hT(xs[sl], ("xs", sl), gO, "gO", dst=hT2[:, :, j * 128:(j + 1) * 128], dk=("hT2", j))
            bank = 1 + G % 2
            pk = ("ps", bank)
            for hh in range(2):
                col0 = (2 * G + hh) * 128
                for kc in range(8):
                    self.mm(ps[bank][:, hh * 256:(hh + 1) * 256], W[:, kc, col0:col0 + 128], hT2[:, kc, :], kc == 0,
                            kc == 7, [("hT2", 0), ("hT2", 1), ("W", kc)], [pk])

        def stB(M, G):
            mp = M % 2
            bank = 1 + G % 2
            self.silu(szt[G % 2], ps[bank][:, :], tmpE[G % 2], [("ps", bank)], [("szt", G % 2)], ("tmpE", G % 2))
            self.tt("pool", yT[:, 2 * G:2 * G + 2, :].rearrange("p a b -> p (a b)"), szt[G % 2],
                    onl[mp][:, G * 512:(G + 1) * 512], ALU.mult, [("szt", G % 2), ("onl", mp, G // 4)], [("yT", G)])
            for hh in range(2):
                h = 2 * G + hh
                for j in range(2):
                    for half in range(2):
                        bk = 3 + 2 * j + half
                        self.mm(ps[bk][:, :], yT[:, h, j * 128:(j + 1) * 128], WO[:, h, half * 512:(half + 1) * 512],
                                h == 0, h == 15, [("yT", G), "WO"], [("ps", bk)])
            if G == 7:
                for j in range(2):
                    n = 2 * M + j
                    sl = mp * 2 + j
                    xo = x2[j]
                    for half in range(2):
                        sl_ = slice(half * 512, (half + 1) * 512)
                        bk = 3 + 2 * j + half
                        self.tt("dve", xo[:, sl_], ps[bk][:, :], xs[sl][:, sl_], ALU.add,
                                [("ps", bk), ("xs", sl)], [("x2", j)])
                    self.rms(xo, ("x2", j))
                    self.stt(xo, xo, self.rstd, gF, ALU.mult, ALU.mult, [("x2", j), "rstd", "gF"], [("x2", j)])
                    self.dma("sp", self.out_d[n * 128:(n + 1) * 128, :], xo, ("st", j), [("x2", j)], [("outd", n)])
                if M + 2 < NM:
                    load(M + 2)

        load(0)
        if NM > 1:
            load(1)
        items = [(M, G) for M in range(NM) for G in range(8)]
        for k, it in enumerate(items):
            stA(*it)
            if k >= 1:
                stB(*items[k - 1])
        stB(*items[-1])


def _host_consts():
    c = np.zeros((128, 3200), np.float32)
    c[:, 0:128] = np.eye(128, dtype=np.float32)
    c[:, 128:256] = 1.0
    s = np.arange(128)[:, None]
    t = np.arange(128)[None, :]
    same = (s // 64) == (t // 64)
    mf = (same & (s <= t)).astype(np.float32)
    mb = (same & (s >= t)).astype(np.float32)
    c[:, 256:768] = np.tile(mf, (1, 4))
    c[:, 768:1280] = np.tile(mb, (1, 4))
    col = np.arange(512)
    c[:, 1280:1792] = (col % 64 != 0).astype(np.float32)[None, :]
    c[:, 1792:2304] = (col % 64 != 63).astype(np.float32)[None, :]
    qi = np.arange(128)[:, None]
    kj = np.arange(384)[None, :] - 128
    dist = np.abs(kj - qi).astype(np.float32)
    c[:, 2304:2688] = np.where(dist <= 128, dist, 1.0e9)
    return c


_PROG_CACHE = {}


def _get_prog(NT, upto=3):
    key = (NT, upto)
    if key not in _PROG_CACHE:
        p = Prog(NT, upto)
        p.build()
        _PROG_CACHE[key] = p
    return _PROG_CACHE[key]


def make_in_maps(inputs, ncores, S):
    f = lambda v: np.ascontiguousarray(np.asarray(v, dtype=np.float32))
    x = f(inputs["x"])
    vecs = np.stack([f(inputs["norm_g_even"])[0], f(inputs["gmlp_ln_g"])[0], f(inputs["gmlp_ln_b"])[0],
                     f(inputs["norm_g_odd"])[0], f(inputs["final_norm_g"])], axis=0)
    ws = f(inputs["gmlp_w_s"])[0]
    wsT = np.ascontiguousarray(ws.transpose(2, 0, 1).reshape(128, 512))
    cols = np.zeros((128, 128), np.float32)
    cols[:, 0:4] = f(inputs["gmlp_b_s"])[0].T
    gf = f(inputs["hgrn_gamma_fwd"]).reshape(2, 16, 128)
    gb = f(inputs["hgrn_gamma_bwd"]).reshape(2, 16, 128)
    cols[:, 4:20] = gf[0].T
    cols[:, 20:36] = gf[1].T
    cols[:, 36:52] = gb[0].T
    cols[:, 52:68] = gb[1].T
    cols[:, 68:84] = f(inputs["hgrn_head_norm_g"])[0].reshape(16, 128).T
    cols[0:64, 84] = 1.0
    cols[64:128, 85] = 1.0
    shared = {
        "w_in_even": f(inputs["w_in_even"])[0], "w_out_even": f(inputs["w_out_even"])[0],
        "w_in_odd": f(inputs["w_in_odd"])[0], "w_out_odd": f(inputs["w_out_odd"])[0],
        "vecs": np.ascontiguousarray(vecs), "wsT": wsT, "cols": cols, "sink": f(inputs["attn_sink"]),
        "cst": _host_consts(),
    }
    maps = []
    for c in range(ncores):
        m = dict(shared)
        m["x"] = np.ascontiguousarray(x[c, :S])
        maps.append(m)
    return maps


def kernel(**inputs):
    x = np.asarray(inputs["x"])
    B, S, _ = x.shape
    prog = _get_prog(S // 128)
    maps = make_in_maps(inputs, B, S)
    res = run_bass_kernel_spmd(prog.nc, maps, core_ids=list(range(B)))
    return np.stack([np.asarray(r["out"]).reshape(S, D) for r in res.results], axis=0).astype(np.float32)
```

```python
import os
import numpy as np
import concourse.bass as bass
import concourse.mybir as mybir
from concourse.bass_utils import run_bass_kernel_spmd

F32 = mybir.dt.float32
BF16 = mybir.dt.bfloat16
AF = mybir.ActivationFunctionType
ALU = mybir.AluOpType
AX = mybir.AxisListType

KSTOP = int(os.environ.get('KSTOP', '99'))
D = 1024
EPS = 1e-6
NCORES = 8
SEQ = 8192
ARENA_WORDS = 53000


class _Op:
    __slots__ = ("eng", "fn", "reads", "writes", "dma", "deps", "alld", "sig", "need", "dur", "seg")

    def __init__(self, eng, fn, reads, writes, dma, dur, seg):
        self.eng, self.fn, self.reads, self.writes, self.dma = eng, fn, tuple(reads), tuple(writes), dma
        self.deps = ()
        self.alld = ()
        self.sig = None
        self.need = False
        self.dur = dur
        self.seg = seg


class Sched:
    ENGS = ("pe", "act", "dve", "pool", "sp")
    XLAT = 350.0

    def __init__(self):
        self.ops = []
        self.last_w = {}
        self.readers = {}
        self.seg = 0

    def add(self, eng, fn, r=(), w=(), dma=None, dur=300.0):
        ops = self.ops
        i = len(ops)
        w = tuple(w) + tuple(k for k in r if isinstance(k, tuple) and k[0] == "ps" and k not in w)
        op = _Op(eng, fn, r, w, dma, dur, self.seg)
        deps = set()
        for k in op.reads:
            j = self.last_w.get(k)
            if j is not None:
                deps.add(j)
        for k in op.writes:
            j = self.last_w.get(k)
            if j is not None:
                deps.add(j)
            deps.update(self.readers.get(k, ()))
        deps = set(j for j in deps if ops[j].seg == self.seg)
        op.alld = deps
        pr = set()
        for j in deps:
            oj = ops[j]
            if oj.dma is None and dma is None and oj.eng == eng:
                if eng == "pe":
                    continue
                if not any(k in oj.writes for k in op.reads):
                    continue
            pr.add(j)
        op.deps = pr
        for k in op.reads:
            self.readers.setdefault(k, []).append(i)
        for k in op.writes:
            self.last_w[k] = i
            self.readers[k] = []
        ops.append(op)
        return i

    def barrier(self):
        self.seg += 1

    def _sched_segment(self, idxs):
        import heapq
        ops = self.ops
        nd = {}
        succ = {}
        for i in idxs:
            d = ops[i].alld
            nd[i] = len(d)
            for j in d:
                succ.setdefault(j, []).append(i)
        free = {e: 0.0 for e in self.ENGS}
        fut = {e: [] for e in self.ENGS}
        av = {e: [] for e in self.ENGS}
        rt = {}
        fin = {}
        for i in idxs:
            if nd[i] == 0:
                heapq.heappush(fut[ops[i].eng], (0.0, i))
        order = []
        lastop = {}
        left = len(idxs)
        while left:
            best = None
            for e in self.ENGS:
                f, a = fut[e], av[e]
                while f and f[0][0] <= free[e]:
                    heapq.heappush(a, heapq.heappop(f)[1])
                if a:
                    cand = (free[e], a[0], e, True)
                elif f:
                    cand = (f[0][0], f[0][1], e, False)
                else:
                    continue
                if best is None or cand[:2] < best[:2]:
                    best = cand
            start, i, e, fa = best
            if fa:
                heapq.heappop(av[e])
            else:
                heapq.heappop(fut[e])
            op = ops[i]
            if op.dma is not None:
                free[e] = start + 120.0
            else:
                free[e] = start + op.dur
            fin[i] = start + op.dur
            self.tstart[i] = start
            self.whye[i] = (lastop.get(e), start <= rt.get(i, 0.0) + 1e-6)
            lastop[e] = i
            order.append(i)
            left -= 1
            for sidx in succ.get(i, ()):
                so = ops[sidx]
                if so.eng == e and op.dma is None and so.dma is None:
                    lat = 60.0 if sidx in so.deps or i in so.deps else 0.0
                else:
                    lat = self.XLAT
                t = fin[i] + lat
                if rt.get(sidx, 0.0) < t:
                    rt[sidx] = t
                    self.whyd[sidx] = i
                nd[sidx] -= 1
                if nd[sidx] == 0:
                    heapq.heappush(fut[so.eng], (rt[sidx], sidx))
        self.makespan.append(max(fin.values()) if fin else 0.0)
        return order

    def schedule(self):
        ops = self.ops
        nseg = self.seg + 1
        segs = [[] for _ in range(nseg)]
        for i, op in enumerate(ops):
            segs[op.seg].append(i)
        self.makespan = []
        self.tstart = {}
        self.whyd = {}
        self.whye = {}
        order = []
        for sg in segs:
            o = self._sched_segment(sg)
            order.extend(o)
            last = {}
            for i in o:
                op = ops[i]
                if op.dma is not None:
                    last[("d", op.dma)] = i
                elif op.fn is not None:
                    last[("e", op.eng)] = i
            deps = set(last.values())
            for e in self.ENGS:
                order.append(("bar", e, deps))
        self.order = order

    def finalize(self, nc, stack):
        ops = self.ops
        self.schedule()
        for op in ops:
            for j in op.deps:
                ops[j].need = True
        for it in self.order:
            if isinstance(it, tuple):
                for j in it[2]:
                    ops[j].need = True
        sems = {}
        cnt = {}

        def sem_for(key):
            if key not in sems:
                sems[key] = stack.enter_context(nc.semaphore("s%d" % len(sems)))
                cnt[key] = 0
            return sems[key]

        for e in self.ENGS:
            sem_for(("e", e))
        for it in self.order:
            if isinstance(it, tuple):
                continue
            op = ops[it]
            if op.dma is not None:
                k = ("d", op.dma)
                sem_for(k)
                cnt[k] += 16
                op.sig = (k, cnt[k])
            elif op.need and op.fn is not None:
                k = ("e", op.eng)
                cnt[k] += 1
                op.sig = (k, cnt[k])
        self.sems = sems
        return sems

    def emit(self, eng_name, eng):
        ops = self.ops
        sems = self.sems
        known = {}
        for it in self.order:
            if isinstance(it, tuple):
                if it[1] != eng_name:
                    continue
                deps, op = it[2], None
            else:
                op = ops[it]
                if op.eng != eng_name:
                    continue
                deps = op.deps
            need = {}
            for j in deps:
                sg = ops[j].sig
                if sg is None:
                    continue
                if need.get(sg[0], 0) < sg[1]:
                    need[sg[0]] = sg[1]
            for k, v in need.items():
                if known.get(k, 0) < v:
                    eng.wait_ge(sems[k], v)
                    known[k] = v
            if op is None or op.fn is None:
                continue
            ins = op.fn(eng)
            if op.sig is not None:
                ins.then_inc(sems[op.sig[0]], 16 if op.dma is not None else 1)


class Arena:
    def __init__(self, t, words):
        self.t = t
        self.words = words
        self.off = 0

    def alloc(self, shape, dtype):
        n = 1
        for s in shape:
            n *= s
        words = n if dtype == F32 else (n + 1) // 2
        words = (words + 7) // 8 * 8
        assert self.off + words <= self.words, ("arena overflow", self.off, words)
        ap = self.t[:, self.off:self.off + words]
        self.off += words
        if dtype != F32:
            ap = ap.bitcast(dtype)
        ap = ap[:, 0:n]
        if len(shape) == 2:
            ap = ap.rearrange("p (a b) -> p a b", a=shape[0], b=shape[1])
        elif len(shape) == 3:
            ap = ap.rearrange("p (a b c) -> p a b c", a=shape[0], b=shape[1], c=shape[2])
        return ap


class Prog:
    def __init__(self, NT, upto=3):
        self.NT = NT
        self.S = NT * 128
        self.upto = upto
        self.sc = Sched()
        self.nc = bass.Bass("TRN2", target_bir_lowering=False)

    @staticmethod
    def _n(ap):
        n = 1
        for v in ap.shape[1:]:
            n *= v
        return n

    def mm(self, out, lhsT, rhs, start, stop, r, w):
        d = max(self._n(rhs) / 2.4, 104.0)
        self.sc.add("pe", lambda e: e.matmul(out, lhsT, rhs, start=start, stop=stop), r, w, dur=d)

    def tr(self, out, in_, r, w):
        ident = self.ident
        self.sc.add("pe", lambda e: e.transpose(out, in_, ident), tuple(r) + ("const",), w, dur=110.0)

    def act(self, out, in_, func, r, w, bias=None, scale=None, accum=None):
        kw = {}
        if bias is not None:
            kw["bias"] = bias
        if scale is not None:
            kw["scale"] = scale
        if accum is not None:
            kw["accum_out"] = accum
        d = 210.0 + 0.75 * self._n(in_) + (100.0 if accum is not None else 0.0)
        self.sc.add("act", lambda e: e.activation(out, in_, func, **kw), r, w, dur=d)

    def _vd(self, eng, n, k=1.35):
        return (110.0 + 2.25 * n) if eng == "pool" else (70.0 + k * n)

    def ts(self, eng, out, in0, s1, op0, r, w, s2=None, op1=None):
        d = self._vd(eng, self._n(in0))
        if op1 is None:
            self.sc.add(eng, lambda e: e.tensor_scalar(out, in0, s1, None, op0), r, w, dur=d)
        else:
            self.sc.add(eng, lambda e: e.tensor_scalar(out, in0, s1, s2, op0, op1), r, w, dur=d)

    def tt(self, eng, out, in0, in1, op, r, w):
        self.sc.add(eng, lambda e: e.tensor_tensor(out, in0, in1, op), r, w, dur=self._vd(eng, self._n(in0)))

    def stt(self, out, in0, scalar, in1, op0, op1, r, w):
        self.sc.add("dve", lambda e: e.scalar_tensor_tensor(out, in0, scalar, in1, op0, op1), r, w,
                    dur=90.0 + 2.0 * self._n(in0))

    def cp(self, eng, out, in_, r, w):
        if eng == "act":
            self.sc.add("act", lambda e: e.copy(out, in_), r, w, dur=210.0 + 0.75 * self._n(in_))
        else:
            self.sc.add(eng, lambda e: e.tensor_copy(out, in_), r, w, dur=self._vd(eng, self._n(in_), 1.0))

    def dma(self, eng, out, in_, key, r, w, cast=False):
        nbytes = 128 * self._n(out) * 4
        d = 2500.0 + nbytes / 150.0
        if cast:
            self.sc.add(eng, lambda e: e.dma_start(out=out, in_=in_, max_dma_last_dim=4096), r, w, dma=key, dur=d)
        else:
            self.sc.add(eng, lambda e: e.dma_start(out=out, in_=in_), r, w, dma=key, dur=d)

    def silu(self, out, src, tmp, r, w, tk):
        self.act(tmp, src, AF.Exp, r, [tk], scale=-1.0)
        self.act(tmp, tmp, AF.Ln, [tk], [tk], bias=1.0)
        self.act(tmp, tmp, AF.Exp, [tk], [tk], scale=-1.0)
        self.tt("dve", out, src, tmp, ALU.mult, tuple(r) + (tk,), w)

    def rms(self, x_ap, xk, n=1024):
        self.act(self.hb, x_ap, AF.Square, [xk], ["hb", "ss"], accum=self.ss)
        self.act(self.lnv, self.ss, AF.Ln, ["ss"], ["lnv"], scale=1.0 / n, bias=EPS)
        self.act(self.rstd, self.lnv, AF.Exp, ["lnv"], ["rstd"], scale=-0.5)

    def make_hT(self, x_ap, xk, gbc, gk, dst=None, dk="hT", tb=0):
        self.rms(x_ap, xk)
        self.stt(self.hb, x_ap, self.rstd, gbc, ALU.mult, ALU.mult, [xk, "rstd", gk], ["hb"])
        Tb = self.psb[tb]
        for kc in range(8):
            self.tr(Tb[:, kc * 128:(kc + 1) * 128], self.hb[:, kc * 128:(kc + 1) * 128], ["hb"], [("ps", tb)])
        if dst is None:
            self.cp("act", self.hT.rearrange("p a b -> p (a b)"), Tb[:, 0:1024], [("ps", tb)], ["hT"])
        else:
            self.cp("act", dst, Tb[:, 0:1024].rearrange("p (a b) -> p a b", b=128), [("ps", tb)], [dk])

    def proj_tok(self, bank, ncols, W, col0):
        pk = ("ps", bank)
        out = self.ps[bank][:, 0:ncols]
        for kc in range(8):
            self.mm(out, self.hT[:, kc, :], W[:, kc, col0:col0 + ncols], kc == 0, kc == 7, ["hT", ("W", kc)], [pk])

    def proj_feat(self, bank, hh, W, col0):
        pk = ("ps", bank)
        out = self.ps[bank][:, hh * 128:(hh + 1) * 128]
        for kc in range(8):
            self.mm(out, W[:, kc, col0:col0 + 128], self.hT[:, kc, :], kc == 0, kc == 7, ["hT", ("W", kc)], [pk])

    def build(self):
        nc = self.nc
        NT, S = self.NT, self.S
        dt = nc.dram_tensor
        self.x_d = dt("x", [S, D], F32, kind="ExternalInput")
        self.wie_d = dt("w_in_even", [D, 5632], F32, kind="ExternalInput")
        self.woe_d = dt("w_out_even", [2048, D], F32, kind="ExternalInput")
        self.wio_d = dt("w_in_odd", [D, 10240], F32, kind="ExternalInput")
        self.woo_d = dt("w_out_odd", [2048, D], F32, kind="ExternalInput")
        self.vecs_d = dt("vecs", [5, D], F32, kind="ExternalInput")
        self.wsT_d = dt("wsT", [128, 512], F32, kind="ExternalInput")
        self.cols_d = dt("cols", [128, 128], F32, kind="ExternalInput")
        self.sink_d = dt("sink", [1, 8], F32, kind="ExternalInput")
        self.cst_d = dt("cst", [128, 3200], F32, kind="ExternalInput")
        self.out_d = dt("out", [S, D], F32, kind="ExternalOutput")
        self.x1_d = dt("x1s", [S, D], F32, kind="Internal")
        self.ob_d = dt("obs", [NT // 2, 128, 4096], F32, kind="Internal")
        self.on_d = dt("ons", [NT // 2, 128, 4096], F32, kind="Internal")
        self.qs_d = dt("qss", [NT // 2, 128, 4096], F32, kind="Internal")
        self.it_d = dt("its", [NT, 128, 2048], BF16, kind="Internal")

        from contextlib import ExitStack
        with ExitStack() as st:
            arena_t = st.enter_context(nc.sbuf_tensor("arena", [128, ARENA_WORDS], F32))
            self.ps = [st.enter_context(nc.psum_tensor("ps%d" % b, [128, 512], F32)) for b in range(8)]
            self.psb = [p.bitcast(BF16) for p in self.ps]
            self.A = Arena(arena_t, ARENA_WORDS)
            self.consts()
            base = self.A.off
            self.phase0()
            if self.upto >= 1 and self.upto != 12:
                self.sc.barrier()
                self.A.off = base
                self.phase_scan(fwd=False)
            if self.upto >= 2:
                self.sc.barrier()
                self.A.off = base
                self.phase_scan(fwd=True)
            if self.upto >= 3 and self.upto != 12:
                self.sc.barrier()
                self.A.off = base
                self.phase3()
            self.sc.finalize(nc, st)
            with nc.Block() as block:
                sc = self.sc

                @block.sync
                def _(e):
                    sc.emit("sp", e)

                @block.scalar
                def _(e):
                    sc.emit("act", e)

                @block.vector
                def _(e):
                    sc.emit("dve", e)

                @block.gpsimd
                def _(e):
                    sc.emit("pool", e)

                @block.tensor
                def _(e):
                    sc.emit("pe", e)
        return nc

    def consts(self):
        A = self.A
        a = A.alloc
        self.ident = a((128,), BF16)
        self.ones = a((128,), BF16)
        self.maskf = a((512,), BF16)
        self.maskb = a((512,), BF16)
        self.resf = a((512,), F32)
        self.resb = a((512,), F32)
        self.distp = a((384,), F32)
        self.cols = a((128,), F32)
        self.sinkbc = a((8,), F32)
        self.nsink = a((8,), F32)
        self.lbf = a((16,), F32)
        self.omlf = a((16,), F32)
        self.lbb = a((16,), F32)
        self.omlb = a((16,), F32)
        self.lnomlf = a((16,), F32)
        self.lnomlb = a((16,), F32)
        self.ss = a((1,), F32)
        self.lnv = a((1,), F32)
        self.rstd = a((1,), F32)
        self.hb = a((1024,), BF16)
        self.hT = a((8, 128), BF16)
        c = self.cst_d
        p = "pool"
        k = "const"
        self.dma(p, self.ident, c[:, 0:128], "c0", [], [k], cast=True)
        self.dma(p, self.ones, c[:, 128:256], "c1", [], [k], cast=True)
        self.dma(p, self.maskf, c[:, 256:768], "c2", [], [k], cast=True)
        self.dma(p, self.maskb, c[:, 768:1280], "c3", [], [k], cast=True)
        self.dma("sp", self.resf, c[:, 1280:1792], "c4", [], [k])
        self.dma("sp", self.resb, c[:, 1792:2304], "c5", [], [k])
        self.dma("sp", self.distp, c[:, 2304:2688], "c6", [], [k])
        self.dma("sp", self.cols, self.cols_d[:, :], "c7", [], [k])
        self.dma("sp", self.sinkbc, self.sink_d[0:1, :].partition_broadcast(128), "c8", [], [k])
        self.ts("dve", self.nsink, self.sinkbc, -1.0, ALU.mult, [k], ["const2"])
        for (lb, oml, lno, o) in ((self.lbf, self.omlf, self.lnomlf, 4), (self.lbb, self.omlb, self.lnomlb, 36)):
            self.tt("dve", lb, self.cols[:, o:o + 16], self.cols[:, o + 16:o + 32], ALU.subtract, [k], ["const2"])
            self.act(lb, lb, AF.Exp, ["const2"], ["const2"])
            self.ts("dve", lb, lb, 1.0, ALU.add, ["const2"], ["const2"])
            self.sc.add("dve", (lambda lb: (lambda e: e.reciprocal(lb, lb)))(lb), ["const2"], ["const2"])
            self.ts("dve", oml, lb, -1.0, ALU.mult, ["const2"], ["const2"], 1.0, ALU.add)
            self.act(lno, oml, AF.Ln, ["const2"], ["const2"])
        self.bs = self.cols[:, 0:4]
        self.hg = self.cols[:, 68:84]
        self.cmask = self.cols[:, 84:86]

    def load_vec(self, i, key):
        t = self.A.alloc((1024,), F32)
        self.dma("sp", t, self.vecs_d[i:i + 1, :].partition_broadcast(128), key, [], [key])
        return t

    def phase0(self):
        A, NT = self.A, self.NT
        a = A.alloc
        ps, psb = self.ps, self.psb
        W = a((8, 5632), BF16)
        WO = a((16, 1024), BF16)
        gE = self.load_vec(0, "gE")
        lng = self.load_vec(1, "lng")
        lnb = self.load_vec(2, "lnb")
        wsT = a((4, 128), BF16)
        self.dma("pool", wsT.rearrange("p a b -> p (a b)"), self.wsT_d[:, :], "wsT", [], ["wsT"], cast=True)
        wv = self.wie_d.ap().rearrange("(kc p) n -> p kc n", p=128)
        for kc in range(8):
            self.dma("pool", W[:, kc, :], wv[:, kc, :], ("W", kc), [], [("W", kc)], cast=True)
        self.dma("pool", WO, self.woe_d.ap().rearrange("(kc p) n -> p kc n", p=128), "WO", [], ["WO"], cast=True)
        xs = [a((1024,), F32) for _ in range(3)]
        qT = [a((8, 128), BF16) for _ in range(2)]
        kT = [a((2, 128), BF16) for _ in range(3)]
        vb = [a((256,), BF16) for _ in range(3)]
        szb = [a((1024,), BF16) for _ in range(2)]
        cat = [a((2048,), BF16) for _ in range(2)]
        catT = a((16, 128), BF16)
        vh = a((1024,), F32)
        vn = a((1024,), BF16)
        mp = vh
        sza = a((1024,), F32)
        tmpE = [a((512,), F32) for _ in range(2)]
        stats = a((2, 6), F32)
        mv = a((2,), F32)
        lnv2 = a((1,), F32)
        rstd2 = a((1,), F32)
        ssb = [a((384,), F32) for _ in range(2)]
        pp = [a((384,), BF16) for _ in range(2)]
        pT = [a((384,), BF16) for _ in range(2)]
        nrm = a((8,), F32)
        negm = a((8,), F32)
        rsum = a((8,), F32)
        t4 = a((4,), F32)
        rinv = a((4,), F32)
        dst = self.x1_d if self.upto >= 1 else self.out_d

        def load_x(n):
            sl = n % 3
            self.dma("sp", xs[sl], self.x_d[n * 128:(n + 1) * 128, :], ("xs", sl), [], [("xs", sl)])

        load_x(0)
        for n in range(NT + 1):
            if n + 1 < NT:
                load_x(n + 1)
            if n < NT:
                sl, par, s3 = n % 3, n % 2, n % 3
                xk = ("xs", sl)
                self.make_hT(xs[sl], xk, gE, "gE")
                self.proj_tok(1, 512, W, 1024)
                self.proj_tok(2, 512, W, 1536)
                for h in range(8):
                    self.proj_feat(3 + h // 4, h % 4, W, 3072 + h * 128)
                for b in range(2):
                    self.sc.add("dve", (lambda o, i: (lambda e: e.bn_stats(o, i)))(stats[:, b, :], ps[1 + b][:, :]),
                                [("ps", 1 + b)], ["stats"], dur=700.0)
                self.sc.add("dve", lambda e: e.bn_aggr(mv, stats.rearrange("p a b -> p (a b)")), ["stats"], ["mv"])
                self.act(lnv2, mv[:, 1:2], AF.Ln, ["mv"], ["lnv2"], bias=EPS)
                self.act(rstd2, lnv2, AF.Exp, ["lnv2"], ["rstd2"], scale=-0.5)
                for b in range(2):
                    self.ts("dve", vh[:, b * 512:(b + 1) * 512], ps[1 + b][:, :], mv[:, 0:1], ALU.subtract,
                            [("ps", 1 + b), "mv", "rstd2"], [("vm", b)], rstd2, ALU.mult)
                for b in range(2):
                    sl_ = slice(b * 512, (b + 1) * 512)
                    self.tt("pool", vh[:, sl_], vh[:, sl_], lng[:, sl_], ALU.mult, [("vm", b), "lng"], [("vm", b)])
                    self.tt("pool", vn[:, sl_], vh[:, sl_], lnb[:, sl_], ALU.add, [("vm", b), "lnb"], [("vn", b)])
                for g in range(2):
                    self.act(qT[par][:, 4 * g:4 * g + 4, :].rearrange("p a b -> p (a b)"), ps[3 + g][:, :], AF.Identity,
                             [("ps", 3 + g)], [("qT", par)], scale=128.0 ** -0.5)
                for hk in range(2):
                    self.proj_feat(1, hk, W, 4096 + hk * 128)
                self.proj_tok(2, 256, W, 4352)
                self.cp("dve", kT[s3].rearrange("p a b -> p (a b)"), ps[1][:, 0:256], [("ps", 1)], [("kT", s3)])
                self.cp("dve", vb[s3], ps[2][:, 0:256], [("ps", 2)], [("vb", s3)])
                self.proj_tok(3, 512, W, 2048)
                self.proj_tok(4, 512, W, 2560)
                for b in range(2):
                    self.silu(sza[:, b * 512:(b + 1) * 512], ps[3 + b][:, :], tmpE[b], [("ps", 3 + b)], [("sza", b)],
                              ("tmpE", b))
                for g in range(4):
                    self.mm(ps[1 + g // 2][:, (g % 2) * 256:(g % 2) * 256 + 256], wsT[:, g, :],
                            vn[:, g * 256:(g + 1) * 256], True, True, ["wsT", ("vn", g // 2)], [("ps", 1 + g // 2)])
                for g in range(4):
                    self.act(mp[:, g * 256:(g + 1) * 256], ps[1 + g // 2][:, (g % 2) * 256:(g % 2) * 256 + 256],
                             AF.Identity, [("ps", 1 + g // 2), "const"], [("vm", g // 2)], bias=self.bs[:, g:g + 1])
                self.proj_tok(3, 512, W, 0)
                self.proj_tok(4, 512, W, 512)
                for b in range(2):
                    sl_ = slice(b * 512, (b + 1) * 512)
                    self.tt("dve", mp[:, sl_], ps[3 + b][:, :], mp[:, sl_], ALU.mult, [("ps", 3 + b), ("vm", b)],
                            [("vm", b)])
                    self.tt("pool", cat[par][:, sl_], mp[:, sl_], sza[:, sl_], ALU.mult, [("vm", b), ("sza", b)],
                            [("cat", par, b)])
                self.proj_tok(1, 512, W, 4608)
                self.proj_tok(2, 512, W, 5120)
                for b in range(2):
                    self.silu(szb[par][:, b * 512:(b + 1) * 512], ps[1 + b][:, :], tmpE[b], [("ps", 1 + b)],
                              [("szb", par, b)], ("tmpE", b))
            if n >= 1:
                m = n - 1
                pm = m % 2
                js = [j for j in (-1, 0, 1) if 0 <= m + j < NT]
                c0, c1 = (js[0] + 1) * 128, (js[-1] + 2) * 128
                Sb, PTb, Ob = ps[5], psb[6], ps[7]
                for h in range(8):
                    kvh, hh, hs = h // 4, h % 4, h % 2
                    for j in js:
                        self.mm(Sb[:, (j + 1) * 128:(j + 2) * 128], qT[pm][:, h, :], kT[(m + j) % 3][:, kvh, :], True, True,
                                [("qT", pm), ("kT", (m + j) % 3)], [("ps", 5)])
                    self.stt(ssb[hs][:, c0:c1], self.distp[:, c0:c1], -(2.0 ** -(h + 1)), Sb[:, c0:c1], ALU.mult, ALU.add,
                             ["const", ("ps", 5)], [("ssb", hs)])
                    self.sc.add("dve", (lambda o, i: (lambda e: e.tensor_reduce(o, i, AX.X, ALU.max, negate=True)))(
                        nrm[:, h:h + 1], ssb[hs][:, c0:c1]), [("ssb", hs)], [("nrm", h)], dur=550.0)
                    self.tt("dve", negm[:, h:h + 1], nrm[:, h:h + 1], self.nsink[:, h:h + 1], ALU.min,
                            [("nrm", h), "const2"], [("negm", h)])
                    self.act(pp[hs][:, c0:c1], ssb[hs][:, c0:c1], AF.Exp, [("ssb", hs), ("negm", h)],
                             [("pp", hs), ("rsum", h)], bias=negm[:, h:h + 1], accum=rsum[:, h:h + 1])
                    for j in js:
                        cs = slice((j + 1) * 128, (j + 2) * 128)
                        self.tr(PTb[:, cs], pp[hs][:, cs], [("pp", hs)], [("ps", 6)])
                    self.cp("act", pT[hs][:, c0:c1], PTb[:, c0:c1], [("ps", 6)], [("pT", hs)])
                    for j in js:
                        cs = slice((j + 1) * 128, (j + 2) * 128)
                        self.mm(Ob[:, hh * 128:(hh + 1) * 128], pT[hs][:, cs], vb[(m + j) % 3][:, kvh * 128:(kvh + 1) * 128],
                                j == js[0], j == js[-1], [("pT", hs), ("vb", (m + j) % 3)], [("ps", 7)])
                    if hh == 3:
                        g4 = slice(h - 3, h + 1)
                        self.tt("dve", t4, self.sinkbc[:, g4], negm[:, g4], ALU.add,
                                ["const"] + [("negm", q) for q in range(h - 3, h + 1)], ["t4"])
                        self.act(t4, t4, AF.Exp, ["t4"], ["t4"])
                        self.tt("dve", t4, t4, rsum[:, g4], ALU.add, ["t4"] + [("rsum", q) for q in range(h - 3, h + 1)],
                                ["t4"])
                        self.sc.add("dve", lambda e: e.reciprocal(rinv, t4), ["t4"], ["rinv"])
                        for q in range(4):
                            hq = h - 3 + q
                            self.stt(cat[pm][:, 1024 + hq * 128:1024 + (hq + 1) * 128], Ob[:, q * 128:(q + 1) * 128],
                                     rinv[:, q:q + 1], szb[pm][:, hq * 128:(hq + 1) * 128], ALU.mult, ALU.mult,
                                     [("ps", 7), "rinv", ("szb", pm, hq // 4)], [("cat", pm, 2 + hq // 4)])
                for half in range(2):
                    for kc in range(8):
                        c = half * 8 + kc
                        self.tr(psb[0][:, kc * 128:(kc + 1) * 128], cat[pm][:, c * 128:(c + 1) * 128],
                                [("cat", pm, c // 4)], [("ps", 0)])
                    self.cp("act", catT[:, half * 8:half * 8 + 8, :].rearrange("p a b -> p (a b)"), psb[0][:, 0:1024],
                            [("ps", 0)], [("catT", half)])
                for half in range(2):
                    for kc in range(16):
                        self.mm(ps[3 + half][:, :], catT[:, kc, :], WO[:, kc, half * 512:(half + 1) * 512], kc == 0, kc == 15,
                                [("catT", kc // 8), "WO"], [("ps", 3 + half)])
                xo = xs[m % 3]
                for half in range(2):
                    sl_ = slice(half * 512, (half + 1) * 512)
                    self.tt("dve", xo[:, sl_], ps[3 + half][:, :], xs[m % 3][:, sl_], ALU.add,
                            [("ps", 3 + half), ("xs", m % 3)], [("xs", m % 3)])
                self.dma("sp", dst[m * 128:(m + 1) * 128, :], xo, ("st", m % 3), [("xs", m % 3)], [("x1d", m)])

    def phase_scan(self, fwd):
        A, NT = self.A, self.NT
        NM = NT // 2
        a = A.alloc
        ps, psb = self.ps, self.psb
        W = a((8, 2048 if fwd else 6144), BF16)
        fbase = 0 if fwd else 2048
        wv = self.wio_d.ap().rearrange("(kc p) n -> p kc n", p=128)
        fcol = 2048 if fwd else 4096
        for bi, c0 in enumerate((6144, 0, fcol)):
            bo = (2, 0, 1)[bi]
            if fwd and bo != 1:
                continue
            for kc in range(8):
                self.dma("pool", W[:, kc, (0 if fwd else bo * 2048):(0 if fwd else bo * 2048) + 2048], wv[:, kc, c0:c0 + 2048], ("W", kc), [],
                         [("W", kc)], cast=True)
        gO = self.load_vec(3, "gO")
        xs = [a((1024,), F32) for _ in range(4)]
        hT2s = [a((8, 256), BF16) for _ in range(2)]
        itok = [[a((2048,), BF16) for _ in range(2)] for _ in range(2)]
        st32 = a((16, 128), F32)
        stbf = a((16, 128), BF16)
        E = a((512,), F32)
        L1 = a((512,), F32)
        L2 = a((512,), F32)
        bc = a((512,), F32)
        wk = a((512,), F32)
        Eq = a((512,), F32)
        w2 = a((512,), F32)
        sg = a((512,), F32)
        NB = 4 if fwd else 3
        qsb = [a((512,), F32) for _ in range(NB)]
        dn = [a((8,), F32) for _ in range(NB)]
        qt = [a((512,), BF16) for _ in range(NB)]
        kt = [a((512,), BF16) for _ in range(NB)]
        kend = a((512,), BF16)
        kttok = [a((512,), BF16) for _ in range(NB)]
        aT = a((512,), BF16)
        osb = [a((512,), F32) for _ in range(2)]
        if fwd:
            obl = [a((512,), F32) for _ in range(2)]
            osq = a((512,), BF16)
            lnr = a((512,), F32)
        lb, lnoml = (self.lbf, self.lnomlf) if fwd else (self.lbb, self.lnomlb)
        mask = self.maskf if fwd else self.maskb
        res = self.resf if fwd else self.resb
        self.sc.add("pool", lambda e: e.memset(st32.rearrange("p a b -> p (a b)"), 0.0), [],
                    [("st32", h) for h in range(16)], dur=1200.0)
        self.sc.add("pool", lambda e: e.memset(stbf.rearrange("p a b -> p (a b)"), 0.0), [],
                    [("stbf", h) for h in range(16)], dur=700.0)
        morder = list(range(NM)) if fwd else list(range(NM - 1, -1, -1))
        c4order = (0, 1, 2, 3) if fwd else (3, 2, 1, 0)

        def load_x(idx):
            M = morder[idx]
            for j in range(2):
                sl = (idx % 2) * 2 + j
                n = 2 * M + j
                self.dma("sp", xs[sl], self.x1_d[n * 128:(n + 1) * 128, :], ("xs", sl), [("x1d", n)], [("xs", sl)])

        def rev(ap):
            return ap if fwd else ap[:, ::-1]

        def prologue(idx):
            if idx + 1 < NM:
                load_x(idx + 1)
            mp = idx % 2
            hT2 = hT2s[mp]
            for j in range(2):
                sl = mp * 2 + j
                self.make_hT(xs[sl], ("xs", sl), gO, "gO", dst=hT2[:, :, j * 128:(j + 1) * 128], dk=("hT2", mp, j), tb=6)
            for j in range(2):
                n = 2 * morder[idx] + j
                ik = [("itok", mp, j, blk) for blk in range(4)]
                if fwd:
                    self.dma("sp", itok[mp][j], self.it_d[n, :, :], ("ldi", mp, j), [("itd", n)], ik)
                    continue
                for blk in range(4):
                    bank = 6
                    pk = ("ps", bank)
                    for kc in range(8):
                        self.mm(ps[bank][:, :], hT2[:, kc, j * 128:(j + 1) * 128],
                                W[:, kc, 4096 + blk * 512:4096 + (blk + 1) * 512], kc == 0, kc == 7,
                                [("hT2", mp, j), ("W", kc)], [pk])
                    cs = slice(blk * 512, (blk + 1) * 512)
                    self.cp("act", itok[mp][j][:, cs], ps[bank][:, :], [pk], [("itok", mp, j, blk)])
                self.dma("sp", self.it_d[n, :, :], itok[mp][j], ("sti", mp, j), ik, [("itd", n)])

        def projf(bank, hh, col0, mp):
            pk = ("ps", bank)
            out = ps[bank][:, hh * 256:(hh + 1) * 256]
            for kc in range(8):
                self.mm(out, W[:, kc, col0:col0 + 128], hT2s[mp][:, kc, :], kc == 0, kc == 7,
                        [("hT2", mp, 0), ("hT2", mp, 1), ("W", kc)], [pk])

        def stageA(idx, G):
            p = (idx * 8 + G) % NB
            qb, fb = 1, 2
            M = morder[idx]
            if not fwd:
                for hh in range(2):
                    projf(qb, hh, (2 * G + hh) * 128, idx % 2)
            for hh in range(2):
                projf(fb, hh, fbase + (2 * G + hh) * 128, idx % 2)
            Q, Fp = ps[qb], ps[fb]
            self.act(E, Fp[:, :], AF.Exp, [("ps", fb)], ["E"], scale=-1.0)
            self.act(L1, E, AF.Ln, ["E"], ["L1"], bias=1.0)
            for hh in range(2):
                h = 2 * G + hh
                cs = slice(hh * 256, (hh + 1) * 256)
                self.act(L2[:, cs], E[:, cs], AF.Ln, ["E", "const2"], ["L2"], scale=lb[:, h:h + 1], bias=1.0)
            self.tt("pool", L2, L2, L1, ALU.subtract, ["L2", "L1"], ["L2"])
            self.sc.add("dve", (lambda: (lambda e: e.tensor_tensor_scan(rev(bc), rev(res), rev(L2), 0.0, ALU.mult,
                                                                         ALU.add)))(), ["L2", "const"], ["bc"], dur=1250.0)
            bcv = bc.rearrange("p (a b) -> p a b", b=64)
            self.act(dn[p], bcv[:, :, 63] if fwd else bcv[:, :, 0], AF.Exp, ["bc"], [("dn", p)])
            if not fwd:
                self.act(Eq, Q[:, :], AF.Exp, [("ps", qb)], ["Eq"], scale=-1.0)
                self.act(Eq, Eq, AF.Ln, ["Eq"], ["Eq"], bias=1.0)
                self.act(sg, Eq, AF.Exp, ["Eq"], ["sg"], scale=-1.0)
                self.tt("dve", qsb[p], Q[:, :], sg, ALU.mult, [("ps", qb), "sg"], [("qsb", p)])
                self.dma("sp", self.qs_d[M, :, G * 512:(G + 1) * 512], qsb[p], ("stq", p), [("qsb", p)],
                         [("qsd", M, G)])
            else:
                self.dma("sp", qsb[p], self.qs_d[M, :, G * 512:(G + 1) * 512], ("ldq", p), [("qsd", M, G)],
                         [("qsb", p)])
            self.tt("dve", wk, Fp[:, :], L1, ALU.add, [("ps", fb), "L1"], ["wk"])
            self.tt("dve", wk, wk, bc, ALU.add, ["wk", "bc"], ["wk"])
            for hh in range(2):
                h = 2 * G + hh
                cs = slice(hh * 256, (hh + 1) * 256)
                self.act(kt[p][:, cs], wk[:, cs], AF.Exp, ["wk", "const2"], [("kt", p)], scale=-1.0,
                         bias=lnoml[:, h:h + 1])
            self.act(w2, bc, AF.Exp, ["bc"], ["w2"])
            self.tt("pool", qt[p], qsb[p], w2, ALU.mult, [("qsb", p), "w2"], [("qt", p)])
            self.tt("pool", kend.rearrange("p (a b) -> p a b", b=64), kt[p].rearrange("p (a b) -> p a b", b=64),
                    dn[p].unsqueeze(2).to_broadcast([128, 8, 64]), ALU.mult, [("kt", p), ("dn", p)], ["kend"])
            for q4 in range(4):
                cs = slice(q4 * 128, (q4 + 1) * 128)
                self.tr(psb[0][:, cs], kend[:, cs], ["kend"], [("ps", 0)])
            self.cp("act", kttok[p], psb[0][:, 0:512], [("ps", 0)], [("kttok", p)])

        def stageB(idx, G):
            M = morder[idx]
            p = (idx * 8 + G) % NB
            mp = idx % 2
            for q4 in range(4):
                cs = slice(q4 * 128, (q4 + 1) * 128)
                self.mm(ps[3][:, cs], kt[p][:, cs], qt[p][:, cs], True, True, [("kt", p), ("qt", p)], [("ps", 3)])
            self.tt("dve", aT, ps[3][:, :], mask, ALU.mult, [("ps", 3), "const"], ["aT"])
            if fwd:
                ob, ub = (4, 5) if p % 2 == 0 else (4, 1)
            else:
                ob, ub = (4, 5) if p % 2 == 0 else (4, 7)
            O = ps[ob]
            for q4 in range(4):
                hh, j = q4 // 2, q4 % 2
                h = 2 * G + hh
                cs = slice(q4 * 128, (q4 + 1) * 128)
                self.mm(O[:, cs], itok[mp][j][:, h * 128:(h + 1) * 128], aT[:, cs], q4 == 0, False,
                        [("itok", mp, j, h // 4), "aT"], [("ps", ob)])
            for ci, c4 in enumerate(c4order):
                j, c = c4 // 2, c4 % 2
                for hh in range(2):
                    h = 2 * G + hh
                    cs = slice(hh * 256 + c4 * 64, hh * 256 + c4 * 64 + 64)
                    self.mm(O[:, cs], stbf[:, h, :], qt[p][:, cs], False, ci == 3 and hh == 1,
                            [("stbf", h), ("qt", p)], [("ps", ob)])
                for hh in range(2):
                    h = 2 * G + hh
                    q4 = hh * 2 + j
                    self.mm(ps[ub][:, hh * 128:(hh + 1) * 128], kttok[p][c * 64:(c + 1) * 64, q4 * 128:(q4 + 1) * 128],
                            itok[mp][j][c * 64:(c + 1) * 64, h * 128:(h + 1) * 128], True, True,
                            [("kttok", p), ("itok", mp, j, h // 4)], [("ps", ub)])
                for hh in range(2):
                    h = 2 * G + hh
                    dcol = dn[p][:, hh * 4 + c4:hh * 4 + c4 + 1]
                    self.stt(st32[:, h, :], st32[:, h, :], dcol, ps[ub][:, hh * 128:(hh + 1) * 128], ALU.mult, ALU.add,
                             [("st32", h), ("dn", p), ("ps", ub)], [("st32", h)])
                self.cp("act", stbf[:, 2 * G:2 * G + 2, :].rearrange("p a b -> p (a b)"),
                        st32[:, 2 * G:2 * G + 2, :].rearrange("p a b -> p (a b)"),
                        [("st32", 2 * G), ("st32", 2 * G + 1)], [("stbf", 2 * G), ("stbf", 2 * G + 1)])
            if not fwd:
                ob_ = osb[G % 2]
                self.cp("act", ob_, O[:, :], [("ps", ob)], [("osb", G % 2)])
                self.dma("sp", self.ob_d[M, :, G * 512:(G + 1) * 512], ob_, ("st", G % 2), [("osb", G % 2)],
                         [("obd", M, G)])
            else:
                l_ = obl[G % 2]
                o_ = osb[G % 2]
                self.dma("sp", l_, self.ob_d[M, :, G * 512:(G + 1) * 512], ("ld", G % 2), [("obd", M, G)],
                         [("obl", G % 2)])
                self.tt("dve", l_, O[:, :], l_, ALU.add, [("ps", ob), ("obl", G % 2)], [("obl", G % 2)])
                self.act(osq, l_, AF.Square, [("obl", G % 2)], ["osq"])
                self.mm(ps[7][:, :], self.ones, osq, True, True, ["osq", "const"], [("ps", 7)])
                self.act(lnr, ps[7][:, :], AF.Ln, [("ps", 7)], ["lnr"], scale=1.0 / 128, bias=EPS)
                self.act(lnr, lnr, AF.Exp, ["lnr"], ["lnr"], scale=-0.5)
                for hh in range(2):
                    h = 2 * G + hh
                    cs = slice(hh * 256, (hh + 1) * 256)
                    self.stt(o_[:, cs], l_[:, cs], self.hg[:, h:h + 1], lnr[:, cs], ALU.mult, ALU.mult,
                             [("obl", G % 2), "const", "lnr"], [("osb", G % 2)])
                self.dma("sp", self.on_d[M, :, G * 512:(G + 1) * 512], o_, ("st", G % 2), [("osb", G % 2)],
                         [("ond", M, G)])

        load_x(0)
        items = [(idx, G) for idx in range(NM) for G in range(8)]
        for k, (idx, G) in enumerate(items):
            if G == 0:
                prologue(idx)
            stageA(idx, G)
            if k >= 1:
                stageB(*items[k - 1])
        stageB(*items[-1])

    def phase3(self):
        A, NT = self.A, self.NT
        NM = NT // 2
        a = A.alloc
        ps, psb = self.ps, self.psb
        W = a((8, 2048), BF16)
        WO = a((16, 1024), BF16)
        wv = self.wio_d.ap().rearrange("(kc p) n -> p kc n", p=128)
        for kc in range(8):
            self.dma("pool", W[:, kc, :], wv[:, kc, 8192:10240], ("W", kc), [], [("W", kc)], cast=True)
        self.dma("pool", WO, self.woo_d.ap().rearrange("(kc p) n -> p kc n", p=128), "WO", [], ["WO"], cast=True)
        gO = self.load_vec(3, "gO")
        gF = self.load_vec(4, "gF")
        xs = [a((1024,), F32) for _ in range(4)]
        hT2s = [a((8, 256), BF16) for _ in range(2)]
        onl = [a((4096,), F32) for _ in range(2)]
        tmpE = [a((512,), F32) for _ in range(2)]
        szt = [a((512,), F32) for _ in range(2)]
        yT = a((16, 256), BF16)
        x2 = [a((1024,), F32) for _ in range(2)]

        def load(M):
            mp = M % 2
            for j in range(2):
                n = 2 * M + j
                sl = mp * 2 + j
                self.dma("sp", xs[sl], self.x1_d[n * 128:(n + 1) * 128, :], ("xs", sl), [("x1d", n)], [("xs", sl)])
            for hf in range(2):
                self.dma("sp", onl[mp][:, hf * 2048:(hf + 1) * 2048], self.on_d[M, :, hf * 2048:(hf + 1) * 2048],
                         ("ld", mp, hf), [("ond", M, g) for g in range(4 * hf, 4 * hf + 4)], [("onl", mp, hf)])

        def stA(M, G):
            mp = M % 2
            if G == 0:
                for j in range(2):
                    sl = mp * 2 + j
                    self.make_hT(xs[sl], ("xs", sl), gO, "gO", dst=hT2s[mp][:, :, j * 128:(j + 1) * 128],
                                 dk=("hT2", mp, j))
            bank = 1 + G % 2
            pk = ("ps", bank)
            for hh in range(2):
                col0 = (2 * G + hh) * 128
                for kc in range(8):
                    self.mm(ps[bank][:, hh * 256:(hh + 1) * 256], W[:, kc, col0:col0 + 128], hT2s[mp][:, kc, :], kc == 0,
                            kc == 7, [("hT2", mp, 0), ("hT2", mp, 1), ("W", kc)], [pk])

        def stB(M, G):
            mp = M % 2
            bank = 1 + G % 2
            self.silu(szt[G % 2], ps[bank][:, :], tmpE[G % 2], [("ps", bank)], [("szt", G % 2)], ("tmpE", G % 2))
            self.tt("pool", yT[:, 2 * G:2 * G + 2, :].rearrange("p a b -> p (a b)"), szt[G % 2],
                    onl[mp][:, G * 512:(G + 1) * 512], ALU.mult, [("szt", G % 2), ("onl", mp, G // 4)], [("yT", G)])
            for hh in range(2):
                h = 2 * G + hh
                for j in range(2):
                    for half in range(2):
                        bk = 3 + 2 * j + half
                        self.mm(ps[bk][:, :], yT[:, h, j * 128:(j + 1) * 128], WO[:, h, half * 512:(half + 1) * 512],
                                h == 0, h == 15, [("yT", G), "WO"], [("ps", bk)])
            if G == 7:
                for j in range(2):
                    n = 2 * M + j
                    sl = mp * 2 + j
                    xo = x2[j]
                    for half in range(2):
                        sl_ = slice(half * 512, (half + 1) * 512)
                        bk = 3 + 2 * j + half
                        self.tt("dve", xo[:, sl_], ps[bk][:, :], xs[sl][:, sl_], ALU.add,
                                [("ps", bk), ("xs", sl)], [("x2", j)])
                    self.rms(xo, ("x2", j))
                    self.stt(xo, xo, self.rstd, gF, ALU.mult, ALU.mult, [("x2", j), "rstd", "gF"], [("x2", j)])
                    self.dma("sp", self.out_d[n * 128:(n + 1) * 128, :], xo, ("st", j), [("x2", j)], [("outd", n)])
                if M + 2 < NM:
                    load(M + 2)

        load(0)
        if NM > 1:
            load(1)
        items = [(M, G) for M in range(NM) for G in range(8)]
        for k, it in enumerate(items):
            stA(*it)
            if k >= 1:
                stB(*items[k - 1])
        stB(*items[-1])


def _host_consts():
    c = np.zeros((128, 3200), np.float32)
    c[:, 0:128] = np.eye(128, dtype=np.float32)
    c[:, 128:256] = 1.0
    s = np.arange(128)[:, None]
    t = np.arange(128)[None, :]
    same = (s // 64) == (t // 64)
    mf = (same & (s <= t)).astype(np.float32)
    mb = (same & (s >= t)).astype(np.float32)
    c[:, 256:768] = np.tile(mf, (1, 4))
    c[:, 768:1280] = np.tile(mb, (1, 4))
    col = np.arange(512)
    c[:, 1280:1792] = (col % 64 != 0).astype(np.float32)[None, :]
    c[:, 1792:2304] = (col % 64 != 63).astype(np.float32)[None, :]
    qi = np.arange(128)[:, None]
    kj = np.arange(384)[None, :] - 128
    dist = np.abs(kj - qi).astype(np.float32)
    c[:, 2304:2688] = np.where(dist <= 128, dist, 1.0e9)
    return c


_PROG_CACHE = {}


def _get_prog(NT, upto=3):
    key = (NT, upto)
    if key not in _PROG_CACHE:
        p = Prog(NT, upto)
        p.build()
        _PROG_CACHE[key] = p
    return _PROG_CACHE[key]


def make_in_maps(inputs, ncores, S):
    f = lambda v: np.ascontiguousarray(np.asarray(v, dtype=np.float32))
    x = f(inputs["x"])
    vecs = np.stack([f(inputs["norm_g_even"])[0], f(inputs["gmlp_ln_g"])[0], f(inputs["gmlp_ln_b"])[0],
                     f(inputs["norm_g_odd"])[0], f(inputs["final_norm_g"])], axis=0)
    ws = f(inputs["gmlp_w_s"])[0]
    wsT = np.ascontiguousarray(ws.transpose(2, 0, 1).reshape(128, 512))
    cols = np.zeros((128, 128), np.float32)
    cols[:, 0:4] = f(inputs["gmlp_b_s"])[0].T
    gf = f(inputs["hgrn_gamma_fwd"]).reshape(2, 16, 128)
    gb = f(inputs["hgrn_gamma_bwd"]).reshape(2, 16, 128)
    cols[:, 4:20] = gf[0].T
    cols[:, 20:36] = gf[1].T
    cols[:, 36:52] = gb[0].T
    cols[:, 52:68] = gb[1].T
    cols[:, 68:84] = f(inputs["hgrn_head_norm_g"])[0].reshape(16, 128).T
    cols[0:64, 84] = 1.0
    cols[64:128, 85] = 1.0
    shared = {
        "w_in_even": f(inputs["w_in_even"])[0], "w_out_even": f(inputs["w_out_even"])[0],
        "w_in_odd": f(inputs["w_in_odd"])[0], "w_out_odd": f(inputs["w_out_odd"])[0],
        "vecs": np.ascontiguousarray(vecs), "wsT": wsT, "cols": cols, "sink": f(inputs["attn_sink"]),
        "cst": _host_consts(),
    }
    maps = []
    for c in range(ncores):
        m = dict(shared)
        m["x"] = np.ascontiguousarray(x[c, :S])
        maps.append(m)
    return maps


def kernel(**inputs):
    x = np.asarray(inputs["x"])
    B, S, _ = x.shape
    prog = _get_prog(S // 128)
    maps = make_in_maps(inputs, B, S)
    res = run_bass_kernel_spmd(prog.nc, maps, core_ids=list(range(B)))
    return np.stack([np.asarray(r["out"]).reshape(S, D) for r in res.results], axis=0).astype(np.float32)
```
